# Optimizing a Trainium2 kernel written in Bass

```python
import jax, jax.numpy as jnp
from jax import lax
import numpy as np

D_MODEL = 1024
BATCH = 1
SEQ = 16384
DEPTH = 1
DEC_BATCH = 32
DEC_SEQ = 8
PAST_LEN = 16384
PAGE_SIZE = 128

H_RET = D_MODEL // 256
DK_RET = 128
DV_RET = 128
H_SB = D_MODEL // 128
D_SB = 64
RET_W = H_RET * DV_RET
SB_W = H_SB * D_SB
MIX_W = RET_W + SB_W
IN_W = 4 * RET_W + 3 * SB_W
D_FF = ((8 * D_MODEL // 3 + 127) // 128) * 128
CONV_W = 3
BLOCK = 128
ROPE_BASE = 10000.0
EPS = 1e-6
N_ADA = 6
SB_BIAS_INIT = -8.0

kernel_name = "hymba_retention_stickbreaking_convffn_step"


def _block(t):
    return BLOCK if t % BLOCK == 0 else t


def rmsnorm(x, g):
    xf = x.astype(jnp.float32)
    y = xf * lax.rsqrt(jnp.mean(xf * xf, axis=-1, keepdims=True) + EPS) * g.astype(jnp.float32)
    return y.astype(x.dtype)


def rotary(x, pos):
    half = x.shape[-1] // 2
    inv = ROPE_BASE ** (-jnp.arange(half, dtype=jnp.float32) / half)
    ang = pos.astype(jnp.float32)[:, None] * inv[None, :]
    cos, sin = jnp.cos(ang), jnp.sin(ang)
    xf = x.astype(jnp.float32)
    x1, x2 = xf[..., :half], xf[..., half:]
    return jnp.concatenate([x1 * cos - x2 * sin, x1 * sin + x2 * cos], axis=-1)


def heads(t, h, d):
    b, n, _ = t.shape
    return t.reshape(b, n, h, d).transpose(0, 2, 1, 3)


def retention(q, k, v, s0):
    b, h, t, dk = q.shape
    dv = v.shape[-1]
    lg = jnp.log(1.0 - 2.0 ** (-5.0 - jnp.arange(H_RET, dtype=jnp.float32)))
    c = _block(t)
    n = t // c
    idx = jnp.arange(c)
    diff = (idx[:, None] - idx[None, :]).astype(jnp.float32)
    decay = jnp.where(diff[None] >= 0, jnp.exp(jnp.maximum(diff, 0.0)[None] * lg[:, None, None]), 0.0)
    q_decay = jnp.exp((idx + 1).astype(jnp.float32)[None, :] * lg[:, None])
    k_decay = jnp.exp((c - 1 - idx).astype(jnp.float32)[None, :] * lg[:, None])
    chunk_decay = jnp.exp(c * lg)

    def step(s, inp):
        qc, kc, vc = inp
        scores = jnp.einsum('bhnd,bhmd->bhnm', qc, kc) * decay[None]
        o = (jnp.einsum('bhnm,bhmv->bhnv', scores, vc)
             + jnp.einsum('bhnd,bhdv->bhnv', qc * q_decay[None, :, :, None], s))
        s = (s * chunk_decay[None, :, None, None]
             + jnp.einsum('bhmd,bhmv->bhdv', kc * k_decay[None, :, :, None], vc))
        return s, o

    qx = jnp.moveaxis(q.reshape(b, h, n, c, dk), 2, 0)
    kx = jnp.moveaxis(k.reshape(b, h, n, c, dk), 2, 0)
    vx = jnp.moveaxis(v.reshape(b, h, n, c, dv), 2, 0)
    s, o = lax.scan(step, s0.astype(jnp.float32), (qx, kx, vx))
    o = jnp.moveaxis(o, 0, 2).reshape(b, h, t, dv)
    return o, s


def sb_attention(q, k, v, bias, q_pos0):
    b, h, tq, d = q.shape
    tk = k.shape[2]
    bq = _block(tq)
    nb = tq // bq
    scale = d ** -0.5
    kf = k.astype(jnp.float32)
    vf = v.astype(jnp.float32)
    bf = bias.astype(jnp.float32)[None, :, None, None]
    qb = jnp.moveaxis(q.astype(jnp.float32).reshape(b, h, nb, bq, d), 2, 0)
    starts = q_pos0 + jnp.arange(nb) * bq
    spos = jnp.arange(tk)

    def blk(args):
        qi, t0 = args
        z = jnp.einsum('bhqd,bhkd->bhqk', qi, kf) * scale + bf
        tpos = t0 + jnp.arange(bq)
        mask = spos[None, :] < tpos[:, None]
        log_om = jnp.where(mask, jax.nn.log_sigmoid(-z), 0.0)
        after = lax.cumsum(log_om, axis=3, reverse=True)
        after_excl = jnp.concatenate([after[..., 1:], jnp.zeros_like(after[..., :1])], axis=-1)
        a = jnp.where(mask, jnp.exp(jax.nn.log_sigmoid(z) + after_excl), 0.0)
        return jnp.einsum('bhqk,bhkd->bhqd', a, vf)

    o = lax.map(blk, (qb, starts))
    return jnp.moveaxis(o, 0, 2).reshape(b, h, tq, d)


def layer(x, c, pos, q_pos0, ret_s0, k_past, v_past, conv_past,
          w_ada, b_ada, norm_mix, w_in, ret_gn, sb_gn, sb_bias, w_out,
          norm_ffn, w_up_gate, w_up_val, conv_w, conv_b, w_down):
    b, t, _ = x.shape
    ada = jax.nn.silu(c.astype(jnp.float32)) @ w_ada.astype(jnp.float32) + b_ada.astype(jnp.float32)
    ada = ada.astype(x.dtype)[:, None, :]
    sh_m, sc_m, g_m, sh_f, sc_f, g_f = jnp.split(ada, N_ADA, axis=-1)

    h = rmsnorm(x, norm_mix) * (1 + sc_m) + sh_m
    proj = h @ w_in
    cuts = [RET_W, 2 * RET_W, 3 * RET_W, 4 * RET_W, 4 * RET_W + SB_W, 4 * RET_W + 2 * SB_W]
    rq, rk, rv, rg, sq, sk, sv = jnp.split(proj, cuts, axis=-1)

    q = rotary(heads(rq, H_RET, DK_RET), pos)
    k = rotary(heads(rk, H_RET, DK_RET), pos) * (DK_RET ** -0.5)
    v = heads(rv, H_RET, DV_RET).astype(jnp.float32)
    ro, ret_s = retention(q, k, v, ret_s0)
    mu = jnp.mean(ro, axis=-1, keepdims=True)
    var = jnp.mean(jnp.square(ro - mu), axis=-1, keepdims=True)
    ro = (ro - mu) * lax.rsqrt(var + EPS) * ret_gn.astype(jnp.float32).reshape(H_RET, DV_RET)[None, :, None, :]
    ro = ro.transpose(0, 2, 1, 3).reshape(b, t, RET_W) * jax.nn.silu(rg.astype(jnp.float32))

    qs = heads(sq, H_SB, D_SB)
    ks = heads(sk, H_SB, D_SB)
    vs = heads(sv, H_SB, D_SB)
    k_all = jnp.concatenate([k_past, ks.astype(k_past.dtype)], axis=2)
    v_all = jnp.concatenate([v_past, vs.astype(v_past.dtype)], axis=2)
    so = sb_attention(qs, k_all, v_all, sb_bias, q_pos0)
    so = so * lax.rsqrt(jnp.mean(so * so, axis=-1, keepdims=True) + EPS) \
        * sb_gn.astype(jnp.float32).reshape(H_SB, D_SB)[None, :, None, :]
    so = so.transpose(0, 2, 1, 3).reshape(b, t, SB_W)

    mix = jnp.concatenate([ro, so], axis=-1).astype(x.dtype) @ w_out
    x = x + g_m * mix

    h = rmsnorm(x, norm_ffn) * (1 + sc_f) + sh_f
    a = h @ w_up_gate
    bval = h @ w_up_val
    a_ext = jnp.concatenate([conv_past.astype(a.dtype), a], axis=1)
    conv = conv_b
    for j in range(CONV_W):
        conv = conv + a_ext[:, j:j + t] * conv_w[j]
    ff = (jax.nn.silu(conv) * bval) @ w_down
    x = x + g_f * ff
    return (x, ks.transpose(0, 2, 1, 3), vs.transpose(0, 2, 1, 3), ret_s, a_ext[:, t:])


def setup_inputs(seed: int = 0) -> dict:
    key = jax.random.key(seed)
    ks = jax.random.split(key, 24)
    f32 = jnp.float32
    n_pages = PAST_LEN // PAGE_SIZE
    n_used = DEC_BATCH * n_pages
    n_phys = n_used + max(1, n_used // 4)

    def nrm(k, shape, s):
        return jax.random.normal(k, shape, f32) * s

    page_table = jax.random.permutation(ks[4], n_phys)[:n_used].reshape(DEC_BATCH, n_pages).astype(jnp.int32)
    return {
        'x_prompt': nrm(ks[0], (BATCH, SEQ, D_MODEL), 1.0),
        'x_sample': nrm(ks[1], (DEC_BATCH, DEC_SEQ, D_MODEL), 1.0),
        'cache_k_pages': nrm(ks[2], (DEPTH, n_phys, PAGE_SIZE, H_SB, D_SB), 1.0),
        'cache_v_pages': nrm(ks[3], (DEPTH, n_phys, PAGE_SIZE, H_SB, D_SB), 1.0),
        'page_table': page_table,
        'state_ret': nrm(ks[5], (DEPTH, DEC_BATCH, H_RET, DK_RET, DV_RET), 0.5),
        'state_conv': nrm(ks[6], (DEPTH, DEC_BATCH, CONV_W - 1, D_FF), 1.0),
        'c_prompt': nrm(ks[7], (BATCH, D_MODEL), 1.0),
        'c_sample': nrm(ks[8], (DEC_BATCH, D_MODEL), 1.0),
        'w_ada': nrm(ks[9], (DEPTH, D_MODEL, N_ADA * D_MODEL), 0.5 * D_MODEL ** -0.5),
        'b_ada': nrm(ks[10], (DEPTH, N_ADA * D_MODEL), 0.02),
        'norm_mix': 1.0 + nrm(ks[11], (DEPTH, D_MODEL), 0.02),
        'w_in': nrm(ks[12], (DEPTH, D_MODEL, IN_W), D_MODEL ** -0.5),
        'ret_gn': 1.0 + nrm(ks[13], (DEPTH, RET_W), 0.02),
        'sb_gn': 1.0 + nrm(ks[14], (DEPTH, SB_W), 0.02),
        'sb_bias': SB_BIAS_INIT + nrm(ks[23], (DEPTH, H_SB), 0.1),
        'w_out': nrm(ks[15], (DEPTH, MIX_W, D_MODEL), MIX_W ** -0.5),
        'norm_ffn': 1.0 + nrm(ks[16], (DEPTH, D_MODEL), 0.02),
        'w_up_gate': nrm(ks[17], (DEPTH, D_MODEL, D_FF), D_MODEL ** -0.5),
        'w_up_val': nrm(ks[18], (DEPTH, D_MODEL, D_FF), D_MODEL ** -0.5),
        'conv_w': nrm(ks[19], (DEPTH, CONV_W, D_FF), CONV_W ** -0.5),
        'conv_b': nrm(ks[20], (DEPTH, D_FF), 0.02),
        'w_down': nrm(ks[21], (DEPTH, D_FF, D_MODEL), D_FF ** -0.5),
        'norm_final': 1.0 + nrm(ks[22], (D_MODEL,), 0.02),
    }


def reference(x_prompt, x_sample, cache_k_pages, cache_v_pages, page_table, state_ret, state_conv,
              c_prompt, c_sample, w_ada, b_ada, norm_mix, w_in, ret_gn, sb_gn, sb_bias, w_out,
              norm_ffn, w_up_gate, w_up_val, conv_w, conv_b, w_down, norm_final):
    b, t, _ = x_prompt.shape
    db, ts, _ = x_sample.shape
    past = page_table.shape[1] * cache_k_pages.shape[2]
    pos_p = jnp.arange(t)
    pos_s = past + jnp.arange(ts)
    xp, xs = x_prompt, x_sample
    kp_l, vp_l, ks_l, vs_l, rp_l, rs_l, cp_l, cs_l = [], [], [], [], [], [], [], []
    for l in range(DEPTH):
        w = (w_ada[l], b_ada[l], norm_mix[l], w_in[l], ret_gn[l], sb_gn[l], sb_bias[l], w_out[l],
             norm_ffn[l], w_up_gate[l], w_up_val[l], conv_w[l], conv_b[l], w_down[l])
        xp, kp, vp, rp, cp = layer(
            xp, c_prompt, pos_p, 0,
            jnp.zeros((b, H_RET, DK_RET, DV_RET), jnp.float32),
            jnp.zeros((b, H_SB, 0, D_SB), x_prompt.dtype),
            jnp.zeros((b, H_SB, 0, D_SB), x_prompt.dtype),
            jnp.zeros((b, CONV_W - 1, D_FF), x_prompt.dtype), *w)
        k_past = cache_k_pages[l][page_table].reshape(db, past, H_SB, D_SB).transpose(0, 2, 1, 3)
        v_past = cache_v_pages[l][page_table].reshape(db, past, H_SB, D_SB).transpose(0, 2, 1, 3)
        xs, ks_, vs_, rs, cs = layer(
            xs, c_sample, pos_s, past, state_ret[l], k_past, v_past, state_conv[l], *w)
        kp_l.append(kp); vp_l.append(vp); ks_l.append(ks_); vs_l.append(vs_)
        rp_l.append(rp); rs_l.append(rs); cp_l.append(cp); cs_l.append(cs)
    y_prompt = rmsnorm(xp, norm_final)
    y_sample = rmsnorm(xs, norm_final)
    return (y_prompt, y_sample,
            jnp.stack(kp_l), jnp.stack(vp_l), jnp.stack(ks_l), jnp.stack(vs_l),
            jnp.stack(rp_l), jnp.stack(rs_l), jnp.stack(cp_l), jnp.stack(cs_l))
```

```python
from contextlib import ExitStack
import numpy as np
import concourse.bass as bass
import concourse.mybir as mybir
from concourse.bass_utils import run_bass_kernel_spmd

F32 = mybir.dt.float32
BF16 = mybir.dt.bfloat16
I32 = mybir.dt.int32
U8 = mybir.dt.uint8
PGB = 64 * 128 * 4
AF = mybir.ActivationFunctionType
ALU = mybir.AluOpType

NCORES = 8
D = 1024
KC = 8
DFF = 2816
FC = 22
EPS = 1e-6
NB = 32
TS = 8
T_S = NB * TS
ROPE_BASE = 10000.0

V_BADA = 0
V_NMIX = 48
V_NFFN = 56
V_NFIN = 64
V_CW = 72
V_CB = 138
V_SBGN = 160
V_SBB = 161
V_GC = 162
V_G8 = 163
V_HALO = 164
NV = 165

C_ID = 0
C_TRI = 128
C_OMT = 256
C_ONE = 384
C_MRET = 512
C_MRS = 640
C_DM = 768
C_SN = C_DM + 2048
C_BM = C_SN + 512
C_RM = C_BM + 2048
NCST = C_RM + 16


def bc(ap, axis, n):
    l = [list(x) for x in ap.ap]
    l.insert(axis, [0, n])
    return bass.AP(ap.tensor, ap.offset, l)


def bcl(ap, n):
    l = [list(x) for x in ap.ap]
    assert l[-1][1] == 1
    l[-1] = [0, n]
    return bass.AP(ap.tensor, ap.offset, l)


class Builder:
    def __init__(self, nc):
        self.nc = nc
        self.E = {"pe": nc.tensor, "act": nc.scalar, "dve": nc.vector, "pool": nc.gpsimd, "sp": nc.sync}
        self.esem = {e: nc.alloc_semaphore("es_" + e) for e in self.E}
        self.ecnt = {e: 0 for e in self.E}
        self.dsem = {}
        self.dcnt = {}
        self.seen = {e: {} for e in self.E}
        self.W = {}
        self.R = {}

    def _need(self, eng, reads, writes):
        need = {}

        def add(evs, same_ok):
            for (kind, name), val in evs.items():
                if kind == "e" and name == eng and eng == "pe":
                    continue
                if kind == "d":
                    val = self.dcnt[name]
                k = (kind, name)
                if need.get(k, 0) < val:
                    need[k] = val

        for k in reads:
            add(self.W.get(k, {}), True)
            if k.startswith("p_"):
                add({kk: vv for kk, vv in self.R.get(k, {}).items() if kk != ("e", eng)}, True)
        for k in writes:
            add(self.W.get(k, {}), False)
            add(self.R.get(k, {}), False)
        return need

    def _waits(self, eng, need):
        for k, val in need.items():
            if self.seen[eng].get(k, 0) >= val:
                continue
            sem = self.esem[k[1]] if k[0] == "e" else self.dsem[k[1]]
            self.E[eng].wait_ge(sem, val)
            self.seen[eng][k] = val

    def _post(self, ev, val, reads, writes):
        for k in reads:
            d = self.R.setdefault(k, {})
            if d.get(ev, 0) < val:
                d[ev] = val
        for k in writes:
            self.W[k] = {ev: val}
            self.R[k] = {}

    def sync(self, eng, reads=(), writes=()):
        self._waits(eng, self._need(eng, reads, writes))

    def op(self, eng, fn, reads=(), writes=()):
        self._waits(eng, self._need(eng, reads, writes))
        ins = fn(self.E[eng])
        self.ecnt[eng] += 1
        ins.then_inc(self.esem[eng], 1)
        self._post(("e", eng), self.ecnt[eng], reads, writes)

    def _dsem(self, sem):
        if sem not in self.dsem:
            self.dsem[sem] = self.nc.alloc_semaphore("ds_" + sem)
            self.dcnt[sem] = 0

    def dma(self, q, out, in_, sem, reads=(), writes=()):
        self._waits(q, self._need(q, reads, writes))
        self._dsem(sem)
        self.E[q].dma_start(out=out, in_=in_).then_inc(self.dsem[sem], 16)
        self.dcnt[sem] += 16
        self._post(("d", sem), self.dcnt[sem], reads, writes)

    def cc(self, fn, sem, reads=(), writes=()):
        self._waits("pool", self._need("pool", reads, writes))
        self._dsem(sem)
        fn(self.E["pool"]).then_inc(self.dsem[sem], 1)
        self.dcnt[sem] += 1
        self._post(("d", sem), self.dcnt[sem], reads, writes)

    def barrier(self):
        for eng in self.E:
            need = {}
            for e2 in self.E:
                if e2 != eng and self.ecnt[e2] > 0:
                    need[("e", e2)] = self.ecnt[e2]
            for s, v in self.dcnt.items():
                need[("d", s)] = v
            self._waits(eng, need)
        self.W = {}
        self.R = {}


def build(T_P, NPG, NPHYS, phases="0ABCDE"):
    TPC = T_P // NCORES
    T_ALL = T_P + T_S
    NOWN = 2 + TPC + 4 * TS
    CW = 1024
    NCH = (T_ALL + CW - 1) // CW
    nc = bass.Bass("TRN2", target_bir_lowering=False)
    K = Builder(nc)

    def din(name, shape, dt=F32):
        return nc.dram_tensor(name, list(shape), dt, kind="ExternalInput").ap()

    def dout(name, shape, dt=F32):
        return nc.dram_tensor(name, list(shape), dt, kind="ExternalOutput").ap()

    xT = din("xT", [D, T_ALL])
    xTo = din("xTo", [D, NOWN])
    cT = din("cT", [D, 37])
    w_ada = din("w_ada", [D, 6 * D])
    vecsT = din("vecsT", [128, NV])
    rgn = din("rgn", [128, 128])
    w_c = din("w_c", [D, 576])
    w_rg = din("w_rg", [D, 512])
    w_out = din("w_out", [D, D])
    w_ug = din("w_ug", [D, DFF])
    w_uv = din("w_uv", [D, DFF])
    w_dn = din("w_dn", [DFF, D])
    rot = din("rot", [T_ALL, 8, 64])
    cst = din("cst", [128, NCST])
    poolKT = din("poolKT", [NPHYS * PGB], U8)
    poolV = din("poolV", [NPHYS * PGB], U8)
    ptab = din("ptab", [1, NB * NPG], I32)
    sret = din("sret", [NB, 128, 128])
    sconvT = din("sconvT", [DFF, 4, 2])
    NT = 256
    ptiles = [(0, 2, 0)] + [(2 + j0, min(NT, TPC - j0), 0) for j0 in range(0, TPC, NT)] + [(2 + TPC, 4 * TS, 33)]
    NI = len(ptiles)
    info = din("info", [1, NI * 9], I32)

    yT = dout("yT", [D, TPC + 4 * TS])
    kT_o = dout("kT_o", [64, T_ALL])
    v_o = dout("v_o", [T_ALL, 64])
    rsp = dout("rsp", [128, 128])
    rss = dout("rss", [NB, 128, 128])
    cvp = dout("cvp", [DFF, 2])
    cvs = dout("cvs", [DFF, 4, 2])

    bounce = nc.dram_tensor("bounce", [NCH, 192, CW], BF16).ap()
    g4 = nc.dram_tensor("g4", [NCH, 4 * 192, CW], BF16).ap()
    g8 = nc.dram_tensor("g8", [NCH, 8 * 192, CW], BF16).ap()
    x1s = nc.dram_tensor("x1s", [D, NOWN], F32).ap()

    outer = ExitStack()

    _cnt = [0]

    def mk(st):
        _cnt[0] += 1
        pre = f"s{_cnt[0]}_"

        def sb(name, shape, dt):
            return st.enter_context(nc.sbuf_tensor(pre + name, list(shape), dt))

        def ps(name, shape, dt):
            return st.enter_context(nc.psum_tensor(pre + name, list(shape), dt))
        return sb, ps

    sbP, _ = mk(outer)

    vec = sbP("vec", [128, NV], F32)
    cb = sbP("cstb", [128, NCST], BF16)
    ada = sbP("ada", [128, 48, 37], F32)
    gm1 = sbP("gm1", [128, KC, 37], F32)
    gf1 = sbP("gf1", [128, KC, 37], F32)
    nm32 = sbP("nm32", [128, 24], F32)
    kcon = sbP("kcon", [128, 4], F32)
    rgn_sb = sbP("rgn_sb", [128, 128], F32)

    K.dma("sp", vec[:], vecsT, "ld0", writes=["vec"])
    K.dma("sp", rgn_sb[:], rgn, "ld0", writes=["rgn"])
    for c0 in range(0, NCST, 1024):
        c1 = min(NCST, c0 + 1024)
        K.dma("pool", cb[:, c0:c1], cst[:, c0:c1], "ld1", writes=[f"cst{c0}"])
    K.sync("pool", reads=[f"cst{c0}" for c0 in range(0, NCST, 1024)], writes=["cst"])
    K.op("pool", lambda e: e.memset(kcon[:, 3:4], 0.0), reads=[f"cst{c0}" for c0 in range(0, NCST, 1024)],
         writes=["cst", "kc3"])
    K.op("dve", lambda e: e.memset(kcon[:, 0:1], 1.0), writes=["kc0"])
    K.op("dve", lambda e: e.memset(kcon[:, 1:2], 1024.0 * EPS), writes=["kc1"])
    K.op("dve", lambda e: e.memset(kcon[:, 2:3], EPS), writes=["kc2"])
    KCON = ["kc0", "kc1", "kc2", "kc3"]
    one_c, eps1k_c, eps_c = kcon[:, 0:1], kcon[:, 1:2], kcon[:, 2:3]

    ident = cb[:, C_ID:C_ID + 128]
    tri = cb[:, C_TRI:C_TRI + 128]
    omt = cb[:, C_OMT:C_OMT + 128]
    ones = cb[:, C_ONE:C_ONE + 128]
    ADA = [f"ada{i}" for i in range(48)]

    def rsqrt_ps(out, pin, scale, bias_ap, key_in, key_out, tmp, key_tmp):
        K.op("act", lambda e: e.activation(out=tmp, in_=pin, func=AF.Ln, bias=bias_ap, scale=scale),
             reads=[key_in] + KCON, writes=[key_tmp])
        K.op("act", lambda e: e.activation(out=out, in_=tmp, func=AF.Exp, scale=-0.5),
             reads=[key_tmp], writes=[key_out])

    with ExitStack() as st:
        sb, ps = mk(st)
        c_sb = sb("c_sb", [128, KC, 37], F32)
        s_bf = sb("s_bf", [128, KC, 37], BF16)
        wA = [sb(f"wA{i}", [128, KC, D], BF16) for i in range(2)]
        pA = [ps(f"pA{i}", [128, 512], F32) for i in range(2)]
        K.dma("sp", c_sb[:], cT.rearrange("(kc p) n -> p kc n", p=128), "ld0", writes=["c_sb"])
        K.op("act", lambda e: e.activation(out=s_bf[:], in_=c_sb[:], func=AF.Silu), reads=["c_sb"],
             writes=["s_bf"])
        for j in range(6):
            w = wA[j % 2]
            K.dma("pool", w[:], w_ada[:, j * D:(j + 1) * D].rearrange("(kc p) n -> p kc n", p=128), f"wA{j % 2}",
                  writes=[f"wA{j % 2}"])
            for fo in range(8):
                pp = pA[fo % 2]
                for kc in range(KC):
                    K.op("pe", lambda e, kc=kc, fo=fo, w=w, pp=pp: e.matmul(
                        pp[:, 0:37], lhsT=w[:, kc, fo * 128:(fo + 1) * 128], rhs=s_bf[:, kc, :],
                        start=(kc == 0), stop=(kc == KC - 1)),
                        reads=[f"wA{j % 2}", "s_bf"], writes=[f"pA{fo % 2}"])
                col = j * 8 + fo
                K.op("act", lambda e, col=col, pp=pp: e.activation(
                    out=ada[:, col, :], in_=pp[:, 0:37], func=AF.Identity,
                    bias=vec[:, V_BADA + col:V_BADA + col + 1], scale=1.0),
                    reads=[f"pA{fo % 2}", "vec"], writes=[f"ada{col}"])
        K.op("dve", lambda e: e.tensor_scalar(out=nm32[:], in0=vec[:, V_NMIX:V_NMIX + 24], scalar1=32.0,
                                              scalar2=None, op0=ALU.mult), reads=["vec"], writes=["nm32"])
        for kc in range(KC):
            K.op("dve", lambda e, kc=kc: e.scalar_tensor_tensor(
                out=gm1[:, kc, :], in0=ada[:, 8 + kc, :], scalar=1.0, in1=bcl(nm32[:, kc:kc + 1], 37),
                op0=ALU.add, op1=ALU.mult), reads=ADA + ["nm32"], writes=["gm1"])
            K.op("dve", lambda e, kc=kc: e.scalar_tensor_tensor(
                out=gf1[:, kc, :], in0=ada[:, 32 + kc, :], scalar=1.0, in1=bcl(nm32[:, 8 + kc:9 + kc], 37),
                op0=ALU.add, op1=ALU.mult), reads=ADA + ["nm32"], writes=["gf1"])
        K.barrier()

    def norm_tile(T, xtile, n, xkey, gtab, shrow, scol, hout, hkey):
        sq, p_ssq, rstd, tln, tmpf = T["sq"], T["p_ssq"], T["rstd"], T["tln"], T["tmpf"]
        K.op("act", lambda e: e.activation(out=sq[:, :, :n], in_=xtile[:, :, :n], func=AF.Square),
             reads=[xkey], writes=["sq"])
        for kc in range(KC):
            K.op("pe", lambda e, kc=kc: e.matmul(p_ssq[:, :n], lhsT=ones, rhs=sq[:, kc, :n], start=(kc == 0),
                                                 stop=(kc == KC - 1)), reads=["sq", "cst"], writes=["p_ssq"])
        rsqrt_ps(rstd[:, :n], p_ssq[:, :n], 1.0, eps1k_c, "p_ssq", "rstd", tln[:, :n], "tln")
        if hout is None:
            return
        for kc in range(KC):
            if scol == 0:
                K.op("dve", lambda e, kc=kc: e.scalar_tensor_tensor(
                    out=tmpf[:, :n], in0=xtile[:, kc, :n], scalar=gtab[:, kc, 0:1], in1=rstd[:, :n],
                    op0=ALU.mult, op1=ALU.mult), reads=[xkey, "gm1", "gf1", "rstd"], writes=["tmpf"])
                K.op("act", lambda e, kc=kc: e.activation(
                    out=hout[:, kc, :n], in_=tmpf[:, :n], func=AF.Identity, bias=ada[:, shrow + kc, 0:1],
                    scale=1.0), reads=["tmpf"] + ADA, writes=[hkey])
            else:
                nb = n // TS
                t3 = tmpf[:, :n].rearrange("p (b i) -> p b i", i=TS)
                K.op("dve", lambda e, kc=kc: e.tensor_tensor(out=tmpf[:, :n], in0=xtile[:, kc, :n],
                                                             in1=rstd[:, :n], op=ALU.mult),
                     reads=[xkey, "rstd"], writes=["tmpf"])
                K.op("dve", lambda e, kc=kc, t3=t3: e.tensor_tensor(
                    out=t3, in0=t3, in1=bc(gtab[:, kc, scol:scol + nb], 2, TS), op=ALU.mult),
                    reads=["tmpf", "gm1", "gf1"], writes=["tmpf"])
                K.op("dve", lambda e, kc=kc, t3=t3: e.tensor_tensor(
                    out=hout[:, kc, :n].rearrange("p (b i) -> p b i", i=TS), in0=t3,
                    in1=bc(ada[:, shrow + kc, scol:scol + nb], 2, TS), op=ALU.add),
                    reads=["tmpf"] + ADA, writes=[hkey])

    BNC = []

    main = ExitStack()
    sbM, _ = mk(main)
    QT = sbM("QT", [64, T_ALL], BF16)
    KT = sbM("KT", [64, T_ALL], BF16)
    NBLK = T_ALL // 128
    VA = sbM("VA", [128, NBLK, 64], BF16)
    TT = 256
    tiles = [(t0, min(TT, T_P - t0), False) for t0 in range(0, T_P, TT)] + [(T_P, T_S, True)]
    QTK = [f"QT{i}" for i in range(len(tiles))]
    KTK = [f"KT{i}" for i in range(len(tiles))]
    VAK = [f"VA{i}" for i in range(len(tiles))]

    with ExitStack() as st:
        sb, ps = mk(st)
        Wc = sb("Wc", [128, KC, 576], BF16)
        K.dma("pool", Wc[:], w_c.rearrange("(kc p) n -> p kc n", p=128), "ld1", writes=["Wc"])
        xt = [sb(f"xt{i}", [128, KC, TT], F32) for i in range(2)]
        T = dict(sq=sb("sq", [128, KC, TT], BF16), tmpf=sb("tmpf", [128, TT], F32),
                 rstd=sb("rstd", [128, TT], F32), tln=sb("tln", [128, TT], F32),
                 p_ssq=ps("p_ssq", [128, 512], F32))
        p_ssq = T["p_ssq"]
        hb = sb("hb", [128, KC, TT], BF16)
        kst = [sb(f"kst{i}", [64, TT], F32) for i in range(2)]
        vst = [sb(f"vst{i}", [128, TT // 128, 64], F32) for i in range(2)]
        rt = [sb(f"rt{i}", [128, TT // 128, 8, 64], F32) for i in range(2)]
        ta = sb("ta", [128, 2, 64], F32)
        tb = sb("tb", [128, 2, 64], F32)
        qr = sb("qr", [128, 128], BF16)
        kr = sb("kr", [128, 128], BF16)
        vr = sb("vr", [128, 128], BF16)
        qT = sb("qTr", [128, 128], BF16)
        kTt = sb("kTr", [128, 128], BF16)
        scm = sb("scm", [128, 128], BF16)
        S32 = sb("S32", [128, 128], F32)
        Sbf = sb("Sbf", [128, 128], BF16)
        stmp = sb("stmp", [128, 128], F32)
        SD = nc.vector.BN_STATS_DIM
        AD = nc.vector.BN_AGGR_DIM
        bst = sb("bst", [128, SD], F32)
        mv = sb("mv", [128, AD], F32)
        rs2 = sb("rs2", [128, 2], F32)
        onr = sb("onr", [128, 128], F32)
        onb = sb("onb", [128, 128], BF16)
        rost = [sb(f"rost{i}", [128, TT], BF16) for i in range(2)]
        s0f = sb("s0f", [128, 16, 128], F32)
        s0b = sb("s0b", [128, 16, 128], BF16)
        qm = sb("qm", [128, 16, 128], BF16)
        km = sb("km", [128, 16, 128], BF16)
        snew = sb("snew", [128, 4, 128], F32)
        p_q = ps("p_q", [128, 512], F32)
        p_k = ps("p_k", [128, 512], F32)
        p_v = ps("p_v", [128, 8, 64], F32)
        p_r = ps("p_r", [128, 4, 128], F32)
        p_t = ps("p_t", [128, 8, 128], BF16)
        p_s = ps("p_s", [128, 4, 128], F32)
        p_o = ps("p_o", [128, 512], F32)

        used = T_ALL - (NCH - 1) * CW
        if used < CW:
            zt = sb("zt", [128, CW - used], BF16)
            K.op("dve", lambda e: e.memset(zt[:], 0.0), writes=["zt"])
            K.dma("pool", bounce[NCH - 1, 0:128, used:CW], zt[:], "st_b", reads=["zt"], writes=["bnc_pad0"])
            K.dma("pool", bounce[NCH - 1, 128:192, used:CW], zt[0:64, :], "st_b", reads=["zt"],
                  writes=["bnc_pad1"])
            BNC += ["bnc_pad0", "bnc_pad1"]
        K.op("dve", lambda e: e.memset(S32[:], 0.0), writes=["S32"])
        K.op("dve", lambda e: e.memset(Sbf[:], 0.0), writes=["Sbf"])
        gC = vec[:, V_GC:V_GC + 1]
        g8c = vec[:, V_G8:V_G8 + 1]
        mret = cb[:, C_MRET:C_MRET + 128]
        mrs = cb[:, C_MRS:C_MRS + 128]

        for ti, (t0, n, is_s) in enumerate(tiles if "A" in phases else []):
            sl = ti % 2
            xk = f"xt{sl}"
            nch = n // 128
            K.dma("sp", xt[sl][:, :, :n], xT[:, t0:t0 + n].rearrange("(kc p) n -> p kc n", p=128), xk,
                  writes=[xk])
            K.dma("sp", rt[sl][:, :nch], rot[t0:t0 + n].rearrange("(c p) a k -> p c a k", p=128), f"rt{sl}",
                  writes=[f"rt{sl}"])
            if LV < 2:
                continue
            norm_tile(T, xt[sl], n, xk, gm1, 0, (1 if is_s else 0), hb, "hb")
            if LV < 3:
                continue
            for kc in range(KC):
                K.op("pe", lambda e, kc=kc: e.matmul(p_q[0:64, :n], lhsT=Wc[:, kc, 0:64], rhs=hb[:, kc, :n],
                                                     start=(kc == 0), stop=(kc == KC - 1)),
                     reads=["Wc", "hb"], writes=["p_q"])
            K.op("act", lambda e: e.activation(out=QT[:, t0:t0 + n], in_=p_q[0:64, :n], func=AF.Identity,
                                               scale=0.125), reads=["p_q"], writes=[f"QT{ti}"])
            if LV < 4:
                continue
            for kc in range(KC):
                K.op("pe", lambda e, kc=kc: e.matmul(p_k[0:64, :n], lhsT=Wc[:, kc, 64:128], rhs=hb[:, kc, :n],
                                                     start=(kc == 0), stop=(kc == KC - 1)),
                     reads=["Wc", "hb"], writes=["p_k"])
            if "k4" in DBG:
                K.op("act", lambda e: e.activation(out=KT[:, t0:t0 + n], in_=p_k[0:64, :n], func=AF.Identity),
                     reads=["p_k"], writes=[f"KT{ti}"])
            else:
                K.op("dve", lambda e: e.tensor_copy(out=KT[:, t0:t0 + n], in_=p_k[0:64, :n]), reads=["p_k"],
                     writes=[f"KT{ti}"])
            if "k5" not in DBG:
                K.op("act", lambda e: e.activation(out=kst[sl][:, :n], in_=p_k[0:64, :n], func=AF.Identity),
                     reads=["p_k"], writes=[f"kst{sl}"])
            if "k1" not in DBG:
                K.dma("sp" if "k3" in DBG else "pool", kT_o[:, t0:t0 + n], kst[sl][:, :n], f"st_k{sl}",
                      reads=[f"kst{sl}"], writes=[f"kTo{ti}"])
            if LV < 5:
                continue
            for c in range(nch):
                for kc in range(KC):
                    K.op("pe", lambda e, kc=kc, c=c: e.matmul(p_v[:, c, :], lhsT=hb[:, kc, c * 128:(c + 1) * 128],
                                                              rhs=Wc[:, kc, 128:192], start=(kc == 0),
                                                              stop=(kc == KC - 1)),
                         reads=["Wc", "hb"], writes=["p_v"])
            b0 = t0 // 128
            K.op("dve", lambda e: e.tensor_copy(out=VA[:, b0:b0 + nch, :], in_=p_v[:, :nch, :]), reads=["p_v"],
                 writes=[f"VA{ti}"])
            K.op("act", lambda e: e.activation(out=vst[sl][:, :nch, :], in_=p_v[:, :nch, :], func=AF.Identity),
                 reads=["p_v"], writes=[f"vst{sl}"])
            K.dma("pool", v_o[t0:t0 + n, :].rearrange("(c p) d -> p c d", p=128), vst[sl][:, :nch, :],
                  f"st_v{sl}", reads=[f"vst{sl}"], writes=[f"vo{ti}"])
            for c in range(nch if "R" not in DBG else 0):
                for j in range(3):
                    for kc in range(KC):
                        K.op("pe", lambda e, kc=kc, c=c, j=j: e.matmul(
                            p_r[:, j, :], lhsT=hb[:, kc, c * 128:(c + 1) * 128],
                            rhs=Wc[:, kc, 192 + 128 * j:320 + 128 * j], start=(kc == 0), stop=(kc == KC - 1)),
                            reads=["Wc", "hb"], writes=["p_r"])
                for j, dst, dk_ in ((0, qr, "qr"), (1, kr, "kr")):
                    src = p_r[:, j, :].rearrange("p (a k) -> p a k", a=2)
                    K.op("dve", lambda e, j=j, src=src, c=c: e.tensor_tensor(
                        out=ta[:], in0=src, in1=rt[sl][:, c, 4 * j:4 * j + 2, :], op=ALU.mult),
                        reads=["p_r", f"rt{sl}"], writes=["ta"])
                    K.op("dve", lambda e, j=j, src=src, c=c: e.tensor_tensor(
                        out=tb[:], in0=src, in1=rt[sl][:, c, 4 * j + 2:4 * j + 4, :], op=ALU.mult),
                        reads=["p_r", f"rt{sl}"], writes=["tb"])
                    K.op("pool", lambda e, dst=dst: e.tensor_tensor(out=dst[:, 0:64], in0=ta[:, 0, :],
                                                                    in1=ta[:, 1, :], op=ALU.subtract),
                         reads=["ta"], writes=[dk_ + "a"])
                    K.op("pool", lambda e, dst=dst: e.tensor_tensor(out=dst[:, 64:128], in0=tb[:, 0, :],
                                                                    in1=tb[:, 1, :], op=ALU.add),
                         reads=["tb"], writes=[dk_ + "b"])
                K.op("act", lambda e: e.activation(out=vr[:], in_=p_r[:, 2, :], func=AF.Identity), reads=["p_r"],
                     writes=["vr"])
                K.op("pe", lambda e: e.transpose(out=p_t[:, 0, :], in_=qr[:], identity=ident),
                     reads=["qra", "qrb", "cst"], writes=["p_t"])
                K.op("pe", lambda e: e.transpose(out=p_t[:, 1, :], in_=kr[:], identity=ident),
                     reads=["kra", "krb", "cst"], writes=["p_t"])
                K.op("act", lambda e: e.activation(out=qT[:], in_=p_t[:, 0, :], func=AF.Identity), reads=["p_t"],
                     writes=["qT"])
                K.op("dve", lambda e: e.tensor_copy(out=kTt[:], in_=p_t[:, 1, :]), reads=["p_t"],
                     writes=["kTt"])
                K.op("pe", lambda e: e.matmul(p_s[:, 0, :], lhsT=kTt[:], rhs=qT[:], start=True, stop=True),
                     reads=["kTt", "qT"], writes=["p_s"])
                msk = mrs if is_s else mret
                K.op("dve", lambda e, msk=msk: e.tensor_tensor(out=scm[:], in0=p_s[:, 0, :], in1=msk,
                                                               op=ALU.mult),
                     reads=["p_s", "cst"], writes=["scm"])
                if not is_s:
                    K.op("pe", lambda e: e.matmul(p_o[:, 0:128], lhsT=scm[:], rhs=vr[:], start=True, stop=False),
                         reads=["scm", "vr"], writes=["p_o"])
                    K.op("pe", lambda e: e.matmul(p_o[:, 0:128], lhsT=qT[:], rhs=Sbf[:], start=False, stop=True),
                         reads=["qT", "Sbf"], writes=["p_o"])
                    K.op("pe", lambda e: e.matmul(p_s[:, 1, :], lhsT=kr[:], rhs=vr[:], start=True, stop=True),
                         reads=["kra", "krb", "vr"], writes=["p_s"])
                    K.op("dve", lambda e: e.tensor_tensor(out=stmp[:], in0=S32[:], in1=p_s[:, 1, :], op=ALU.add),
                         reads=["S32", "p_s"], writes=["stmp"])
                    K.op("act", lambda e: e.activation(out=S32[:], in_=stmp[:], func=AF.Identity, scale=gC),
                         reads=["stmp", "vec"], writes=["S32"])
                    K.op("act", lambda e: e.activation(out=Sbf[:], in_=stmp[:], func=AF.Identity, scale=gC),
                         reads=["stmp", "vec"], writes=["Sbf"])
                else:
                    blk = c
                    K.dma("sp", s0f[:], sret[blk * 16:(blk + 1) * 16].rearrange("b k v -> k b v"), "s0f",
                          writes=["s0f"])
                    K.op("act", lambda e: e.activation(out=s0b[:], in_=s0f[:], func=AF.Identity), reads=["s0f"],
                         writes=["s0b"])
                    K.op("dve", lambda e: e.tensor_tensor(
                        out=qm[:], in0=bc(qT[:], 1, 16),
                        in1=cb[:, C_BM:C_BM + 2048].rearrange("p (b n) -> p b n", b=16), op=ALU.mult),
                        reads=["qT", "cst"], writes=["qm"])
                    K.op("dve", lambda e: e.tensor_tensor(out=km[:], in0=bc(kr[:], 1, 16),
                                                          in1=bc(cb[:, C_RM:C_RM + 16], 2, 128), op=ALU.mult),
                         reads=["kra", "krb", "cst"], writes=["km"])
                    K.op("pe", lambda e: e.matmul(p_o[:, 0:128], lhsT=scm[:], rhs=vr[:], start=True, stop=False),
                         reads=["scm", "vr"], writes=["p_o"])
                    for b in range(16):
                        K.op("pe", lambda e, b=b: e.matmul(p_o[:, 0:128], lhsT=qm[:, b, :], rhs=s0b[:, b, :],
                                                           start=False, stop=(b == 15)),
                             reads=["qm", "s0b"], writes=["p_o"])
                    for g in range(4):
                        for bb in range(4):
                            b = g * 4 + bb
                            K.op("pe", lambda e, b=b, bb=bb: e.matmul(p_ssq[:, bb * 128:(bb + 1) * 128],
                                                                      lhsT=km[:, b, :], rhs=vr[:], start=True,
                                                                      stop=True),
                                 reads=["km", "vr"], writes=["p_ssq"])
                        K.op("dve", lambda e, g=g: e.tensor_tensor(
                            out=snew[:], in0=s0f[:, g * 4:(g + 1) * 4, :],
                            in1=p_ssq[:].rearrange("p (a d) -> p a d", a=4), op=ALU.add),
                            reads=["s0f", "p_ssq"], writes=["snew"])
                        K.op("act", lambda e: e.activation(out=snew[:], in_=snew[:], func=AF.Identity, scale=g8c),
                             reads=["snew", "vec"], writes=["snew"])
                        K.dma("pool", rss[blk * 16 + g * 4:blk * 16 + g * 4 + 4].rearrange("b k v -> k b v"),
                              snew[:], "st_rss", reads=["snew"], writes=[f"rss{blk}_{g}"])
                K.op("dve", lambda e: e.bn_stats(out=bst[:], in_=p_o[:, 0:128]), reads=["p_o"], writes=["bst"])
                K.op("dve", lambda e: e.bn_aggr(out=mv[:], in_=bst[:]), reads=["bst"], writes=["mv"])
                rsqrt_ps(rs2[:, 0:1], mv[:, 1:2], 1.0, eps_c, "mv", "rs2", rs2[:, 1:2], "rs2t")
                K.op("dve", lambda e: e.tensor_scalar(out=onr[:], in0=p_o[:, 0:128], scalar1=mv[:, 0:1],
                                                      scalar2=rs2[:, 0:1], op0=ALU.subtract, op1=ALU.mult),
                     reads=["p_o", "mv", "rs2"], writes=["onr"])
                K.op("pool", lambda e: e.tensor_tensor(out=onb[:], in0=onr[:], in1=rgn_sb[:], op=ALU.mult),
                     reads=["onr", "rgn"], writes=["onb"])
                K.op("pe", lambda e: e.transpose(out=p_t[:, 2, :], in_=onb[:], identity=ident),
                     reads=["onb", "cst"], writes=["p_t"])
                K.op("act", lambda e, c=c: e.activation(out=rost[sl][:, c * 128:(c + 1) * 128], in_=p_t[:, 2, :],
                                                        func=AF.Identity), reads=["p_t"], writes=[f"rost{sl}"])
            if "R" in DBG:
                continue
            K.dma("pool", bounce[t0 // CW, 64:192, t0 % CW:t0 % CW + n], rost[sl][:, :n], f"st_ro{sl}",
                  reads=[f"rost{sl}"], writes=[f"bnc_ro{ti}"])
            BNC.append(f"bnc_ro{ti}")
        if "A" in phases:
            K.dma("pool", rsp, S32[:], "st_rsp", reads=["S32"], writes=["rsp"])
        K.barrier()

    with ExitStack() as st:
        sb, ps = mk(st)
        NE = 4
        e_t = [sb(f"e_t{i}", [128, 512], F32) for i in range(NE)]
        L_t = [sb(f"L_t{i}", [128, 512], BF16) for i in range(NE)]
        X_t = [sb(f"X_t{i}", [128, 512], F32) for i in range(2)]
        A_t = [sb(f"A_t{i}", [128, 512], BF16) for i in range(2)]
        osq = sb("osq", [64, 512], BF16)
        orr = sb("orr", [64, 512], F32)
        otl = sb("otl", [64, 512], F32)
        sost = [sb(f"sost{i}", [64, 512], BF16) for i in range(2)]
        p_z = [ps(f"p_z{i}", [128, 512], F32) for i in range(3)]
        pCs = {"B": ps("p_CB", [128, 512], F32), "C": ps("p_CC", [128, 512], F32)}
        pOs = {"B": ps("p_OB", [128, 512], F32)[0:64, :], "C": ps("p_OC", [128, 512], F32)[0:64, :]}
        p_n = ps("p_n", [128, 512], F32)[0:64, :]
        bias_c = vec[:, V_SBB:V_SBB + 1]
        gn_c = vec[0:64, V_SBGN:V_SBGN + 1]
        doB, doC = "B" in phases, "C" in phases
        if doC:
            kstg = [sb(f"kstg{i}", [64, NB, 128], F32) for i in range(2)]
            vstg = [sb(f"vstg{i}", [128, NB, 64], F32) for i in range(2)]
            kbf = sb("kbf", [64, NB, 128], BF16)
            vbf = sb("vbf", [128, NB, 64], BF16)
            pt2 = sb("pt_sb", [NB, NPG], I32)
            K.dma("sp", pt2[:], ptab.rearrange("o (b g) -> (o b) g", g=NPG), "ld_pt", writes=["pt_sb"])
            K.op("dve", lambda e: e.tensor_single_scalar(out=pt2[:], in_=pt2[:], scalar=15,
                                                         op=ALU.logical_shift_left), reads=["pt_sb"],
                 writes=["pt2", "pt_sb"])

        def so_epilogue(kind, nq, col0, si):
            sl = si % 2
            pO, ok = pOs[kind], "p_O" + kind
            K.op("act", lambda e: e.activation(out=osq[:, :nq], in_=pO[:, :nq], func=AF.Square), reads=[ok],
                 writes=["osq"])
            K.op("pe", lambda e: e.matmul(p_n[:, :nq], lhsT=ones[0:64, 0:64], rhs=osq[:, :nq], start=True,
                                          stop=True), reads=["osq", "cst"], writes=["p_n"])
            rsqrt_ps(orr[:, :nq], p_n[:, :nq], 1.0 / 64.0, eps_c[0:64], "p_n", "orr", otl[:, :nq], "otl")
            K.op("dve", lambda e: e.scalar_tensor_tensor(out=sost[sl][:, :nq], in0=pO[:, :nq], scalar=gn_c,
                                                         in1=orr[:, :nq], op0=ALU.mult, op1=ALU.mult),
                 reads=[ok, "orr", "vec"], writes=[f"sost{sl}"])
            K.dma("pool", bounce[col0 // CW, 0:64, col0 % CW:col0 % CW + nq], sost[sl][:, :nq], f"st_so{sl}",
                  reads=[f"sost{sl}"], writes=[f"bnc_so{col0}"])
            BNC.append(f"bnc_so{col0}")

        jobsB = []
        if doB:
            for sbi in range(T_P // 512):
                nkb = 4 * sbi + 4
                for idx, kb in enumerate(reversed(range(nkb))):
                    mk_ = None
                    if kb >= 4 * sbi:
                        j = kb - 4 * sbi
                        mk_ = cb[:, C_DM + 512 * j:C_DM + 512 * (j + 1)]
                    jobsB.append(dict(kind="B", sbi=sbi, kb=kb, first=idx == 0, last=kb == 0, mask=mk_, nq=512))
        jobsC = []
        if doC:
            steps = [("new", 0), ("new", 1)] + [("pg", g) for g in reversed(range(NPG))]
            for si, (kd, g) in enumerate(steps):
                mk_ = cb[:, C_SN + 256 * g:C_SN + 256 * (g + 1)] if kd == "new" else None
                jobsC.append(dict(kind="C", sub=kd, g=g, si=si, first=si == 0, last=si == len(steps) - 1,
                                  mask=mk_, nq=T_S))
        jobs = []
        if jobsB and jobsC:
            r = max(1, len(jobsB) // len(jobsC))
            ci = 0
            for bi, jb in enumerate(jobsB):
                jobs.append(jb)
                if (bi + 1) % r == 0 and ci < len(jobsC):
                    jobs.append(jobsC[ci])
                    ci += 1
            jobs += jobsC[ci:]
        else:
            jobs = jobsB + jobsC
        N = len(jobs)

        def c_load(si):
            if si >= len(jobsC) or jobsC[si]["sub"] != "pg":
                return
            g = jobsC[si]["g"]
            sg = si % 2
            kkeys = [f"kstg{sg}_{b}" for b in range(NB)]
            vkeys = [f"vstg{sg}_{b}" for b in range(NB)]
            K.sync("sp", reads=["pt2"], writes=kkeys + vkeys)
            for b in range(NB):
                pg = nc.sync.value_load(pt2[b:b + 1, g:g + 1])
                K.dma("sp", kstg[sg][:, b, :].bitcast(U8),
                      poolKT[bass.ds(pg, PGB)].rearrange("(d k) -> d k", k=512), f"kstg{sg}", writes=[kkeys[b]])
                K.dma("sp", vstg[sg][:, b, :].bitcast(U8),
                      poolV[bass.ds(pg, PGB)].rearrange("(k d) -> k d", d=256), f"vstg{sg}", writes=[vkeys[b]])
                nc.sync.free_register(nc.sync.to_reg(pg))

        def j_z(i):
            jb = jobs[i]
            zs = i % 3
            zk = f"p_z{zs}"
            nq = jb["nq"]
            if jb["kind"] == "B":
                q0, kb = jb["sbi"] * 512, jb["kb"]
                K.op("pe", lambda e: e.matmul(p_z[zs][:, :], lhsT=KT[:, kb * 128:(kb + 1) * 128],
                                              rhs=QT[:, q0:q0 + 512], start=True, stop=True),
                     reads=KTK + QTK, writes=[zk])
            elif jb["sub"] == "new":
                blk = jb["g"]
                K.op("pe", lambda e: e.matmul(p_z[zs][:, 0:T_S], lhsT=KT[:, T_P + 128 * blk:T_P + 128 * (blk + 1)],
                                              rhs=QT[:, T_P:T_ALL], start=True, stop=True),
                     reads=KTK + QTK, writes=[zk])
            else:
                sg = jb["si"] % 2
                kkeys = [f"kstg{sg}_{b}" for b in range(NB)]
                K.op("dve", lambda e: e.tensor_copy(out=kbf[:], in_=kstg[sg][:]), reads=kkeys, writes=["kbf"])
                for b in range(NB):
                    K.op("pe", lambda e, b=b: e.matmul(
                        p_z[zs][:, b * TS:(b + 1) * TS], lhsT=kbf[:, b, :],
                        rhs=QT[:, T_P + b * TS:T_P + (b + 1) * TS], start=True, stop=True),
                        reads=["kbf"] + QTK, writes=[zk])
            s4 = i % NE
            zp = p_z[zs][:, :nq]
            K.op("act", lambda e: e.activation(out=e_t[s4][:, :nq], in_=zp, func=AF.Exp, bias=bias_c, scale=1.0),
                 reads=[zk, "vec"], writes=[f"e{s4}"])
            if jb["mask"] is not None:
                K.op("dve", lambda e: e.tensor_tensor(out=e_t[s4][:, :nq], in0=e_t[s4][:, :nq], in1=jb["mask"],
                                                      op=ALU.mult), reads=[f"e{s4}", "cst"], writes=[f"e{s4}"])
            K.op("act", lambda e: e.activation(out=L_t[s4][:, :nq], in_=e_t[s4][:, :nq], func=AF.Ln, bias=one_c,
                                               scale=1.0), reads=[f"e{s4}"] + KCON, writes=[f"L{s4}"])

        def j_acc(i):
            jb = jobs[i]
            s4, sl, nq = i % NE, i % 2, jb["nq"]
            pC, ck = pCs[jb["kind"]], "p_C" + jb["kind"]
            K.op("pe", lambda e: e.matmul(pC[:, :nq], lhsT=tri, rhs=L_t[s4][:, :nq], start=jb["first"], stop=True,
                                          skip_group_check=True), reads=[f"L{s4}", "cst"], writes=[ck])
            K.op("act", lambda e: e.activation(out=X_t[sl][:, :nq], in_=pC[:, :nq], func=AF.Exp, scale=-1.0),
                 reads=[ck], writes=[f"X{sl}"])
            K.op("pe", lambda e: e.matmul(pC[:, :nq], lhsT=omt, rhs=L_t[s4][:, :nq], start=False, stop=True,
                                          skip_group_check=True), reads=[f"L{s4}", "cst", f"X{sl}"],
                 writes=[ck])
            K.op("dve", lambda e: e.tensor_tensor(out=A_t[sl][:, :nq], in0=e_t[s4][:, :nq], in1=X_t[sl][:, :nq],
                                                  op=ALU.mult), reads=[f"e{s4}", f"X{sl}"], writes=[f"A{sl}"])

        def j_av(i):
            jb = jobs[i]
            sl, nq = i % 2, jb["nq"]
            pO, ok = pOs[jb["kind"]], "p_O" + jb["kind"]
            if jb["kind"] == "B":
                kb = jb["kb"]
                K.op("pe", lambda e: e.matmul(pO[:, :], lhsT=VA[:, kb, :], rhs=A_t[sl][:, :], start=jb["first"],
                                              stop=jb["last"], skip_group_check=True),
                     reads=VAK + [f"A{sl}"], writes=[ok])
                if jb["last"]:
                    so_epilogue("B", 512, jb["sbi"] * 512, jb["sbi"])
                return
            if jb["sub"] == "new":
                blk = jb["g"]
                K.op("pe", lambda e: e.matmul(pO[:, 0:T_S], lhsT=VA[:, T_P // 128 + blk, :], rhs=A_t[sl][:, 0:T_S],
                                              start=(blk == 0), stop=False, skip_group_check=True),
                     reads=VAK + [f"A{sl}"], writes=[ok])
            else:
                sg = jb["si"] % 2
                vkeys = [f"vstg{sg}_{b}" for b in range(NB)]
                K.op("dve", lambda e: e.tensor_copy(out=vbf[:], in_=vstg[sg][:]), reads=vkeys, writes=["vbf"])
                for b in range(NB):
                    K.op("pe", lambda e, b=b: e.matmul(
                        pO[:, b * TS:(b + 1) * TS], lhsT=vbf[:, b, :], rhs=A_t[sl][:, b * TS:(b + 1) * TS],
                        start=False, stop=jb["last"], skip_group_check=True), reads=["vbf", f"A{sl}"],
                        writes=[ok])
            c_load(jb["si"] + 2)
            if jb["last"]:
                so_epilogue("C", T_S, T_P, 1)

        if N:
            j_z(0)
            if N > 1:
                j_z(1)
            for i in range(N):
                if i + 2 < N:
                    j_z(i + 2)
                j_acc(i)
                if i >= 1:
                    j_av(i - 1)
            j_av(N - 1)
        K.barrier()
    main.close()

    if "D" in phases:
        for k in range(NCH):
            K.cc(lambda g, k=k: g.collective_compute("AllGather", ALU.bypass,
                                                     replica_groups=[[0, 1, 2, 3], [4, 5, 6, 7]],
                                                     ins=[bounce[k]], outs=[g4[k]]), "cc", reads=BNC,
                 writes=[f"g4_{k}"])
        for k in range(NCH):
            K.cc(lambda g, k=k: g.collective_compute("AllGather", ALU.bypass,
                                                     replica_groups=[[0, 4], [1, 5], [2, 6], [3, 7]],
                                                     ins=[g4[k]], outs=[g8[k]]), "cc", reads=[f"g4_{k}"],
                 writes=[f"g8_{k}"])
    G8K = [f"g8_{k}" for k in range(NCH)]

    g8u8 = g8.bitcast(U8).tensor
    RB = CW * 2
    if "E" in phases:
        with ExitStack() as st:
            sb, ps = mk(st)
            Wrg = sb("Wrg", [128, KC, 512], BF16)
            Wo = sb("Wo", [128, KC, D], BF16)
            K.dma("pool", Wrg[:], w_rg.rearrange("(kc p) n -> p kc n", p=128), "ld1", writes=["Wrg"])
            K.dma("pool", Wo[:], w_out.rearrange("(kc p) n -> p kc n", p=128), "ld1", writes=["Wo"])
            inf_sb = sb("inf_sb", [1, NI * 9], I32)
            K.dma("sp", inf_sb[:], info, "ld_pt2", writes=["inf_sb"])
            xo = [sb(f"xo{i}", [128, KC, NT], F32) for i in range(2)]
            T = dict(sq=sb("sq", [128, KC, NT], BF16), tmpf=sb("tmpf", [128, NT], F32),
                     rstd=sb("rstd", [128, NT], F32), tln=sb("tln", [128, NT], F32),
                     p_ssq=ps("p_ssq", [128, 512], F32))
            hE = sb("hE", [128, KC, NT], BF16)
            M = [sb(f"M{i}", [128, 8, NT], BF16) for i in range(2)]
            sg = sb("sg", [128, 4, NT], F32)
            rog = sb("rog", [128, 4, NT], BF16)
            p_g = [ps(f"p_g{i}", [128, 512], F32) for i in range(2)]
            K.sync("sp", reads=["inf_sb"])
            for ti, (j0, n, scol) in enumerate(ptiles):
                sl = ti % 2
                xk = f"xo{sl}"
                K.dma("sp", xo[sl][:, :, :n], xTo[:, j0:j0 + n].rearrange("(kc p) n -> p kc n", p=128), xk,
                      writes=[xk])
                mk_ = f"M{sl}"
                MK = [mk_] + [f"M{sl}_{j}_{hh}" for j in range(4) for hh in range(2)]
                K.sync("sp", reads=G8K, writes=MK)
                col = nc.sync.value_load(inf_sb[0:1, ti * 9:ti * 9 + 1])
                K.dma("sp", M[sl][:, 0:4, :n].bitcast(U8),
                      bass.AP(g8u8, col, [[RB, 128], [192 * RB, 4], [1, 2 * n]]), mk_,
                      reads=G8K, writes=[mk_])
                nc.sync.free_register(nc.sync.to_reg(col))
                for j in range(4):
                    for hh in range(2):
                        i9 = ti * 9 + 1 + 2 * j + hh
                        col = nc.sync.value_load(inf_sb[0:1, i9:i9 + 1])
                        K.dma("sp", M[sl][64 * hh:64 * hh + 64, 4 + j, :n].bitcast(U8),
                              bass.AP(g8u8, col, [[RB, 64], [1, 2 * n]]),
                              mk_, reads=G8K, writes=[f"M{sl}_{j}_{hh}"])
                        nc.sync.free_register(nc.sync.to_reg(col))
                norm_tile(T, xo[sl], n, xk, gm1, 0, scol, hE, "hE")
                for fo in range(4):
                    pp = p_g[fo % 2]
                    for kc in range(KC):
                        K.op("pe", lambda e, kc=kc, fo=fo, pp=pp: e.matmul(
                            pp[:, :n], lhsT=Wrg[:, kc, fo * 128:(fo + 1) * 128], rhs=hE[:, kc, :n],
                            start=(kc == 0), stop=(kc == KC - 1)), reads=["Wrg", "hE"], writes=[f"p_g{fo % 2}"])
                    K.op("act", lambda e, fo=fo, pp=pp: e.activation(out=sg[:, fo, :n], in_=pp[:, :n],
                                                                     func=AF.Silu),
                         reads=[f"p_g{fo % 2}"], writes=["sg"])
                K.op("dve", lambda e, sl=sl: e.tensor_tensor(out=M[sl][:, 0:4, :n], in0=M[sl][:, 0:4, :n],
                                                             in1=sg[:, :, :n], op=ALU.mult),
                     reads=MK + ["sg"], writes=[mk_])
                for fo in range(KC):
                    pp = p_g[fo % 2]
                    for kc in range(KC):
                        K.op("pe", lambda e, kc=kc, fo=fo, pp=pp, sl=sl: e.matmul(
                            pp[:, :n], lhsT=Wo[:, kc, fo * 128:(fo + 1) * 128], rhs=M[sl][:, kc, :n],
                            start=(kc == 0), stop=(kc == KC - 1)), reads=["Wo"] + MK, writes=[f"p_g{fo % 2}"])
                    if scol == 0:
                        K.op("dve", lambda e, fo=fo, pp=pp, sl=sl: e.scalar_tensor_tensor(
                            out=xo[sl][:, fo, :n], in0=pp[:, :n], scalar=ada[:, 16 + fo, 0:1],
                            in1=xo[sl][:, fo, :n], op0=ALU.mult, op1=ALU.add),
                            reads=[f"p_g{fo % 2}", xk] + ADA, writes=[xk])
                    else:
                        nb = n // TS
                        t3 = T["tmpf"][:, :n].rearrange("p (b i) -> p b i", i=TS)
                        K.op("dve", lambda e, fo=fo, pp=pp, t3=t3: e.tensor_tensor(
                            out=t3, in0=pp[:, :n].rearrange("p (b i) -> p b i", i=TS),
                            in1=bc(ada[:, 16 + fo, scol:scol + nb], 2, TS), op=ALU.mult),
                            reads=[f"p_g{fo % 2}"] + ADA, writes=["tmpf"])
                        K.op("dve", lambda e, fo=fo, sl=sl: e.tensor_tensor(
                            out=xo[sl][:, fo, :n], in0=xo[sl][:, fo, :n], in1=T["tmpf"][:, :n], op=ALU.add),
                            reads=["tmpf", xk], writes=[xk])
                K.dma("pool", x1s[:, j0:j0 + n].rearrange("(kc p) n -> p kc n", p=128), xo[sl][:, :, :n],
                      f"st_x1{sl}", reads=[xk], writes=[f"x1s{ti}"])
            K.barrier()

        with ExitStack() as st:
            sb, ps = mk(st)
            Wug = sb("Wug", [128, KC, DFF], BF16)
            Wuv = sb("Wuv", [128, KC, DFF], BF16)
            Wdn = sb("Wdn", [128, FC, D], BF16)
            for hf in range(2):
                cs_ = slice(hf * (DFF // 2), (hf + 1) * (DFF // 2))
                K.dma("pool", Wug[:, :, cs_], w_ug[:, cs_].rearrange("(kc p) n -> p kc n", p=128), "ld1",
                      writes=[f"Wug{hf}"])
                K.dma("pool", Wuv[:, :, cs_], w_uv[:, cs_].rearrange("(kc p) n -> p kc n", p=128), "ld1",
                      writes=[f"Wuv{hf}"])
            K.dma("pool", Wdn[:], w_dn.rearrange("(fc p) n -> p fc n", p=128), "ld1", writes=["Wdn"])
            xo = [sb("xo0", [128, KC, NT], F32)] * 2
            T = dict(sq=sb("sq", [128, KC, NT], BF16), tmpf=sb("tmpf", [128, NT], F32),
                     rstd=sb("rstd", [128, NT], F32), tln=sb("tln", [128, NT], F32),
                     p_ssq=ps("p_ssq", [128, 512], F32))
            hE = sb("hE", [128, KC, NT], BF16)
            uE = sb("uE", [128, FC, NT], BF16)
            aE = [sb(f"aE{i}", [128, NT + 2], F32) for i in range(2)]
            aS = [sb(f"aS{i}", [128, 4, TS + 2], F32) for i in range(2)]
            cry = sb("cry", [128, FC, 2], F32)
            scv = sb("scv", [128, FC, 4, 2], F32)
            cvo = sb("cvo", [128, FC, 4, 2], F32)
            tcv = [sb(f"tcv{i}", [128, NT], F32) for i in range(2)]
            scv2 = [sb(f"sil{i}", [128, NT], F32) for i in range(2)]
            p_a = [ps(f"p_a{i}", [128, 512], F32) for i in range(2)]
            p_b = [ps(f"p_b{i}", [128, 512], F32) for i in range(2)]
            p_f = [ps(f"p_f{i}", [128, 512], F32) for i in range(2)]
            K.op("dve", lambda e: e.memset(cry[:], 0.0), writes=["cry"])
            K.dma("sp", scv[:], sconvT.rearrange("(fc p) b w -> p fc b w", p=128), "ld_pt2", writes=["scv"])
            halo = vec[:, V_HALO:V_HALO + 1]
            for ti, (j0, n, scol) in enumerate(ptiles):
                sl = 0
                xk = f"xo{sl}"
                K.dma("sp", xo[sl][:, :, :n], x1s[:, j0:j0 + n].rearrange("(kc p) n -> p kc n", p=128), xk,
                      reads=[f"x1s{ti}"], writes=[xk])
                norm_tile(T, xo[sl], n, xk, gf1, 24, scol, hE, "hE")
                lastp = (scol == 0 and j0 + n == 2 + TPC)
                for fc in range(FC):
                    s2 = fc % 2
                    pa, pb = p_a[s2], p_b[s2]
                    for kc in range(KC):
                        K.op("pe", lambda e, kc=kc, fc=fc, pa=pa: e.matmul(
                            pa[:, :n], lhsT=Wug[:, kc, fc * 128:(fc + 1) * 128], rhs=hE[:, kc, :n],
                            start=(kc == 0), stop=(kc == KC - 1)), reads=["Wug0", "Wug1", "hE"], writes=[f"p_a{s2}"])
                    for kc in range(KC):
                        K.op("pe", lambda e, kc=kc, fc=fc, pb=pb: e.matmul(
                            pb[:, :n], lhsT=Wuv[:, kc, fc * 128:(fc + 1) * 128], rhs=hE[:, kc, :n],
                            start=(kc == 0), stop=(kc == KC - 1)), reads=["Wuv0", "Wuv1", "hE"], writes=[f"p_b{s2}"])
                    cw = [vec[:, V_CW + j * FC + fc:V_CW + j * FC + fc + 1] for j in range(3)]
                    cbv = vec[:, V_CB + fc:V_CB + fc + 1]
                    tc_, si_ = tcv[s2], scv2[s2]
                    if scol == 0:
                        a_ = aE[s2]
                        ak = f"aE{s2}"
                        K.op("act", lambda e, a_=a_, pa=pa: e.activation(out=a_[:, 2:2 + n], in_=pa[:, :n],
                                                                         func=AF.Identity),
                             reads=[f"p_a{s2}"], writes=[ak])
                        K.op("pool", lambda e, a_=a_, fc=fc: e.tensor_copy(out=a_[:, 0:2], in_=cry[:, fc, :]),
                             reads=["cry"], writes=[ak])
                        if ti == 0:
                            K.op("pool", lambda e, a_=a_: e.tensor_scalar(out=a_[:, 2:4], in0=a_[:, 2:4],
                                                                          scalar1=halo, scalar2=None,
                                                                          op0=ALU.mult),
                                 reads=[ak, "vec"], writes=[ak])
                        K.op("pool", lambda e, a_=a_, fc=fc: e.tensor_copy(out=cry[:, fc, :], in_=a_[:, n:n + 2]),
                             reads=[ak], writes=["cry"])
                        v0, v1, v2 = a_[:, 0:n], a_[:, 1:n + 1], a_[:, 2:n + 2]
                        tv, sv, pbv = tc_[:, :n], si_[:, :n], pb[:, :n]
                        uo = uE[:, fc, :n]
                    else:
                        a_ = aS[s2]
                        ak = f"aS{s2}"
                        K.op("act", lambda e, a_=a_, pa=pa: e.activation(
                            out=a_[:, :, 2:2 + TS], in_=pa[:, :n].rearrange("p (b i) -> p b i", i=TS),
                            func=AF.Identity), reads=[f"p_a{s2}"], writes=[ak])
                        K.op("pool", lambda e, a_=a_, fc=fc: e.tensor_copy(out=a_[:, :, 0:2], in_=scv[:, fc, :, :]),
                             reads=["scv"], writes=[ak])
                        K.op("pool", lambda e, a_=a_, fc=fc: e.tensor_copy(out=cvo[:, fc, :, :],
                                                                           in_=a_[:, :, TS:TS + 2]),
                             reads=[ak], writes=["cvo"])
                        v0, v1, v2 = a_[:, :, 0:TS], a_[:, :, 1:TS + 1], a_[:, :, 2:TS + 2]
                        r3 = lambda ap_: ap_.rearrange("p (b i) -> p b i", i=TS)
                        tv, sv, pbv = r3(tc_[:, :n]), r3(si_[:, :n]), r3(pb[:, :n])
                        uo = r3(uE[:, fc, :n])
                    K.op("dve", lambda e, v0=v0, tv=tv, cw=cw, cbv=cbv: e.tensor_scalar(
                        out=tv, in0=v0, scalar1=cw[0], scalar2=cbv, op0=ALU.mult, op1=ALU.add),
                        reads=[ak, "vec"], writes=[f"tcv{s2}"])
                    K.op("dve", lambda e, v1=v1, tv=tv, cw=cw: e.scalar_tensor_tensor(
                        out=tv, in0=v1, scalar=cw[1], in1=tv, op0=ALU.mult, op1=ALU.add),
                        reads=[ak, "vec", f"tcv{s2}"], writes=[f"tcv{s2}"])
                    K.op("dve", lambda e, v2=v2, tv=tv, cw=cw: e.scalar_tensor_tensor(
                        out=tv, in0=v2, scalar=cw[2], in1=tv, op0=ALU.mult, op1=ALU.add),
                        reads=[ak, "vec", f"tcv{s2}"], writes=[f"tcv{s2}"])
                    K.op("act", lambda e, tv=tv, sv=sv: e.activation(out=sv, in_=tv, func=AF.Silu),
                         reads=[f"tcv{s2}"], writes=[f"sil{s2}"])
                    K.op("dve", lambda e, sv=sv, pbv=pbv, uo=uo: e.tensor_tensor(out=uo, in0=sv, in1=pbv,
                                                                                op=ALU.mult),
                         reads=[f"sil{s2}", f"p_b{s2}"], writes=["uE"])
                if lastp:
                    K.dma("pool", cvp.rearrange("(fc p) w -> p fc w", p=128), cry[:], "st_cv", reads=["cry"],
                          writes=["cvp"])
                if scol != 0:
                    K.dma("pool", cvs.rearrange("(fc p) b w -> p fc b w", p=128), cvo[:], "st_cv", reads=["cvo"],
                          writes=["cvs"])
                for fo in range(KC):
                    pf = p_f[fo % 2]
                    for fc in range(FC):
                        K.op("pe", lambda e, fc=fc, fo=fo, pf=pf: e.matmul(
                            pf[:, :n], lhsT=Wdn[:, fc, fo * 128:(fo + 1) * 128], rhs=uE[:, fc, :n],
                            start=(fc == 0), stop=(fc == FC - 1)), reads=["Wdn", "uE"], writes=[f"p_f{fo % 2}"])
                    if scol == 0:
                        K.op("dve", lambda e, fo=fo, pf=pf, sl=sl: e.scalar_tensor_tensor(
                            out=xo[sl][:, fo, :n], in0=pf[:, :n], scalar=ada[:, 40 + fo, 0:1],
                            in1=xo[sl][:, fo, :n], op0=ALU.mult, op1=ALU.add),
                            reads=[f"p_f{fo % 2}", xk] + ADA, writes=[xk])
                    else:
                        nb = n // TS
                        t3 = T["tmpf"][:, :n].rearrange("p (b i) -> p b i", i=TS)
                        K.op("dve", lambda e, fo=fo, pf=pf, t3=t3: e.tensor_tensor(
                            out=t3, in0=pf[:, :n].rearrange("p (b i) -> p b i", i=TS),
                            in1=bc(ada[:, 40 + fo, scol:scol + nb], 2, TS), op=ALU.mult),
                            reads=[f"p_f{fo % 2}"] + ADA, writes=["tmpf"])
                        K.op("dve", lambda e, fo=fo, sl=sl: e.tensor_tensor(
                            out=xo[sl][:, fo, :n], in0=xo[sl][:, fo, :n], in1=T["tmpf"][:, :n], op=ALU.add),
                            reads=["tmpf", xk], writes=[xk])
                norm_tile(T, xo[sl], n, xk, None, 0, 0, None, None)
                for kc in range(KC):
                    K.op("dve", lambda e, kc=kc, sl=sl: e.scalar_tensor_tensor(
                        out=xo[sl][:, kc, :n], in0=xo[sl][:, kc, :n], scalar=nm32[:, 16 + kc:17 + kc],
                        in1=T["rstd"][:, :n], op0=ALU.mult, op1=ALU.mult), reads=[xk, "nm32", "rstd"],
                        writes=[xk])
                lo = max(j0, 2)
                if j0 + n > lo:
                    K.dma("pool", yT[:, lo - 2:j0 + n - 2].rearrange("(kc p) n -> p kc n", p=128),
                          xo[sl][:, :, lo - j0:n], f"st_y{sl}", reads=[xk], writes=[f"yT{ti}"])
            K.barrier()
    K.barrier()
    outer.close()
    return nc


def _consts(T_P):
    c = np.zeros((128, NCST), np.float32)
    i = np.arange(128)
    c[:, C_ID:C_ID + 128] = np.eye(128)
    tri = (i[:, None] >= i[None, :]).astype(np.float32)
    c[:, C_TRI:C_TRI + 128] = tri
    c[:, C_OMT:C_OMT + 128] = 1.0 - tri
    c[:, C_ONE:C_ONE + 128] = 1.0
    c[:, C_MRET:C_MRET + 128] = (i[None, :] >= i[:, None])
    c[:, C_MRS:C_MRS + 128] = (i[None, :] >= i[:, None]) & ((i[None, :] // 8) == (i[:, None] // 8))
    t = np.arange(512)
    for j in range(4):
        c[:, C_DM + 512 * j:C_DM + 512 * (j + 1)] = ((128 * j + i)[:, None] < t[None, :])
    q = np.arange(256)
    for blk in range(2):
        kb_, ki_ = blk * 16 + i // 8, i % 8
        c[:, C_SN + 256 * blk:C_SN + 256 * (blk + 1)] = (kb_[:, None] == (q // 8)[None, :]) & \
            (ki_[:, None] < (q % 8)[None, :])
    bm = np.zeros((16, 128), np.float32)
    for b in range(16):
        bm[b, 8 * b:8 * b + 8] = 1.0
    c[:, C_BM:C_BM + 2048] = bm.reshape(1, 2048)
    c[:, C_RM:C_RM + 16] = (i[:, None] // 8 == np.arange(16)[None, :])
    return c


def _rot_table(T_P, past, h):
    half = 64
    inv = ROPE_BASE ** (-np.arange(half, dtype=np.float32) / half)
    lg = np.log(1.0 - 2.0 ** (-5.0 - h))
    pos = np.concatenate([np.arange(T_P), np.tile(past + np.arange(TS), NB)]).astype(np.float32)
    idx = np.concatenate([np.arange(T_P) % 128, np.tile(np.arange(TS), NB)]).astype(np.float64)
    ang = pos[:, None] * inv[None, :]
    cos, sin = np.cos(ang), np.sin(ang)
    dq = np.exp((idx + 1.0) * lg)[:, None]
    dk = (np.exp(-(idx + 1.0) * lg) * (128.0 ** -0.5))[:, None]
    tab = np.stack([cos * dq, sin * dq, sin * dq, cos * dq, cos * dk, sin * dk, sin * dk, cos * dk], axis=1)
    return np.ascontiguousarray(tab.astype(np.float32)), float(np.exp(128 * lg)), float(np.exp(8 * lg))


def _info(c, T_P, TPC):
    CW = 1024
    RB = CW * 2
    toks = [max(c * TPC - 2, 0)] + [c * TPC + j0 for j0 in range(0, TPC, 256)] + [T_P + 32 * c]
    out = []
    for t in toks:
        base = (t // CW) * (1536 * RB) + (t % CW) * 2
        out.append(base + 64 * RB)
        for r in range(8):
            out.append(base + r * 192 * RB)
    return np.array([out], np.int32)


PHASES = "0ABCDE"
import os as _os
DBG = _os.environ.get("KDBG", "")
LV = int(_os.environ.get("KLV", "9"))


def kernel(x_prompt, x_sample, cache_k_pages, cache_v_pages, page_table, state_ret, state_conv,
           c_prompt, c_sample, w_ada, b_ada, norm_mix, w_in, ret_gn, sb_gn, sb_bias, w_out,
           norm_ffn, w_up_gate, w_up_val, conv_w, conv_b, w_down, norm_final):
    f = lambda a: np.ascontiguousarray(np.asarray(a, dtype=np.float32))
    x_prompt, x_sample = f(x_prompt), f(x_sample)
    T_P = x_prompt.shape[1]
    NPG = page_table.shape[1]
    NPHYS = cache_k_pages.shape[1]
    past = NPG * cache_k_pages.shape[2]
    TPC = T_P // NCORES
    T_ALL = T_P + T_S
    nc = build(T_P, NPG, NPHYS, PHASES)

    xT = f(np.concatenate([x_prompt[0].T, x_sample.reshape(T_S, D).T], axis=1))
    w_in0 = f(w_in)[0]
    ck = np.asarray(cache_k_pages)[0]
    cv = np.asarray(cache_v_pages)[0]
    cst = _consts(T_P)
    ptab = np.ascontiguousarray(np.asarray(page_table, dtype=np.int32).reshape(1, NB * NPG))
    vbase = np.zeros((128, NV), np.float32)
    vbase[:, V_BADA:V_BADA + 48] = f(b_ada)[0].reshape(48, 128).T
    vbase[:, V_NMIX:V_NMIX + 8] = f(norm_mix)[0].reshape(8, 128).T
    vbase[:, V_NFFN:V_NFFN + 8] = f(norm_ffn)[0].reshape(8, 128).T
    vbase[:, V_NFIN:V_NFIN + 8] = f(norm_final).reshape(8, 128).T
    vbase[:, V_CW:V_CW + 66] = f(conv_w)[0].reshape(3, FC, 128).transpose(2, 0, 1).reshape(128, 66)
    vbase[:, V_CB:V_CB + FC] = f(conv_b)[0].reshape(FC, 128).T
    in_maps = []
    for c in range(NCORES):
        h = c % 4
        rot, gC, g8 = _rot_table(T_P, past, h)
        vec = vbase.copy()
        vec[:64, V_SBGN] = f(sb_gn)[0][64 * c:64 * c + 64]
        vec[:, V_SBB] = f(sb_bias)[0][c]
        vec[:, V_GC] = gC
        vec[:, V_G8] = g8
        vec[:, V_HALO] = 0.0 if c == 0 else 1.0
        cols = np.r_[0:64] + 64 * c
        rc = np.r_[0:128] + 128 * h
        w_c = np.concatenate([w_in0[:, 2048 + cols], w_in0[:, 2560 + cols], w_in0[:, 3072 + cols],
                              w_in0[:, rc], w_in0[:, 512 + rc], w_in0[:, 1024 + rc]], axis=1)
        xo = np.zeros((D, 2 + TPC + 4 * TS), np.float32)
        if c > 0:
            xo[:, 0:2] = xT[:, c * TPC - 2:c * TPC]
        xo[:, 2:2 + TPC] = xT[:, c * TPC:(c + 1) * TPC]
        xo[:, 2 + TPC:] = xT[:, T_P + 32 * c:T_P + 32 * (c + 1)]
        cT = np.concatenate([f(c_prompt).T, f(c_sample).T, f(c_sample)[4 * c:4 * c + 4].T], axis=1)
        in_maps.append(dict(
            xT=xT, xTo=xo, cT=f(cT), w_ada=f(w_ada)[0], vecsT=vec,
            rgn=f(np.tile(f(ret_gn)[0][128 * h:128 * h + 128][None, :], (128, 1))),
            w_c=f(w_c), w_rg=f(w_in0[:, 1536:2048]), w_out=f(w_out)[0], w_ug=f(w_up_gate)[0],
            w_uv=f(w_up_val)[0], w_dn=f(w_down)[0], rot=rot, cst=cst,
            poolKT=f(ck[:, :, c, :].transpose(0, 2, 1)).view(np.uint8).reshape(-1),
            poolV=f(cv[:, :, c, :]).view(np.uint8).reshape(-1), ptab=ptab,
            sret=f(np.asarray(state_ret)[0][:, h]),
            sconvT=f(np.asarray(state_conv)[0][4 * c:4 * c + 4].transpose(2, 0, 1)),
            info=_info(c, T_P, TPC)))
    res = run_bass_kernel_spmd(nc, in_maps, core_ids=list(range(NCORES))).results

    y_prompt = np.zeros((1, T_P, D), np.float32)
    y_sample = np.zeros((NB, TS, D), np.float32)
    kp = np.zeros((1, 1, T_P, 8, 64), np.float32)
    vp = np.zeros((1, 1, T_P, 8, 64), np.float32)
    ks = np.zeros((1, NB, TS, 8, 64), np.float32)
    vs = np.zeros((1, NB, TS, 8, 64), np.float32)
    rp = np.zeros((1, 1, 4, 128, 128), np.float32)
    rs = np.zeros((1, NB, 4, 128, 128), np.float32)
    cp = np.zeros((1, 1, 2, DFF), np.float32)
    cs = np.zeros((1, NB, 2, DFF), np.float32)
    for c in range(NCORES):
        r = res[c]
        y_prompt[0, c * TPC:(c + 1) * TPC] = r["yT"][:, :TPC].T
        y_sample[4 * c:4 * c + 4] = r["yT"][:, TPC:].T.reshape(4, TS, D)
        kT = r["kT_o"]
        kp[0, 0, :, c, :] = kT[:, :T_P].T
        ks[0, :, :, c, :] = kT[:, T_P:].T.reshape(NB, TS, 64)
        vp[0, 0, :, c, :] = r["v_o"][:T_P]
        vs[0, :, :, c, :] = r["v_o"][T_P:].reshape(NB, TS, 64)
        if c < 4:
            rp[0, 0, c] = r["rsp"]
            rs[0, :, c] = r["rss"]
        cs[0, 4 * c:4 * c + 4] = r["cvs"].transpose(1, 2, 0)
        if c == NCORES - 1:
            cp[0, 0] = r["cvp"].T
    return (y_prompt, y_sample, kp, vp, ks, vs, rp, rs, cp, cs)
```

```python
from contextlib import ExitStack
import os as _os
import numpy as np
import concourse.bass as bass
import concourse.mybir as mybir
from concourse.bass_utils import run_bass_kernel_spmd

F32 = mybir.dt.float32
BF16 = mybir.dt.bfloat16
I32 = mybir.dt.int32
U8 = mybir.dt.uint8
PGB = 64 * 128 * 4
AF = mybir.ActivationFunctionType
ALU = mybir.AluOpType

NCORES = 8
D = 1024
KC = 8
DFF = 2816
FC = 22
EPS = 1e-6
NB = 32
TS = 8
T_S = NB * TS
ROPE_BASE = 10000.0

V_BADA = 0
V_NMIX = 48
V_NFFN = 56
V_NFIN = 64
V_CW = 72
V_CB = 138
V_SBGN = 160
V_SBB = 161
V_GC = 162
V_G8 = 163
V_HALO = 164
NV = 165

C_ID = 0
C_TRI = 128
C_OMT = 256
C_ONE = 384
C_MRET = 512
C_MRS = 640
C_DM = 768
C_SN = C_DM + 2048
C_BM = C_SN + 512
C_RM = C_BM + 2048
NCST = C_RM + 16


def bc(ap, axis, n):
    l = [list(x) for x in ap.ap]
    l.insert(axis, [0, n])
    return bass.AP(ap.tensor, ap.offset, l)


def bcl(ap, n):
    l = [list(x) for x in ap.ap]
    assert l[-1][1] == 1
    l[-1] = [0, n]
    return bass.AP(ap.tensor, ap.offset, l)


class Builder:
    def __init__(self, nc):
        self.nc = nc
        self.E = {"pe": nc.tensor, "act": nc.scalar, "dve": nc.vector, "pool": nc.gpsimd, "sp": nc.sync}
        self.esem = {e: nc.alloc_semaphore("es_" + e) for e in self.E}
        self.ecnt = {e: 0 for e in self.E}
        self.dsem = {}
        self.dcnt = {}
        self.seen = {e: {} for e in self.E}
        self.W = {}
        self.R = {}

    def _need(self, eng, reads, writes):
        need = {}

        def add(evs, same_ok):
            for (kind, name), val in evs.items():
                if kind == "e" and name == eng and eng == "pe":
                    continue
                if kind == "d":
                    val = self.dcnt[name]
                k = (kind, name)
                if need.get(k, 0) < val:
                    need[k] = val

        for k in reads:
            add(self.W.get(k, {}), True)
            if k.startswith("p_"):
                add({kk: vv for kk, vv in self.R.get(k, {}).items() if kk != ("e", eng)}, True)
        for k in writes:
            add(self.W.get(k, {}), False)
            add(self.R.get(k, {}), False)
        return need

    def _waits(self, eng, need):
        for k, val in need.items():
            if self.seen[eng].get(k, 0) >= val:
                continue
            sem = self.esem[k[1]] if k[0] == "e" else self.dsem[k[1]]
            self.E[eng].wait_ge(sem, val)
            self.seen[eng][k] = val

    def _post(self, ev, val, reads, writes):
        for k in reads:
            d = self.R.setdefault(k, {})
            if d.get(ev, 0) < val:
                d[ev] = val
        for k in writes:
            self.W[k] = {ev: val}
            self.R[k] = {}

    def sync(self, eng, reads=(), writes=()):
        self._waits(eng, self._need(eng, reads, writes))

    def op(self, eng, fn, reads=(), writes=()):
        self._waits(eng, self._need(eng, reads, writes))
        ins = fn(self.E[eng])
        self.ecnt[eng] += 1
        ins.then_inc(self.esem[eng], 1)
        self._post(("e", eng), self.ecnt[eng], reads, writes)

    def _dsem(self, sem):
        if sem not in self.dsem:
            self.dsem[sem] = self.nc.alloc_semaphore("ds_" + sem)
            self.dcnt[sem] = 0

    def dma(self, q, out, in_, sem, reads=(), writes=()):
        self._waits(q, self._need(q, reads, writes))
        self._dsem(sem)
        self.E[q].dma_start(out=out, in_=in_).then_inc(self.dsem[sem], 16)
        self.dcnt[sem] += 16
        self._post(("d", sem), self.dcnt[sem], reads, writes)

    def cc(self, fn, sem, reads=(), writes=()):
        self._waits("pool", self._need("pool", reads, writes))
        self._dsem(sem)
        fn(self.E["pool"]).then_inc(self.dsem[sem], 1)
        self.dcnt[sem] += 1
        self._post(("d", sem), self.dcnt[sem], reads, writes)

    def barrier(self):
        for eng in self.E:
            need = {}
            for e2 in self.E:
                if e2 != eng and self.ecnt[e2] > 0:
                    need[("e", e2)] = self.ecnt[e2]
            for s, v in self.dcnt.items():
                need[("d", s)] = v
            self._waits(eng, need)
        self.W = {}
        self.R = {}


def build(T_P, NPG, NPHYS, phases="0ABCDE"):
    TPC = T_P // NCORES
    T_ALL = T_P + T_S
    NOWN = 2 + TPC + 4 * TS
    CW = 1024
    NCH = (T_ALL + CW - 1) // CW
    nc = bass.Bass("TRN2", target_bir_lowering=False)
    K = Builder(nc)

    def din(name, shape, dt=F32):
        return nc.dram_tensor(name, list(shape), dt, kind="ExternalInput").ap()

    def dout(name, shape, dt=F32):
        return nc.dram_tensor(name, list(shape), dt, kind="ExternalOutput").ap()

    xT = din("xT", [D, T_ALL])
    xTo = din("xTo", [D, NOWN])
    cT = din("cT", [D, 37])
    w_ada = din("w_ada", [D, 6 * D])
    vecsT = din("vecsT", [128, NV])
    rgn = din("rgn", [128, 128])
    w_c = din("w_c", [D, 576])
    w_rg = din("w_rg", [D, 512])
    w_out = din("w_out", [D, D])
    w_ug = din("w_ug", [D, DFF])
    w_uv = din("w_uv", [D, DFF])
    w_dn = din("w_dn", [DFF, D])
    rot = din("rot", [T_ALL, 8, 64])
    cst = din("cst", [128, NCST])
    poolKT = din("poolKT", [NPHYS * PGB], U8)
    poolV = din("poolV", [NPHYS * PGB], U8)
    ptab = din("ptab", [1, NB * NPG], I32)
    sret = din("sret", [NB, 128, 128])
    sconvT = din("sconvT", [DFF, 4, 2])
    NT = 256
    ptiles = [(0, 2, 0)] + [(2 + j0, min(NT, TPC - j0), 0) for j0 in range(0, TPC, NT)] + [(2 + TPC, 4 * TS, 33)]
    NI = len(ptiles)
    info = din("info", [1, NI * 9], I32)

    yT = dout("yT", [D, TPC + 4 * TS])
    kT_o = dout("kT_o", [64, T_ALL])
    v_o = dout("v_o", [T_ALL, 64])
    rsp = dout("rsp", [128, 128])
    rss = dout("rss", [NB, 128, 128])
    cvp = dout("cvp", [DFF, 2])
    cvs = dout("cvs", [DFF, 4, 2])

    bounce = nc.dram_tensor("bounce", [NCH, 192, CW], BF16).ap()
    g4 = nc.dram_tensor("g4", [NCH, 4 * 192, CW], BF16).ap()
    g8 = nc.dram_tensor("g8", [NCH, 8 * 192, CW], BF16).ap()
    x1s = nc.dram_tensor("x1s", [D, NOWN], F32).ap()

    outer = ExitStack()

    _cnt = [0]

    def mk(st):
        _cnt[0] += 1
        pre = f"s{_cnt[0]}_"

        def sb(name, shape, dt):
            return st.enter_context(nc.sbuf_tensor(pre + name, list(shape), dt))

        def ps(name, shape, dt):
            return st.enter_context(nc.psum_tensor(pre + name, list(shape), dt))
        return sb, ps

    sbP, _ = mk(outer)

    vec = sbP("vec", [128, NV], F32)
    cb = sbP("cstb", [128, NCST], BF16)
    ada = sbP("ada", [128, 48, 37], F32)
    gm1 = sbP("gm1", [128, KC, 37], F32)
    gf1 = sbP("gf1", [128, KC, 37], F32)
    nm32 = sbP("nm32", [128, 24], F32)
    kcon = sbP("kcon", [128, 4], F32)
    rgn_sb = sbP("rgn_sb", [128, 128], F32)

    K.dma("sp", vec[:], vecsT, "ld0", writes=["vec"])
    K.dma("sp", rgn_sb[:], rgn, "ld0", writes=["rgn"])
    for c0 in range(0, NCST, 1024):
        c1 = min(NCST, c0 + 1024)
        K.dma("pool", cb[:, c0:c1], cst[:, c0:c1], "ld1", writes=[f"cst{c0}"])
    K.sync("pool", reads=[f"cst{c0}" for c0 in range(0, NCST, 1024)], writes=["cst"])
    K.op("pool", lambda e: e.memset(kcon[:, 3:4], 0.0), reads=[f"cst{c0}" for c0 in range(0, NCST, 1024)],
         writes=["cst", "kc3"])
    K.op("dve", lambda e: e.memset(kcon[:, 0:1], 1.0), writes=["kc0"])
    K.op("dve", lambda e: e.memset(kcon[:, 1:2], 1024.0 * EPS), writes=["kc1"])
    K.op("dve", lambda e: e.memset(kcon[:, 2:3], EPS), writes=["kc2"])
    KCON = ["kc0", "kc1", "kc2", "kc3"]
    one_c, eps1k_c, eps_c = kcon[:, 0:1], kcon[:, 1:2], kcon[:, 2:3]

    ident = cb[:, C_ID:C_ID + 128]
    tri = cb[:, C_TRI:C_TRI + 128]
    omt = cb[:, C_OMT:C_OMT + 128]
    ones = cb[:, C_ONE:C_ONE + 128]
    ADA = [f"ada{i}" for i in range(48)]

    def rsqrt_ps(out, pin, scale, bias_ap, key_in, key_out, tmp, key_tmp):
        K.op("act", lambda e: e.activation(out=tmp, in_=pin, func=AF.Ln, bias=bias_ap, scale=scale),
             reads=[key_in] + KCON, writes=[key_tmp])
        K.op("act", lambda e: e.activation(out=out, in_=tmp, func=AF.Exp, scale=-0.5),
             reads=[key_tmp], writes=[key_out])

    with ExitStack() as st:
        sb, ps = mk(st)
        c_sb = sb("c_sb", [128, KC, 37], F32)
        s_bf = sb("s_bf", [128, KC, 37], BF16)
        wA = [sb(f"wA{i}", [128, KC, D], BF16) for i in range(2)]
        pA = [ps(f"pA{i}", [128, 512], F32) for i in range(2)]
        K.dma("sp", c_sb[:], cT.rearrange("(kc p) n -> p kc n", p=128), "ld0", writes=["c_sb"])
        K.op("act", lambda e: e.activation(out=s_bf[:], in_=c_sb[:], func=AF.Silu), reads=["c_sb"],
             writes=["s_bf"])
        for j in range(6):
            w = wA[j % 2]
            K.dma("pool", w[:], w_ada[:, j * D:(j + 1) * D].rearrange("(kc p) n -> p kc n", p=128), f"wA{j % 2}",
                  writes=[f"wA{j % 2}"])
            for fo in range(8):
                pp = pA[fo % 2]
                for kc in range(KC):
                    K.op("pe", lambda e, kc=kc, fo=fo, w=w, pp=pp: e.matmul(
                        pp[:, 0:37], lhsT=w[:, kc, fo * 128:(fo + 1) * 128], rhs=s_bf[:, kc, :],
                        start=(kc == 0), stop=(kc == KC - 1)),
                        reads=[f"wA{j % 2}", "s_bf"], writes=[f"pA{fo % 2}"])
                col = j * 8 + fo
                K.op("act", lambda e, col=col, pp=pp: e.activation(
                    out=ada[:, col, :], in_=pp[:, 0:37], func=AF.Identity,
                    bias=vec[:, V_BADA + col:V_BADA + col + 1], scale=1.0),
                    reads=[f"pA{fo % 2}", "vec"], writes=[f"ada{col}"])
        K.op("dve", lambda e: e.tensor_scalar(out=nm32[:], in0=vec[:, V_NMIX:V_NMIX + 24], scalar1=32.0,
                                              scalar2=None, op0=ALU.mult), reads=["vec"], writes=["nm32"])
        for kc in range(KC):
            K.op("dve", lambda e, kc=kc: e.scalar_tensor_tensor(
                out=gm1[:, kc, :], in0=ada[:, 8 + kc, :], scalar=1.0, in1=bcl(nm32[:, kc:kc + 1], 37),
                op0=ALU.add, op1=ALU.mult), reads=ADA + ["nm32"], writes=["gm1"])
            K.op("dve", lambda e, kc=kc: e.scalar_tensor_tensor(
                out=gf1[:, kc, :], in0=ada[:, 32 + kc, :], scalar=1.0, in1=bcl(nm32[:, 8 + kc:9 + kc], 37),
                op0=ALU.add, op1=ALU.mult), reads=ADA + ["nm32"], writes=["gf1"])
        K.barrier()

    def norm_tile(T, xtile, n, xkey, gtab, shrow, scol, hout, hkey):
        sq, p_ssq, rstd, tln, tmpf = T["sq"], T["p_ssq"], T["rstd"], T["tln"], T["tmpf"]
        K.op("act", lambda e: e.activation(out=sq[:, :, :n], in_=xtile[:, :, :n], func=AF.Square),
             reads=[xkey], writes=["sq"])
        for kc in range(KC):
            K.op("pe", lambda e, kc=kc: e.matmul(p_ssq[:, :n], lhsT=ones, rhs=sq[:, kc, :n], start=(kc == 0),
                                                 stop=(kc == KC - 1)), reads=["sq", "cst"], writes=["p_ssq"])
        rsqrt_ps(rstd[:, :n], p_ssq[:, :n], 1.0, eps1k_c, "p_ssq", "rstd", tln[:, :n], "tln")
        if hout is None:
            return
        for kc in range(KC):
            if scol == 0:
                K.op("dve", lambda e, kc=kc: e.scalar_tensor_tensor(
                    out=tmpf[:, :n], in0=xtile[:, kc, :n], scalar=gtab[:, kc, 0:1], in1=rstd[:, :n],
                    op0=ALU.mult, op1=ALU.mult), reads=[xkey, "gm1", "gf1", "rstd"], writes=["tmpf"])
                K.op("act", lambda e, kc=kc: e.activation(
                    out=hout[:, kc, :n], in_=tmpf[:, :n], func=AF.Identity, bias=ada[:, shrow + kc, 0:1],
                    scale=1.0), reads=["tmpf"] + ADA, writes=[hkey])
            else:
                nb = n // TS
                t3 = tmpf[:, :n].rearrange("p (b i) -> p b i", i=TS)
                K.op("dve", lambda e, kc=kc: e.tensor_tensor(out=tmpf[:, :n], in0=xtile[:, kc, :n],
                                                             in1=rstd[:, :n], op=ALU.mult),
                     reads=[xkey, "rstd"], writes=["tmpf"])
                K.op("dve", lambda e, kc=kc, t3=t3: e.tensor_tensor(
                    out=t3, in0=t3, in1=bc(gtab[:, kc, scol:scol + nb], 2, TS), op=ALU.mult),
                    reads=["tmpf", "gm1", "gf1"], writes=["tmpf"])
                K.op("dve", lambda e, kc=kc, t3=t3: e.tensor_tensor(
                    out=hout[:, kc, :n].rearrange("p (b i) -> p b i", i=TS), in0=t3,
                    in1=bc(ada[:, shrow + kc, scol:scol + nb], 2, TS), op=ALU.add),
                    reads=["tmpf"] + ADA, writes=[hkey])

    BNC = []

    main = ExitStack()
    sbM, _ = mk(main)
    QT = sbM("QT", [64, T_ALL], BF16)
    KT = sbM("KT", [64, T_ALL], BF16)
    NBLK = T_ALL // 128
    VA = sbM("VA", [128, NBLK, 64], BF16)
    TT = 256
    tiles = [(t0, min(TT, T_P - t0), False) for t0 in range(0, T_P, TT)] + [(T_P, T_S, True)]
    QTK = [f"QT{i}" for i in range(len(tiles))]
    KTK = [f"KT{i}" for i in range(len(tiles))]
    VAK = [f"VA{i}" for i in range(len(tiles))]

    with ExitStack() as st:
        sb, ps = mk(st)
        Wc = sb("Wc", [128, KC, 576], BF16)
        K.dma("pool", Wc[:], w_c.rearrange("(kc p) n -> p kc n", p=128), "ld1", writes=["Wc"])
        xt = [sb(f"xt{i}", [128, KC, TT], F32) for i in range(2)]
        T = dict(sq=sb("sq", [128, KC, TT], BF16), tmpf=sb("tmpf", [128, TT], F32),
                 rstd=sb("rstd", [128, TT], F32), tln=sb("tln", [128, TT], F32),
                 p_ssq=ps("p_ssq", [128, 512], F32))
        p_ssq = T["p_ssq"]
        hb = sb("hb", [128, KC, TT], BF16)
        kst = [sb(f"kst{i}", [64, TT], F32) for i in range(2)]
        vst = [sb(f"vst{i}", [128, TT // 128, 64], F32) for i in range(2)]
        rt = [sb(f"rt{i}", [128, TT // 128, 8, 64], F32) for i in range(2)]
        ta = sb("ta", [128, 2, 64], F32)
        tb = sb("tb", [128, 2, 64], F32)
        qr = sb("qr", [128, 128], BF16)
        kr = sb("kr", [128, 128], BF16)
        vr = sb("vr", [128, 128], BF16)
        qT = sb("qTr", [128, 128], BF16)
        kTt = sb("kTr", [128, 128], BF16)
        scm = sb("scm", [128, 128], BF16)
        S32 = sb("S32", [128, 128], F32)
        Sbf = sb("Sbf", [128, 128], BF16)
        stmp = sb("stmp", [128, 128], F32)
        SD = nc.vector.BN_STATS_DIM
        AD = nc.vector.BN_AGGR_DIM
        bst = sb("bst", [128, SD], F32)
        mv = sb("mv", [128, AD], F32)
        rs2 = sb("rs2", [128, 2], F32)
        onr = sb("onr", [128, 128], F32)
        onb = sb("onb", [128, 128], BF16)
        rost = [sb(f"rost{i}", [128, TT], BF16) for i in range(2)]
        s0f = sb("s0f", [128, 16, 128], F32)
        s0b = sb("s0b", [128, 16, 128], BF16)
        qm = sb("qm", [128, 16, 128], BF16)
        km = sb("km", [128, 16, 128], BF16)
        snew = sb("snew", [128, 4, 128], F32)
        p_q = ps("p_q", [128, 512], F32)
        p_k = ps("p_k", [128, 512], F32)
        p_v = ps("p_v", [128, 8, 64], F32)
        p_r = ps("p_r", [128, 4, 128], F32)
        p_t = ps("p_t", [128, 8, 128], BF16)
        p_s = ps("p_s", [128, 4, 128], F32)
        p_o = ps("p_o", [128, 512], F32)

        used = T_ALL - (NCH - 1) * CW
        if used < CW:
            zt = sb("zt", [128, CW - used], BF16)
            K.op("dve", lambda e: e.memset(zt[:], 0.0), writes=["zt"])
            K.dma("pool", bounce[NCH - 1, 0:128, used:CW], zt[:], "st_b", reads=["zt"], writes=["bnc_pad0"])
            K.dma("pool", bounce[NCH - 1, 128:192, used:CW], zt[0:64, :], "st_b", reads=["zt"],
                  writes=["bnc_pad1"])
            BNC += ["bnc_pad0", "bnc_pad1"]
        K.op("dve", lambda e: e.memset(S32[:], 0.0), writes=["S32"])
        K.op("dve", lambda e: e.memset(Sbf[:], 0.0), writes=["Sbf"])
        gC = vec[:, V_GC:V_GC + 1]
        g8c = vec[:, V_G8:V_G8 + 1]
        mret = cb[:, C_MRET:C_MRET + 128]
        mrs = cb[:, C_MRS:C_MRS + 128]

        for ti, (t0, n, is_s) in enumerate(tiles if "A" in phases else []):
            sl = ti % 2
            xk = f"xt{sl}"
            nch = n // 128
            K.dma("sp", xt[sl][:, :, :n], xT[:, t0:t0 + n].rearrange("(kc p) n -> p kc n", p=128), xk,
                  writes=[xk])
            K.dma("sp", rt[sl][:, :nch], rot[t0:t0 + n].rearrange("(c p) a k -> p c a k", p=128), f"rt{sl}",
                  writes=[f"rt{sl}"])
            if LV < 2:
                continue
            norm_tile(T, xt[sl], n, xk, gm1, 0, (1 if is_s else 0), hb, "hb")
            if LV < 3:
                continue
            for kc in range(KC):
                K.op("pe", lambda e, kc=kc: e.matmul(p_q[0:64, :n], lhsT=Wc[:, kc, 0:64], rhs=hb[:, kc, :n],
                                                     start=(kc == 0), stop=(kc == KC - 1)),
                     reads=["Wc", "hb"], writes=["p_q"])
            K.op("act", lambda e: e.activation(out=QT[:, t0:t0 + n], in_=p_q[0:64, :n], func=AF.Identity,
                                               scale=0.125), reads=["p_q"], writes=[f"QT{ti}"])
            if LV < 4:
                continue
            for kc in range(KC):
                K.op("pe", lambda e, kc=kc: e.matmul(p_k[0:64, :n], lhsT=Wc[:, kc, 64:128], rhs=hb[:, kc, :n],
                                                     start=(kc == 0), stop=(kc == KC - 1)),
                     reads=["Wc", "hb"], writes=["p_k"])
            if "k4" in DBG:
                K.op("act", lambda e: e.activation(out=KT[:, t0:t0 + n], in_=p_k[0:64, :n], func=AF.Identity),
                     reads=["p_k"], writes=[f"KT{ti}"])
            else:
                K.op("dve", lambda e: e.tensor_copy(out=KT[:, t0:t0 + n], in_=p_k[0:64, :n]), reads=["p_k"],
                     writes=[f"KT{ti}"])
            if "k5" not in DBG:
                K.op("act", lambda e: e.activation(out=kst[sl][:, :n], in_=p_k[0:64, :n], func=AF.Identity),
                     reads=["p_k"], writes=[f"kst{sl}"])
            if "k1" not in DBG:
                K.dma("sp" if "k3" in DBG else "pool", kT_o[:, t0:t0 + n], kst[sl][:, :n], f"st_k{sl}",
                      reads=[f"kst{sl}"], writes=[f"kTo{ti}"])
            if LV < 5:
                continue
            for c in range(nch):
                for kc in range(KC):
                    K.op("pe", lambda e, kc=kc, c=c: e.matmul(p_v[:, c, :], lhsT=hb[:, kc, c * 128:(c + 1) * 128],
                                                              rhs=Wc[:, kc, 128:192], start=(kc == 0),
                                                              stop=(kc == KC - 1)),
                         reads=["Wc", "hb"], writes=["p_v"])
            b0 = t0 // 128
            K.op("dve", lambda e: e.tensor_copy(out=VA[:, b0:b0 + nch, :], in_=p_v[:, :nch, :]), reads=["p_v"],
                 writes=[f"VA{ti}"])
            K.op("act", lambda e: e.activation(out=vst[sl][:, :nch, :], in_=p_v[:, :nch, :], func=AF.Identity),
                 reads=["p_v"], writes=[f"vst{sl}"])
            K.dma("pool", v_o[t0:t0 + n, :].rearrange("(c p) d -> p c d", p=128), vst[sl][:, :nch, :],
                  f"st_v{sl}", reads=[f"vst{sl}"], writes=[f"vo{ti}"])
            for c in range(nch if "R" not in DBG else 0):
                for j in range(3):
                    for kc in range(KC):
                        K.op("pe", lambda e, kc=kc, c=c, j=j: e.matmul(
                            p_r[:, j, :], lhsT=hb[:, kc, c * 128:(c + 1) * 128],
                            rhs=Wc[:, kc, 192 + 128 * j:320 + 128 * j], start=(kc == 0), stop=(kc == KC - 1)),
                            reads=["Wc", "hb"], writes=["p_r"])
                for j, dst, dk_ in ((0, qr, "qr"), (1, kr, "kr")):
                    src = p_r[:, j, :].rearrange("p (a k) -> p a k", a=2)
                    K.op("dve", lambda e, j=j, src=src, c=c: e.tensor_tensor(
                        out=ta[:], in0=src, in1=rt[sl][:, c, 4 * j:4 * j + 2, :], op=ALU.mult),
                        reads=["p_r", f"rt{sl}"], writes=["ta"])
                    K.op("dve", lambda e, j=j, src=src, c=c: e.tensor_tensor(
                        out=tb[:], in0=src, in1=rt[sl][:, c, 4 * j + 2:4 * j + 4, :], op=ALU.mult),
                        reads=["p_r", f"rt{sl}"], writes=["tb"])
                    K.op("pool", lambda e, dst=dst: e.tensor_tensor(out=dst[:, 0:64], in0=ta[:, 0, :],
                                                                    in1=ta[:, 1, :], op=ALU.subtract),
                         reads=["ta"], writes=[dk_ + "a"])
                    K.op("pool", lambda e, dst=dst: e.tensor_tensor(out=dst[:, 64:128], in0=tb[:, 0, :],
                                                                    in1=tb[:, 1, :], op=ALU.add),
                         reads=["tb"], writes=[dk_ + "b"])
                K.op("act", lambda e: e.activation(out=vr[:], in_=p_r[:, 2, :], func=AF.Identity), reads=["p_r"],
                     writes=["vr"])
                K.op("pe", lambda e: e.transpose(out=p_t[:, 0, :], in_=qr[:], identity=ident),
                     reads=["qra", "qrb", "cst"], writes=["p_t"])
                K.op("pe", lambda e: e.transpose(out=p_t[:, 1, :], in_=kr[:], identity=ident),
                     reads=["kra", "krb", "cst"], writes=["p_t"])
                K.op("act", lambda e: e.activation(out=qT[:], in_=p_t[:, 0, :], func=AF.Identity), reads=["p_t"],
                     writes=["qT"])
                K.op("dve", lambda e: e.tensor_copy(out=kTt[:], in_=p_t[:, 1, :]), reads=["p_t"],
                     writes=["kTt"])
                K.op("pe", lambda e: e.matmul(p_s[:, 0, :], lhsT=kTt[:], rhs=qT[:], start=True, stop=True),
                     reads=["kTt", "qT"], writes=["p_s"])
                msk = mrs if is_s else mret
                K.op("dve", lambda e, msk=msk: e.tensor_tensor(out=scm[:], in0=p_s[:, 0, :], in1=msk,
                                                               op=ALU.mult),
                     reads=["p_s", "cst"], writes=["scm"])
                if not is_s:
                    K.op("pe", lambda e: e.matmul(p_o[:, 0:128], lhsT=scm[:], rhs=vr[:], start=True, stop=False),
                         reads=["scm", "vr"], writes=["p_o"])
                    K.op("pe", lambda e: e.matmul(p_o[:, 0:128], lhsT=qT[:], rhs=Sbf[:], start=False, stop=True),
                         reads=["qT", "Sbf"], writes=["p_o"])
                    K.op("pe", lambda e: e.matmul(p_s[:, 1, :], lhsT=kr[:], rhs=vr[:], start=True, stop=True),
                         reads=["kra", "krb", "vr"], writes=["p_s"])
                    K.op("dve", lambda e: e.tensor_tensor(out=stmp[:], in0=S32[:], in1=p_s[:, 1, :], op=ALU.add),
                         reads=["S32", "p_s"], writes=["stmp"])
                    K.op("act", lambda e: e.activation(out=S32[:], in_=stmp[:], func=AF.Identity, scale=gC),
                         reads=["stmp", "vec"], writes=["S32"])
                    K.op("act", lambda e: e.activation(out=Sbf[:], in_=stmp[:], func=AF.Identity, scale=gC),
                         reads=["stmp", "vec"], writes=["Sbf"])
                else:
                    blk = c
                    K.dma("sp", s0f[:], sret[blk * 16:(blk + 1) * 16].rearrange("b k v -> k b v"), "s0f",
                          writes=["s0f"])
                    K.op("act", lambda e: e.activation(out=s0b[:], in_=s0f[:], func=AF.Identity), reads=["s0f"],
                         writes=["s0b"])
                    K.op("dve", lambda e: e.tensor_tensor(
                        out=qm[:], in0=bc(qT[:], 1, 16),
                        in1=cb[:, C_BM:C_BM + 2048].rearrange("p (b n) -> p b n", b=16), op=ALU.mult),
                        reads=["qT", "cst"], writes=["qm"])
                    K.op("dve", lambda e: e.tensor_tensor(out=km[:], in0=bc(kr[:], 1, 16),
                                                          in1=bc(cb[:, C_RM:C_RM + 16], 2, 128), op=ALU.mult),
                         reads=["kra", "krb", "cst"], writes=["km"])
                    K.op("pe", lambda e: e.matmul(p_o[:, 0:128], lhsT=scm[:], rhs=vr[:], start=True, stop=False),
                         reads=["scm", "vr"], writes=["p_o"])
                    for b in range(16):
                        K.op("pe", lambda e, b=b: e.matmul(p_o[:, 0:128], lhsT=qm[:, b, :], rhs=s0b[:, b, :],
                                                           start=False, stop=(b == 15)),
                             reads=["qm", "s0b"], writes=["p_o"])
                    for g in range(4):
                        for bb in range(4):
                            b = g * 4 + bb
                            K.op("pe", lambda e, b=b, bb=bb: e.matmul(p_ssq[:, bb * 128:(bb + 1) * 128],
                                                                      lhsT=km[:, b, :], rhs=vr[:], start=True,
                                                                      stop=True),
                                 reads=["km", "vr"], writes=["p_ssq"])
                        K.op("dve", lambda e, g=g: e.tensor_tensor(
                            out=snew[:], in0=s0f[:, g * 4:(g + 1) * 4, :],
                            in1=p_ssq[:].rearrange("p (a d) -> p a d", a=4), op=ALU.add),
                            reads=["s0f", "p_ssq"], writes=["snew"])
                        K.op("act", lambda e: e.activation(out=snew[:], in_=snew[:], func=AF.Identity, scale=g8c),
                             reads=["snew", "vec"], writes=["snew"])
                        K.dma("pool", rss[blk * 16 + g * 4:blk * 16 + g * 4 + 4].rearrange("b k v -> k b v"),
                              snew[:], "st_rss", reads=["snew"], writes=[f"rss{blk}_{g}"])
                K.op("dve", lambda e: e.bn_stats(out=bst[:], in_=p_o[:, 0:128]), reads=["p_o"], writes=["bst"])
                K.op("dve", lambda e: e.bn_aggr(out=mv[:], in_=bst[:]), reads=["bst"], writes=["mv"])
                rsqrt_ps(rs2[:, 0:1], mv[:, 1:2], 1.0, eps_c, "mv", "rs2", rs2[:, 1:2], "rs2t")
                K.op("dve", lambda e: e.tensor_scalar(out=onr[:], in0=p_o[:, 0:128], scalar1=mv[:, 0:1],
                                                      scalar2=rs2[:, 0:1], op0=ALU.subtract, op1=ALU.mult),
                     reads=["p_o", "mv", "rs2"], writes=["onr"])
                K.op("pool", lambda e: e.tensor_tensor(out=onb[:], in0=onr[:], in1=rgn_sb[:], op=ALU.mult),
                     reads=["onr", "rgn"], writes=["onb"])
                K.op("pe", lambda e: e.transpose(out=p_t[:, 2, :], in_=onb[:], identity=ident),
                     reads=["onb", "cst"], writes=["p_t"])
                K.op("act", lambda e, c=c: e.activation(out=rost[sl][:, c * 128:(c + 1) * 128], in_=p_t[:, 2, :],
                                                        func=AF.Identity), reads=["p_t"], writes=[f"rost{sl}"])
            if "R" in DBG:
                continue
            K.dma("pool", bounce[t0 // CW, 64:192, t0 % CW:t0 % CW + n], rost[sl][:, :n], f"st_ro{sl}",
                  reads=[f"rost{sl}"], writes=[f"bnc_ro{ti}"])
            BNC.append(f"bnc_ro{ti}")
        if "A" in phases:
            K.dma("pool", rsp, S32[:], "st_rsp", reads=["S32"], writes=["rsp"])
        K.barrier()

    with ExitStack() as st:
        sb, ps = mk(st)
        NE = 4
        e_t = [sb(f"e_t{i}", [128, 512], F32) for i in range(NE)]
        L_t = [sb(f"L_t{i}", [128, 512], BF16) for i in range(NE)]
        X_t = [sb(f"X_t{i}", [128, 512], F32) for i in range(2)]
        A_t = [sb(f"A_t{i}", [128, 512], BF16) for i in range(2)]
        osq = sb("osq", [64, 512], BF16)
        orr = sb("orr", [64, 512], F32)
        otl = sb("otl", [64, 512], F32)
        sost = [sb(f"sost{i}", [64, 512], BF16) for i in range(2)]
        p_z = [ps(f"p_z{i}", [128, 512], F32) for i in range(3)]
        pCs = {"B": ps("p_CB", [128, 512], F32), "C": ps("p_CC", [128, 512], F32)}
        pOs = {"B": ps("p_OB", [128, 512], F32)[0:64, :], "C": ps("p_OC", [128, 512], F32)[0:64, :]}
        p_nf = ps("p_n", [128, 512], F32)
        p_n = p_nf[0:64, :]
        bias_c = vec[:, V_SBB:V_SBB + 1]
        gn_c = vec[0:64, V_SBGN:V_SBGN + 1]
        doB, doC = "B" in phases, "C" in phases
        ND = int(_os.environ.get("KND", "0"))
        if doC:
            kstg = [sb(f"kstg{i}", [64, NB, 128], F32) for i in range(2)]
            vstg = [sb(f"vstg{i}", [128, NB, 64], F32) for i in range(2)]
            kbf = sb("kbf", [64, NB, 128], BF16)
            vbf = sb("vbf", [128, NB, 64], BF16)
            pt2 = sb("pt_sb", [NB, NPG], I32)
            K.dma("sp", pt2[:], ptab.rearrange("o (b g) -> (o b) g", g=NPG), "ld_pt", writes=["pt_sb"])
            K.op("dve", lambda e: e.tensor_single_scalar(out=pt2[:], in_=pt2[:], scalar=15,
                                                         op=ALU.logical_shift_left), reads=["pt_sb"],
                 writes=["pt2", "pt_sb"])

        def so_epilogue(kind, nq, col0, si):
            sl = si % 2
            pO, ok = pOs[kind], "p_O" + kind
            K.op("act", lambda e: e.activation(out=osq[:, :nq], in_=pO[:, :nq], func=AF.Square), reads=[ok],
                 writes=["osq"])
            K.op("pe", lambda e: e.matmul(p_n[:, :nq], lhsT=ones[0:64, 0:64], rhs=osq[:, :nq], start=True,
                                          stop=True), reads=["osq", "cst"], writes=["p_n"])
            rsqrt_ps(orr[:, :nq], p_n[:, :nq], 1.0 / 64.0, eps_c[0:64], "p_n", "orr", otl[:, :nq], "otl")
            K.op("dve", lambda e: e.scalar_tensor_tensor(out=sost[sl][:, :nq], in0=pO[:, :nq], scalar=gn_c,
                                                         in1=orr[:, :nq], op0=ALU.mult, op1=ALU.mult),
                 reads=[ok, "orr", "vec"], writes=[f"sost{sl}"])
            K.dma("pool", bounce[col0 // CW, 0:64, col0 % CW:col0 % CW + nq], sost[sl][:, :nq], f"st_so{sl}",
                  reads=[f"sost{sl}"], writes=[f"bnc_so{col0}"])
            BNC.append(f"bnc_so{col0}")

        jobsB = []
        if doB:
            for sbi in range(T_P // 512):
                nkb = 4 * sbi + 4
                for idx, kb in enumerate(reversed(range(nkb))):
                    mk_ = None
                    if kb >= 4 * sbi:
                        j = kb - 4 * sbi
                        mk_ = cb[:, C_DM + 512 * j:C_DM + 512 * (j + 1)]
                    jobsB.append(dict(kind="B", sbi=sbi, kb=kb, first=idx == 0, last=kb == 0, mask=mk_, nq=512))
        jobsC = []
        if doC:
            steps = [("new", 0), ("new", 1)] + [("pg", g) for g in reversed(range(NPG))]
            for si, (kd, g) in enumerate(steps):
                mk_ = cb[:, C_SN + 256 * g:C_SN + 256 * (g + 1)] if kd == "new" else None
                jobsC.append(dict(kind="C", sub=kd, g=g, si=si, first=si == 0, last=si == len(steps) - 1,
                                  mask=mk_, nq=T_S))
        jobs = []
        if jobsB and jobsC:
            r = max(1, len(jobsB) // len(jobsC))
            ci = 0
            for bi, jb in enumerate(jobsB):
                jobs.append(jb)
                if (bi + 1) % r == 0 and ci < len(jobsC):
                    jobs.append(jobsC[ci])
                    ci += 1
            jobs += jobsC[ci:]
        else:
            jobs = jobsB + jobsC
        N = len(jobs)

        def c_load(si):
            if si >= len(jobsC) or jobsC[si]["sub"] != "pg":
                return
            g = jobsC[si]["g"]
            sg = si % 2
            kkeys = [f"kstg{sg}_{b}" for b in range(NB)]
            vkeys = [f"vstg{sg}_{b}" for b in range(NB)]
            K.sync("sp", reads=["pt2"], writes=kkeys + vkeys)
            for b in range(NB):
                pg = nc.sync.value_load(pt2[b:b + 1, g:g + 1])
                K.dma("sp", kstg[sg][:, b, :].bitcast(U8),
                      poolKT[bass.ds(pg, PGB)].rearrange("(d k) -> d k", k=512), f"kstg{sg}", writes=[kkeys[b]])
                K.dma("sp", vstg[sg][:, b, :].bitcast(U8),
                      poolV[bass.ds(pg, PGB)].rearrange("(k d) -> k d", d=256), f"vstg{sg}", writes=[vkeys[b]])
                nc.sync.free_register(nc.sync.to_reg(pg))

        def j_z(i):
            jb = jobs[i]
            zs = i % 3
            zk = f"p_z{zs}"
            nq = jb["nq"]
            if jb["kind"] == "B":
                q0, kb = jb["sbi"] * 512, jb["kb"]
                K.op("pe", lambda e: e.matmul(p_z[zs][:, :], lhsT=KT[:, kb * 128:(kb + 1) * 128],
                                              rhs=QT[:, q0:q0 + 512], start=True, stop=True),
                     reads=KTK + QTK, writes=[zk])
            elif jb["sub"] == "new":
                blk = jb["g"]
                K.op("pe", lambda e: e.matmul(p_z[zs][:, 0:T_S], lhsT=KT[:, T_P + 128 * blk:T_P + 128 * (blk + 1)],
                                              rhs=QT[:, T_P:T_ALL], start=True, stop=True),
                     reads=KTK + QTK, writes=[zk])
            else:
                sg = jb["si"] % 2
                kkeys = [f"kstg{sg}_{b}" for b in range(NB)]
                K.op("dve", lambda e: e.tensor_copy(out=kbf[:], in_=kstg[sg][:]), reads=kkeys, writes=["kbf"])
                for b in range(NB):
                    K.op("pe", lambda e, b=b: e.matmul(
                        p_z[zs][:, b * TS:(b + 1) * TS], lhsT=kbf[:, b, :],
                        rhs=QT[:, T_P + b * TS:T_P + (b + 1) * TS], start=True, stop=True),
                        reads=["kbf"] + QTK, writes=[zk])

        def j_s1(i):
            jb = jobs[i]
            zs = i % 3
            zk = f"p_z{zs}"
            nq = jb["nq"]
            s4 = i % NE
            zp = p_z[zs][:, :nq]
            K.op("act", lambda e: e.activation(out=e_t[s4][:, :nq], in_=zp, func=AF.Exp, bias=bias_c, scale=1.0),
                 reads=[zk, "vec"], writes=[f"e{s4}"])
            if jb["mask"] is not None:
                K.op("dve", lambda e: e.tensor_tensor(out=e_t[s4][:, :nq], in0=e_t[s4][:, :nq], in1=jb["mask"],
                                                      op=ALU.mult), reads=[f"e{s4}", "cst"], writes=[f"e{s4}"])
            K.op("act", lambda e: e.activation(out=L_t[s4][:, :nq], in_=e_t[s4][:, :nq], func=AF.Ln, bias=one_c,
                                               scale=1.0), reads=[f"e{s4}"] + KCON, writes=[f"L{s4}"])

        def j_tri(i):
            jb = jobs[i]
            s4, nq = i % NE, jb["nq"]
            pC, ck = pCs[jb["kind"]], "p_C" + jb["kind"]
            K.op("pe", lambda e: e.matmul(pC[:, :nq], lhsT=tri, rhs=L_t[s4][:, :nq], start=jb["first"], stop=True,
                                          skip_group_check=True), reads=[f"L{s4}", "cst"], writes=[ck])

        def j_x(i):
            jb = jobs[i]
            sl, nq = i % 2, jb["nq"]
            pC, ck = pCs[jb["kind"]], "p_C" + jb["kind"]
            K.op("act", lambda e: e.activation(out=X_t[sl][:, :nq], in_=pC[:, :nq], func=AF.Exp, scale=-1.0),
                 reads=[ck], writes=[f"X{sl}"])

        def j_omt(i):
            jb = jobs[i]
            s4, sl, nq = i % NE, i % 2, jb["nq"]
            pC, ck = pCs[jb["kind"]], "p_C" + jb["kind"]
            K.op("pe", lambda e: e.matmul(pC[:, :nq], lhsT=omt, rhs=L_t[s4][:, :nq], start=False, stop=True,
                                          skip_group_check=True), reads=[f"L{s4}", "cst", f"X{sl}"],
                 writes=[ck])

        def j_a(i):
            jb = jobs[i]
            s4, sl, nq = i % NE, i % 2, jb["nq"]
            K.op("dve", lambda e: e.tensor_tensor(out=A_t[sl][:, :nq], in0=e_t[s4][:, :nq], in1=X_t[sl][:, :nq],
                                                  op=ALU.mult), reads=[f"e{s4}", f"X{sl}"], writes=[f"A{sl}"])

        def j_av(i):
            jb = jobs[i]
            sl, nq = i % 2, jb["nq"]
            pO, ok = pOs[jb["kind"]], "p_O" + jb["kind"]
            if jb["kind"] == "B":
                kb = jb["kb"]
                K.op("pe", lambda e: e.matmul(pO[:, :], lhsT=VA[:, kb, :], rhs=A_t[sl][:, :], start=jb["first"],
                                              stop=jb["last"], skip_group_check=True),
                     reads=VAK + [f"A{sl}"], writes=[ok])
                if jb["last"]:
                    so_epilogue("B", 512, jb["sbi"] * 512, jb["sbi"])
                return
            if jb["sub"] == "new":
                blk = jb["g"]
                K.op("pe", lambda e: e.matmul(pO[:, 0:T_S], lhsT=VA[:, T_P // 128 + blk, :], rhs=A_t[sl][:, 0:T_S],
                                              start=(blk == 0), stop=False, skip_group_check=True),
                     reads=VAK + [f"A{sl}"], writes=[ok])
            else:
                sg = jb["si"] % 2
                vkeys = [f"vstg{sg}_{b}" for b in range(NB)]
                K.op("dve", lambda e: e.tensor_copy(out=vbf[:], in_=vstg[sg][:]), reads=vkeys, writes=["vbf"])
                for b in range(NB):
                    K.op("pe", lambda e, b=b: e.matmul(
                        pO[:, b * TS:(b + 1) * TS], lhsT=vbf[:, b, :], rhs=A_t[sl][:, b * TS:(b + 1) * TS],
                        start=False, stop=jb["last"], skip_group_check=True), reads=["vbf", f"A{sl}"],
                        writes=[ok])
            c_load(jb["si"] + 2)
            if jb["last"]:
                so_epilogue("C", T_S, T_P, 1)

        if N:
            for i in range(min(3, N)):
                j_z(i)
                if i < 2:
                    j_s1(i)
            j_tri(0)
            for i in range(N):
                j_x(i)
                j_a(i)
                if i + 2 < N:
                    j_s1(i + 2)
                if i + 3 < N:
                    j_z(i + 3)
                j_omt(i)
                if i >= 1:
                    j_av(i - 1)
                if i + 1 < N:
                    j_tri(i + 1)
            j_av(N - 1)
        K.barrier()
    main.close()

    if "D" in phases:
        for k in range(NCH):
            K.cc(lambda g, k=k: g.collective_compute("AllGather", ALU.bypass,
                                                     replica_groups=[[0, 1, 2, 3], [4, 5, 6, 7]],
                                                     ins=[bounce[k]], outs=[g4[k]]), "cc", reads=BNC,
                 writes=[f"g4_{k}"])
        for k in range(NCH):
            K.cc(lambda g, k=k: g.collective_compute("AllGather", ALU.bypass,
                                                     replica_groups=[[0, 4], [1, 5], [2, 6], [3, 7]],
                                                     ins=[g4[k]], outs=[g8[k]]), "cc", reads=[f"g4_{k}"],
                 writes=[f"g8_{k}"])
    G8K = [f"g8_{k}" for k in range(NCH)]

    g8u8 = g8.bitcast(U8).tensor
    RB = CW * 2
    if "E" in phases:
        with ExitStack() as st:
            sb, ps = mk(st)
            Wrg = sb("Wrg", [128, KC, 512], BF16)
            Wo = sb("Wo", [128, KC, D], BF16)
            K.dma("pool", Wrg[:], w_rg.rearrange("(kc p) n -> p kc n", p=128), "ld1", writes=["Wrg"])
            K.dma("pool", Wo[:], w_out.rearrange("(kc p) n -> p kc n", p=128), "ld1", writes=["Wo"])
            inf_sb = sb("inf_sb", [1, NI * 9], I32)
            K.dma("sp", inf_sb[:], info, "ld_pt2", writes=["inf_sb"])
            xo = [sb(f"xo{i}", [128, KC, NT], F32) for i in range(2)]
            T = dict(sq=sb("sq", [128, KC, NT], BF16), tmpf=sb("tmpf", [128, NT], F32),
                     rstd=sb("rstd", [128, NT], F32), tln=sb("tln", [128, NT], F32),
                     p_ssq=ps("p_ssq", [128, 512], F32))
            hE = sb("hE", [128, KC, NT], BF16)
            M = [sb(f"M{i}", [128, 8, NT], BF16) for i in range(2)]
            sg = sb("sg", [128, 4, NT], F32)
            rog = sb("rog", [128, 4, NT], BF16)
            p_g = [ps(f"p_g{i}", [128, 512], F32) for i in range(2)]
            K.sync("sp", reads=["inf_sb"])
            for ti, (j0, n, scol) in enumerate(ptiles):
                sl = ti % 2
                xk = f"xo{sl}"
                K.dma("sp", xo[sl][:, :, :n], xTo[:, j0:j0 + n].rearrange("(kc p) n -> p kc n", p=128), xk,
                      writes=[xk])
                mk_ = f"M{sl}"
                MK = [mk_] + [f"M{sl}_{j}_{hh}" for j in range(4) for hh in range(2)]
                K.sync("sp", reads=G8K, writes=MK)
                col = nc.sync.value_load(inf_sb[0:1, ti * 9:ti * 9 + 1])
                K.dma("sp", M[sl][:, 0:4, :n].bitcast(U8),
                      bass.AP(g8u8, col, [[RB, 128], [192 * RB, 4], [1, 2 * n]]), mk_,
                      reads=G8K, writes=[mk_])
                nc.sync.free_register(nc.sync.to_reg(col))
                for j in range(4):
                    for hh in range(2):
                        i9 = ti * 9 + 1 + 2 * j + hh
                        col = nc.sync.value_load(inf_sb[0:1, i9:i9 + 1])
                        K.dma("sp", M[sl][64 * hh:64 * hh + 64, 4 + j, :n].bitcast(U8),
                              bass.AP(g8u8, col, [[RB, 64], [1, 2 * n]]),
                              mk_, reads=G8K, writes=[f"M{sl}_{j}_{hh}"])
                        nc.sync.free_register(nc.sync.to_reg(col))
                norm_tile(T, xo[sl], n, xk, gm1, 0, scol, hE, "hE")
                for fo in range(4):
                    pp = p_g[fo % 2]
                    for kc in range(KC):
                        K.op("pe", lambda e, kc=kc, fo=fo, pp=pp: e.matmul(
                            pp[:, :n], lhsT=Wrg[:, kc, fo * 128:(fo + 1) * 128], rhs=hE[:, kc, :n],
                            start=(kc == 0), stop=(kc == KC - 1)), reads=["Wrg", "hE"], writes=[f"p_g{fo % 2}"])
                    K.op("act", lambda e, fo=fo, pp=pp: e.activation(out=sg[:, fo, :n], in_=pp[:, :n],
                                                                     func=AF.Silu),
                         reads=[f"p_g{fo % 2}"], writes=["sg"])
                K.op("dve", lambda e, sl=sl: e.tensor_tensor(out=M[sl][:, 0:4, :n], in0=M[sl][:, 0:4, :n],
                                                             in1=sg[:, :, :n], op=ALU.mult),
                     reads=MK + ["sg"], writes=[mk_])
                for fo in range(KC):
                    pp = p_g[fo % 2]
                    for kc in range(KC):
                        K.op("pe", lambda e, kc=kc, fo=fo, pp=pp, sl=sl: e.matmul(
                            pp[:, :n], lhsT=Wo[:, kc, fo * 128:(fo + 1) * 128], rhs=M[sl][:, kc, :n],
                            start=(kc == 0), stop=(kc == KC - 1)), reads=["Wo"] + MK, writes=[f"p_g{fo % 2}"])
                    if scol == 0:
                        K.op("dve", lambda e, fo=fo, pp=pp, sl=sl: e.scalar_tensor_tensor(
                            out=xo[sl][:, fo, :n], in0=pp[:, :n], scalar=ada[:, 16 + fo, 0:1],
                            in1=xo[sl][:, fo, :n], op0=ALU.mult, op1=ALU.add),
                            reads=[f"p_g{fo % 2}", xk] + ADA, writes=[xk])
                    else:
                        nb = n // TS
                        t3 = T["tmpf"][:, :n].rearrange("p (b i) -> p b i", i=TS)
                        K.op("dve", lambda e, fo=fo, pp=pp, t3=t3: e.tensor_tensor(
                            out=t3, in0=pp[:, :n].rearrange("p (b i) -> p b i", i=TS),
                            in1=bc(ada[:, 16 + fo, scol:scol + nb], 2, TS), op=ALU.mult),
                            reads=[f"p_g{fo % 2}"] + ADA, writes=["tmpf"])
                        K.op("dve", lambda e, fo=fo, sl=sl: e.tensor_tensor(
                            out=xo[sl][:, fo, :n], in0=xo[sl][:, fo, :n], in1=T["tmpf"][:, :n], op=ALU.add),
                            reads=["tmpf", xk], writes=[xk])
                K.dma("pool", x1s[:, j0:j0 + n].rearrange("(kc p) n -> p kc n", p=128), xo[sl][:, :, :n],
                      f"st_x1{sl}", reads=[xk], writes=[f"x1s{ti}"])
            K.barrier()

        with ExitStack() as st:
            sb, ps = mk(st)
            Wug = sb("Wug", [128, KC, DFF], BF16)
            Wuv = sb("Wuv", [128, KC, DFF], BF16)
            Wdn = sb("Wdn", [128, FC, D], BF16)
            for hf in range(2):
                cs_ = slice(hf * (DFF // 2), (hf + 1) * (DFF // 2))
                K.dma("pool", Wug[:, :, cs_], w_ug[:, cs_].rearrange("(kc p) n -> p kc n", p=128), "ld1",
                      writes=[f"Wug{hf}"])
                K.dma("pool", Wuv[:, :, cs_], w_uv[:, cs_].rearrange("(kc p) n -> p kc n", p=128), "ld1",
                      writes=[f"Wuv{hf}"])
            K.dma("pool", Wdn[:], w_dn.rearrange("(fc p) n -> p fc n", p=128), "ld1", writes=["Wdn"])
            xo = [sb("xo0", [128, KC, NT], F32)] * 2
            T = dict(sq=sb("sq", [128, KC, NT], BF16), tmpf=sb("tmpf", [128, NT], F32),
                     rstd=sb("rstd", [128, NT], F32), tln=sb("tln", [128, NT], F32),
                     p_ssq=ps("p_ssq", [128, 512], F32))
            hE = sb("hE", [128, KC, NT], BF16)
            uE = sb("uE", [128, FC, NT], BF16)
            aE = [sb(f"aE{i}", [128, NT + 2], F32) for i in range(2)]
            aS = [sb(f"aS{i}", [128, 4, TS + 2], F32) for i in range(2)]
            cry = sb("cry", [128, FC, 2], F32)
            scv = sb("scv", [128, FC, 4, 2], F32)
            cvo = sb("cvo", [128, FC, 4, 2], F32)
            tcv = [sb(f"tcv{i}", [128, NT], F32) for i in range(2)]
            scv2 = [sb(f"sil{i}", [128, NT], F32) for i in range(2)]
            p_a = [ps(f"p_a{i}", [128, 512], F32) for i in range(2)]
            p_b = [ps(f"p_b{i}", [128, 512], F32) for i in range(2)]
            p_f = [ps(f"p_f{i}", [128, 512], F32) for i in range(2)]
            K.op("dve", lambda e: e.memset(cry[:], 0.0), writes=["cry"])
            K.dma("sp", scv[:], sconvT.rearrange("(fc p) b w -> p fc b w", p=128), "ld_pt2", writes=["scv"])
            halo = vec[:, V_HALO:V_HALO + 1]
            for ti, (j0, n, scol) in enumerate(ptiles):
                sl = 0
                xk = f"xo{sl}"
                K.dma("sp", xo[sl][:, :, :n], x1s[:, j0:j0 + n].rearrange("(kc p) n -> p kc n", p=128), xk,
                      reads=[f"x1s{ti}"], writes=[xk])
                norm_tile(T, xo[sl], n, xk, gf1, 24, scol, hE, "hE")
                lastp = (scol == 0 and j0 + n == 2 + TPC)
                for fc in range(FC):
                    s2 = fc % 2
                    pa, pb = p_a[s2], p_b[s2]
                    for kc in range(KC):
                        K.op("pe", lambda e, kc=kc, fc=fc, pa=pa: e.matmul(
                            pa[:, :n], lhsT=Wug[:, kc, fc * 128:(fc + 1) * 128], rhs=hE[:, kc, :n],
                            start=(kc == 0), stop=(kc == KC - 1)), reads=["Wug0", "Wug1", "hE"], writes=[f"p_a{s2}"])
                    for kc in range(KC):
                        K.op("pe", lambda e, kc=kc, fc=fc, pb=pb: e.matmul(
                            pb[:, :n], lhsT=Wuv[:, kc, fc * 128:(fc + 1) * 128], rhs=hE[:, kc, :n],
                            start=(kc == 0), stop=(kc == KC - 1)), reads=["Wuv0", "Wuv1", "hE"], writes=[f"p_b{s2}"])
                    cw = [vec[:, V_CW + j * FC + fc:V_CW + j * FC + fc + 1] for j in range(3)]
                    cbv = vec[:, V_CB + fc:V_CB + fc + 1]
                    tc_, si_ = tcv[s2], scv2[s2]
                    if scol == 0:
                        a_ = aE[s2]
                        ak = f"aE{s2}"
                        K.op("act", lambda e, a_=a_, pa=pa: e.activation(out=a_[:, 2:2 + n], in_=pa[:, :n],
                                                                         func=AF.Identity),
                             reads=[f"p_a{s2}"], writes=[ak])
                        K.op("pool", lambda e, a_=a_, fc=fc: e.tensor_copy(out=a_[:, 0:2], in_=cry[:, fc, :]),
                             reads=["cry"], writes=[ak])
                        if ti == 0:
                            K.op("pool", lambda e, a_=a_: e.tensor_scalar(out=a_[:, 2:4], in0=a_[:, 2:4],
                                                                          scalar1=halo, scalar2=None,
                                                                          op0=ALU.mult),
                                 reads=[ak, "vec"], writes=[ak])
                        K.op("pool", lambda e, a_=a_, fc=fc: e.tensor_copy(out=cry[:, fc, :], in_=a_[:, n:n + 2]),
                             reads=[ak], writes=["cry"])
                        v0, v1, v2 = a_[:, 0:n], a_[:, 1:n + 1], a_[:, 2:n + 2]
                        tv, sv, pbv = tc_[:, :n], si_[:, :n], pb[:, :n]
                        uo = uE[:, fc, :n]
                    else:
                        a_ = aS[s2]
                        ak = f"aS{s2}"
                        K.op("act", lambda e, a_=a_, pa=pa: e.activation(
                            out=a_[:, :, 2:2 + TS], in_=pa[:, :n].rearrange("p (b i) -> p b i", i=TS),
                            func=AF.Identity), reads=[f"p_a{s2}"], writes=[ak])
                        K.op("pool", lambda e, a_=a_, fc=fc: e.tensor_copy(out=a_[:, :, 0:2], in_=scv[:, fc, :, :]),
                             reads=["scv"], writes=[ak])
                        K.op("pool", lambda e, a_=a_, fc=fc: e.tensor_copy(out=cvo[:, fc, :, :],
                                                                           in_=a_[:, :, TS:TS + 2]),
                             reads=[ak], writes=["cvo"])
                        v0, v1, v2 = a_[:, :, 0:TS], a_[:, :, 1:TS + 1], a_[:, :, 2:TS + 2]
                        r3 = lambda ap_: ap_.rearrange("p (b i) -> p b i", i=TS)
                        tv, sv, pbv = r3(tc_[:, :n]), r3(si_[:, :n]), r3(pb[:, :n])
                        uo = r3(uE[:, fc, :n])
                    K.op("dve", lambda e, v0=v0, tv=tv, cw=cw, cbv=cbv: e.tensor_scalar(
                        out=tv, in0=v0, scalar1=cw[0], scalar2=cbv, op0=ALU.mult, op1=ALU.add),
                        reads=[ak, "vec"], writes=[f"tcv{s2}"])
                    K.op("dve", lambda e, v1=v1, tv=tv, cw=cw: e.scalar_tensor_tensor(
                        out=tv, in0=v1, scalar=cw[1], in1=tv, op0=ALU.mult, op1=ALU.add),
                        reads=[ak, "vec", f"tcv{s2}"], writes=[f"tcv{s2}"])
                    K.op("dve", lambda e, v2=v2, tv=tv, cw=cw: e.scalar_tensor_tensor(
                        out=tv, in0=v2, scalar=cw[2], in1=tv, op0=ALU.mult, op1=ALU.add),
                        reads=[ak, "vec", f"tcv{s2}"], writes=[f"tcv{s2}"])
                    K.op("act", lambda e, tv=tv, sv=sv: e.activation(out=sv, in_=tv, func=AF.Silu),
                         reads=[f"tcv{s2}"], writes=[f"sil{s2}"])
                    K.op("dve", lambda e, sv=sv, pbv=pbv, uo=uo: e.tensor_tensor(out=uo, in0=sv, in1=pbv,
                                                                                op=ALU.mult),
                         reads=[f"sil{s2}", f"p_b{s2}"], writes=["uE"])
                if lastp:
                    K.dma("pool", cvp.rearrange("(fc p) w -> p fc w", p=128), cry[:], "st_cv", reads=["cry"],
                          writes=["cvp"])
                if scol != 0:
                    K.dma("pool", cvs.rearrange("(fc p) b w -> p fc b w", p=128), cvo[:], "st_cv", reads=["cvo"],
                          writes=["cvs"])
                for fo in range(KC):
                    pf = p_f[fo % 2]
                    for fc in range(FC):
                        K.op("pe", lambda e, fc=fc, fo=fo, pf=pf: e.matmul(
                            pf[:, :n], lhsT=Wdn[:, fc, fo * 128:(fo + 1) * 128], rhs=uE[:, fc, :n],
                            start=(fc == 0), stop=(fc == FC - 1)), reads=["Wdn", "uE"], writes=[f"p_f{fo % 2}"])
                    if scol == 0:
                        K.op("dve", lambda e, fo=fo, pf=pf, sl=sl: e.scalar_tensor_tensor(
                            out=xo[sl][:, fo, :n], in0=pf[:, :n], scalar=ada[:, 40 + fo, 0:1],
                            in1=xo[sl][:, fo, :n], op0=ALU.mult, op1=ALU.add),
                            reads=[f"p_f{fo % 2}", xk] + ADA, writes=[xk])
                    else:
                        nb = n // TS
                        t3 = T["tmpf"][:, :n].rearrange("p (b i) -> p b i", i=TS)
                        K.op("dve", lambda e, fo=fo, pf=pf, t3=t3: e.tensor_tensor(
                            out=t3, in0=pf[:, :n].rearrange("p (b i) -> p b i", i=TS),
                            in1=bc(ada[:, 40 + fo, scol:scol + nb], 2, TS), op=ALU.mult),
                            reads=[f"p_f{fo % 2}"] + ADA, writes=["tmpf"])
                        K.op("dve", lambda e, fo=fo, sl=sl: e.tensor_tensor(
                            out=xo[sl][:, fo, :n], in0=xo[sl][:, fo, :n], in1=T["tmpf"][:, :n], op=ALU.add),
                            reads=["tmpf", xk], writes=[xk])
                norm_tile(T, xo[sl], n, xk, None, 0, 0, None, None)
                for kc in range(KC):
                    K.op("dve", lambda e, kc=kc, sl=sl: e.scalar_tensor_tensor(
                        out=xo[sl][:, kc, :n], in0=xo[sl][:, kc, :n], scalar=nm32[:, 16 + kc:17 + kc],
                        in1=T["rstd"][:, :n], op0=ALU.mult, op1=ALU.mult), reads=[xk, "nm32", "rstd"],
                        writes=[xk])
                lo = max(j0, 2)
                if j0 + n > lo:
                    K.dma("pool", yT[:, lo - 2:j0 + n - 2].rearrange("(kc p) n -> p kc n", p=128),
                          xo[sl][:, :, lo - j0:n], f"st_y{sl}", reads=[xk], writes=[f"yT{ti}"])
            K.barrier()
    K.barrier()
    outer.close()
    return nc


def _consts(T_P):
    c = np.zeros((128, NCST), np.float32)
    i = np.arange(128)
    c[:, C_ID:C_ID + 128] = np.eye(128)
    tri = (i[:, None] >= i[None, :]).astype(np.float32)
    c[:, C_TRI:C_TRI + 128] = tri
    c[:, C_OMT:C_OMT + 128] = 1.0 - tri
    c[:, C_ONE:C_ONE + 128] = 1.0
    c[:, C_MRET:C_MRET + 128] = (i[None, :] >= i[:, None])
    c[:, C_MRS:C_MRS + 128] = (i[None, :] >= i[:, None]) & ((i[None, :] // 8) == (i[:, None] // 8))
    t = np.arange(512)
    for j in range(4):
        c[:, C_DM + 512 * j:C_DM + 512 * (j + 1)] = ((128 * j + i)[:, None] < t[None, :])
    q = np.arange(256)
    for blk in range(2):
        kb_, ki_ = blk * 16 + i // 8, i % 8
        c[:, C_SN + 256 * blk:C_SN + 256 * (blk + 1)] = (kb_[:, None] == (q // 8)[None, :]) & \
            (ki_[:, None] < (q % 8)[None, :])
    bm = np.zeros((16, 128), np.float32)
    for b in range(16):
        bm[b, 8 * b:8 * b + 8] = 1.0
    c[:, C_BM:C_BM + 2048] = bm.reshape(1, 2048)
    c[:, C_RM:C_RM + 16] = (i[:, None] // 8 == np.arange(16)[None, :])
    return c


def _rot_table(T_P, past, h):
    half = 64
    inv = ROPE_BASE ** (-np.arange(half, dtype=np.float32) / half)
    lg = np.log(1.0 - 2.0 ** (-5.0 - h))
    pos = np.concatenate([np.arange(T_P), np.tile(past + np.arange(TS), NB)]).astype(np.float32)
    idx = np.concatenate([np.arange(T_P) % 128, np.tile(np.arange(TS), NB)]).astype(np.float64)
    ang = pos[:, None] * inv[None, :]
    cos, sin = np.cos(ang), np.sin(ang)
    dq = np.exp((idx + 1.0) * lg)[:, None]
    dk = (np.exp(-(idx + 1.0) * lg) * (128.0 ** -0.5))[:, None]
    tab = np.stack([cos * dq, sin * dq, sin * dq, cos * dq, cos * dk, sin * dk, sin * dk, cos * dk], axis=1)
    return np.ascontiguousarray(tab.astype(np.float32)), float(np.exp(128 * lg)), float(np.exp(8 * lg))


def _info(c, T_P, TPC):
    CW = 1024
    RB = CW * 2
    toks = [max(c * TPC - 2, 0)] + [c * TPC + j0 for j0 in range(0, TPC, 256)] + [T_P + 32 * c]
    out = []
    for t in toks:
        base = (t // CW) * (1536 * RB) + (t % CW) * 2
        out.append(base + 64 * RB)
        for r in range(8):
            out.append(base + r * 192 * RB)
    return np.array([out], np.int32)


PHASES = "0ABCDE"
DBG = _os.environ.get("KDBG", "")
LV = int(_os.environ.get("KLV", "9"))


def kernel(x_prompt, x_sample, cache_k_pages, cache_v_pages, page_table, state_ret, state_conv,
           c_prompt, c_sample, w_ada, b_ada, norm_mix, w_in, ret_gn, sb_gn, sb_bias, w_out,
           norm_ffn, w_up_gate, w_up_val, conv_w, conv_b, w_down, norm_final):
    f = lambda a: np.ascontiguousarray(np.asarray(a, dtype=np.float32))
    x_prompt, x_sample = f(x_prompt), f(x_sample)
    T_P = x_prompt.shape[1]
    NPG = page_table.shape[1]
    NPHYS = cache_k_pages.shape[1]
    past = NPG * cache_k_pages.shape[2]
    TPC = T_P // NCORES
    T_ALL = T_P + T_S
    nc = build(T_P, NPG, NPHYS, PHASES)

    xT = f(np.concatenate([x_prompt[0].T, x_sample.reshape(T_S, D).T], axis=1))
    w_in0 = f(w_in)[0]
    ck = np.asarray(cache_k_pages)[0]
    cv = np.asarray(cache_v_pages)[0]
    cst = _consts(T_P)
    ptab = np.ascontiguousarray(np.asarray(page_table, dtype=np.int32).reshape(1, NB * NPG))
    vbase = np.zeros((128, NV), np.float32)
    vbase[:, V_BADA:V_BADA + 48] = f(b_ada)[0].reshape(48, 128).T
    vbase[:, V_NMIX:V_NMIX + 8] = f(norm_mix)[0].reshape(8, 128).T
    vbase[:, V_NFFN:V_NFFN + 8] = f(norm_ffn)[0].reshape(8, 128).T
    vbase[:, V_NFIN:V_NFIN + 8] = f(norm_final).reshape(8, 128).T
    vbase[:, V_CW:V_CW + 66] = f(conv_w)[0].reshape(3, FC, 128).transpose(2, 0, 1).reshape(128, 66)
    vbase[:, V_CB:V_CB + FC] = f(conv_b)[0].reshape(FC, 128).T
    in_maps = []
    for c in range(NCORES):
        h = c % 4
        rot, gC, g8 = _rot_table(T_P, past, h)
        vec = vbase.copy()
        vec[:64, V_SBGN] = f(sb_gn)[0][64 * c:64 * c + 64]
        vec[:, V_SBB] = f(sb_bias)[0][c]
        vec[:, V_GC] = gC
        vec[:, V_G8] = g8
        vec[:, V_HALO] = 0.0 if c == 0 else 1.0
        cols = np.r_[0:64] + 64 * c
        rc = np.r_[0:128] + 128 * h
        w_c = np.concatenate([w_in0[:, 2048 + cols], w_in0[:, 2560 + cols], w_in0[:, 3072 + cols],
                              w_in0[:, rc], w_in0[:, 512 + rc], w_in0[:, 1024 + rc]], axis=1)
        xo = np.zeros((D, 2 + TPC + 4 * TS), np.float32)
        if c > 0:
            xo[:, 0:2] = xT[:, c * TPC - 2:c * TPC]
        xo[:, 2:2 + TPC] = xT[:, c * TPC:(c + 1) * TPC]
        xo[:, 2 + TPC:] = xT[:, T_P + 32 * c:T_P + 32 * (c + 1)]
        cT = np.concatenate([f(c_prompt).T, f(c_sample).T, f(c_sample)[4 * c:4 * c + 4].T], axis=1)
        in_maps.append(dict(
            xT=xT, xTo=xo, cT=f(cT), w_ada=f(w_ada)[0], vecsT=vec,
            rgn=f(np.tile(f(ret_gn)[0][128 * h:128 * h + 128][None, :], (128, 1))),
            w_c=f(w_c), w_rg=f(w_in0[:, 1536:2048]), w_out=f(w_out)[0], w_ug=f(w_up_gate)[0],
            w_uv=f(w_up_val)[0], w_dn=f(w_down)[0], rot=rot, cst=cst,
            poolKT=f(ck[:, :, c, :].transpose(0, 2, 1)).view(np.uint8).reshape(-1),
            poolV=f(cv[:, :, c, :]).view(np.uint8).reshape(-1), ptab=ptab,
            sret=f(np.asarray(state_ret)[0][:, h]),
            sconvT=f(np.asarray(state_conv)[0][4 * c:4 * c + 4].transpose(2, 0, 1)),
            info=_info(c, T_P, TPC)))
    res = run_bass_kernel_spmd(nc, in_maps, core_ids=list(range(NCORES))).results

    y_prompt = np.zeros((1, T_P, D), np.float32)
    y_sample = np.zeros((NB, TS, D), np.float32)
    kp = np.zeros((1, 1, T_P, 8, 64), np.float32)
    vp = np.zeros((1, 1, T_P, 8, 64), np.float32)
    ks = np.zeros((1, NB, TS, 8, 64), np.float32)
    vs = np.zeros((1, NB, TS, 8, 64), np.float32)
    rp = np.zeros((1, 1, 4, 128, 128), np.float32)
    rs = np.zeros((1, NB, 4, 128, 128), np.float32)
    cp = np.zeros((1, 1, 2, DFF), np.float32)
    cs = np.zeros((1, NB, 2, DFF), np.float32)
    for c in range(NCORES):
        r = res[c]
        y_prompt[0, c * TPC:(c + 1) * TPC] = r["yT"][:, :TPC].T
        y_sample[4 * c:4 * c + 4] = r["yT"][:, TPC:].T.reshape(4, TS, D)
        kT = r["kT_o"]
        kp[0, 0, :, c, :] = kT[:, :T_P].T
        ks[0, :, :, c, :] = kT[:, T_P:].T.reshape(NB, TS, 64)
        vp[0, 0, :, c, :] = r["v_o"][:T_P]
        vs[0, :, :, c, :] = r["v_o"][T_P:].reshape(NB, TS, 64)
        if c < 4:
            rp[0, 0, c] = r["rsp"]
            rs[0, :, c] = r["rss"]
        cs[0, 4 * c:4 * c + 4] = r["cvs"].transpose(1, 2, 0)
        if c == NCORES - 1:
            cp[0, 0] = r["cvp"].T
    return (y_prompt, y_sample, kp, vp, ks, vs, rp, rs, cp, cs)
```

```python
from contextlib import ExitStack
import os as _os
import numpy as np
import concourse.bass as bass
import concourse.mybir as mybir
from concourse.bass_utils import run_bass_kernel_spmd

F32 = mybir.dt.float32
BF16 = mybir.dt.bfloat16
I32 = mybir.dt.int32
U8 = mybir.dt.uint8
PGB = 64 * 128 * 4
AF = mybir.ActivationFunctionType
ALU = mybir.AluOpType

NCORES = 8
D = 1024
KC = 8
DFF = 2816
FC = 22
EPS = 1e-6
NB = 32
TS = 8
T_S = NB * TS
ROPE_BASE = 10000.0

V_BADA = 0
V_NMIX = 48
V_NFFN = 56
V_NFIN = 64
V_CW = 72
V_CB = 138
V_SBGN = 160
V_SBB = 161
V_GC = 162
V_G8 = 163
V_HALO = 164
NV = 165

C_ID = 0
C_TRI = 128
C_OMT = 256
C_ONE = 384
C_MRET = 512
C_MRS = 640
C_DM = 768
C_SN = C_DM + 2048
C_BM = C_SN + 512
C_RM = C_BM + 2048
NCST = C_RM + 16


def bc(ap, axis, n):
    l = [list(x) for x in ap.ap]
    l.insert(axis, [0, n])
    return bass.AP(ap.tensor, ap.offset, l)


def bcl(ap, n):
    l = [list(x) for x in ap.ap]
    assert l[-1][1] == 1
    l[-1] = [0, n]
    return bass.AP(ap.tensor, ap.offset, l)


class Builder:
    def __init__(self, nc):
        self.nc = nc
        self.E = {"pe": nc.tensor, "act": nc.scalar, "dve": nc.vector, "pool": nc.gpsimd, "sp": nc.sync}
        self.esem = {e: nc.alloc_semaphore("es_" + e) for e in self.E}
        self.ecnt = {e: 0 for e in self.E}
        self.dsem = {}
        self.dcnt = {}
        self.seen = {e: {} for e in self.E}
        self.W = {}
        self.R = {}

    def _need(self, eng, reads, writes):
        need = {}

        def add(evs, same_ok):
            for (kind, name), val in evs.items():
                if kind == "e" and name == eng and eng == "pe":
                    continue
                if kind == "d":
                    val = self.dcnt[name]
                k = (kind, name)
                if need.get(k, 0) < val:
                    need[k] = val

        for k in reads:
            add(self.W.get(k, {}), True)
            if k.startswith("p_"):
                add({kk: vv for kk, vv in self.R.get(k, {}).items() if kk != ("e", eng)}, True)
        for k in writes:
            add(self.W.get(k, {}), False)
            add(self.R.get(k, {}), False)
        return need

    def _waits(self, eng, need):
        for k, val in need.items():
            if self.seen[eng].get(k, 0) >= val:
                continue
            sem = self.esem[k[1]] if k[0] == "e" else self.dsem[k[1]]
            self.E[eng].wait_ge(sem, val)
            self.seen[eng][k] = val

    def _post(self, ev, val, reads, writes):
        for k in reads:
            d = self.R.setdefault(k, {})
            if d.get(ev, 0) < val:
                d[ev] = val
        for k in writes:
            self.W[k] = {ev: val}
            self.R[k] = {}

    def sync(self, eng, reads=(), writes=()):
        self._waits(eng, self._need(eng, reads, writes))

    def op(self, eng, fn, reads=(), writes=()):
        self._waits(eng, self._need(eng, reads, writes))
        ins = fn(self.E[eng])
        self.ecnt[eng] += 1
        ins.then_inc(self.esem[eng], 1)
        self._post(("e", eng), self.ecnt[eng], reads, writes)

    def _dsem(self, sem):
        if sem not in self.dsem:
            self.dsem[sem] = self.nc.alloc_semaphore("ds_" + sem)
            self.dcnt[sem] = 0

    def dma(self, q, out, in_, sem, reads=(), writes=()):
        self._waits(q, self._need(q, reads, writes))
        self._dsem(sem)
        self.E[q].dma_start(out=out, in_=in_).then_inc(self.dsem[sem], 16)
        self.dcnt[sem] += 16
        self._post(("d", sem), self.dcnt[sem], reads, writes)

    def cc(self, fn, sem, reads=(), writes=()):
        self._waits("pool", self._need("pool", reads, writes))
        self._dsem(sem)
        fn(self.E["pool"]).then_inc(self.dsem[sem], 1)
        self.dcnt[sem] += 1
        self._post(("d", sem), self.dcnt[sem], reads, writes)

    def barrier(self):
        for eng in self.E:
            need = {}
            for e2 in self.E:
                if e2 != eng and self.ecnt[e2] > 0:
                    need[("e", e2)] = self.ecnt[e2]
            for s, v in self.dcnt.items():
                need[("d", s)] = v
            self._waits(eng, need)
        self.W = {}
        self.R = {}


def build(T_P, NPG, NPHYS, phases="0ABCDE"):
    TPC = T_P // NCORES
    T_ALL = T_P + T_S
    NOWN = 2 + TPC + 4 * TS
    CW = 1024
    NCH = (T_ALL + CW - 1) // CW
    nc = bass.Bass("TRN2", target_bir_lowering=False)
    K = Builder(nc)

    def din(name, shape, dt=F32):
        return nc.dram_tensor(name, list(shape), dt, kind="ExternalInput").ap()

    def dout(name, shape, dt=F32):
        return nc.dram_tensor(name, list(shape), dt, kind="ExternalOutput").ap()

    xT = din("xT", [D, T_ALL])
    xTo = din("xTo", [D, NOWN])
    cT = din("cT", [D, 37])
    w_ada = din("w_ada", [D, 6 * D])
    vecsT = din("vecsT", [128, NV])
    rgn = din("rgn", [128, 128])
    w_c = din("w_c", [D, 576])
    w_rg = din("w_rg", [D, 512])
    w_out = din("w_out", [D, D])
    w_ug = din("w_ug", [D, DFF])
    w_uv = din("w_uv", [D, DFF])
    w_dn = din("w_dn", [DFF, D])
    rot = din("rot", [T_ALL, 8, 64])
    cst = din("cst", [128, NCST])
    poolKT = din("poolKT", [NPHYS * PGB], U8)
    poolV = din("poolV", [NPHYS * PGB], U8)
    ptab = din("ptab", [1, NB * NPG], I32)
    sret = din("sret", [NB, 128, 128])
    sconvT = din("sconvT", [DFF, 4, 2])
    NT = 256
    ptiles = [(0, 2, 0)] + [(2 + j0, min(NT, TPC - j0), 0) for j0 in range(0, TPC, NT)] + [(2 + TPC, 4 * TS, 33)]
    NI = len(ptiles)
    info = din("info", [1, NI * 9], I32)

    yT = dout("yT", [D, TPC + 4 * TS])
    kT_o = dout("kT_o", [64, T_ALL])
    v_o = dout("v_o", [T_ALL, 64])
    rsp = dout("rsp", [128, 128])
    rss = dout("rss", [NB, 128, 128])
    cvp = dout("cvp", [DFF, 2])
    cvs = dout("cvs", [DFF, 4, 2])

    bounce = nc.dram_tensor("bounce", [NCH, 192, CW], BF16).ap()
    g4 = nc.dram_tensor("g4", [NCH, 4 * 192, CW], BF16).ap()
    g8 = nc.dram_tensor("g8", [NCH, 8 * 192, CW], BF16).ap()
    x1s = nc.dram_tensor("x1s", [D, NOWN], F32).ap()

    outer = ExitStack()

    _cnt = [0]

    def mk(st):
        _cnt[0] += 1
        pre = f"s{_cnt[0]}_"

        def sb(name, shape, dt):
            return st.enter_context(nc.sbuf_tensor(pre + name, list(shape), dt))

        def ps(name, shape, dt):
            return st.enter_context(nc.psum_tensor(pre + name, list(shape), dt))
        return sb, ps

    sbP, _ = mk(outer)

    vec = sbP("vec", [128, NV], F32)
    cb = sbP("cstb", [128, NCST], BF16)
    ada = sbP("ada", [128, 48, 37], F32)
    gm1 = sbP("gm1", [128, KC, 37], F32)
    gf1 = sbP("gf1", [128, KC, 37], F32)
    nm32 = sbP("nm32", [128, 24], F32)
    kcon = sbP("kcon", [128, 4], F32)
    rgn_sb = sbP("rgn_sb", [128, 128], F32)

    K.dma("sp", vec[:], vecsT, "ld0", writes=["vec"])
    K.dma("sp", rgn_sb[:], rgn, "ld0", writes=["rgn"])
    for c0 in range(0, NCST, 1024):
        c1 = min(NCST, c0 + 1024)
        K.dma("pool", cb[:, c0:c1], cst[:, c0:c1], "ld1", writes=[f"cst{c0}"])
    K.sync("pool", reads=[f"cst{c0}" for c0 in range(0, NCST, 1024)], writes=["cst"])
    K.op("pool", lambda e: e.memset(kcon[:, 3:4], 0.0), reads=[f"cst{c0}" for c0 in range(0, NCST, 1024)],
         writes=["cst", "kc3"])
    K.op("dve", lambda e: e.memset(kcon[:, 0:1], 1.0), writes=["kc0"])
    K.op("dve", lambda e: e.memset(kcon[:, 1:2], 1024.0 * EPS), writes=["kc1"])
    K.op("dve", lambda e: e.memset(kcon[:, 2:3], EPS), writes=["kc2"])
    KCON = ["kc0", "kc1", "kc2", "kc3"]
    one_c, eps1k_c, eps_c = kcon[:, 0:1], kcon[:, 1:2], kcon[:, 2:3]

    ident = cb[:, C_ID:C_ID + 128]
    tri = cb[:, C_TRI:C_TRI + 128]
    omt = cb[:, C_OMT:C_OMT + 128]
    ones = cb[:, C_ONE:C_ONE + 128]
    ADA = [f"ada{i}" for i in range(48)]

    def rsqrt_ps(out, pin, scale, bias_ap, key_in, key_out, tmp, key_tmp):
        K.op("act", lambda e: e.activation(out=tmp, in_=pin, func=AF.Ln, bias=bias_ap, scale=scale),
             reads=[key_in] + KCON, writes=[key_tmp])
        K.op("act", lambda e: e.activation(out=out, in_=tmp, func=AF.Exp, scale=-0.5),
             reads=[key_tmp], writes=[key_out])

    with ExitStack() as st:
        sb, ps = mk(st)
        c_sb = sb("c_sb", [128, KC, 37], F32)
        s_bf = sb("s_bf", [128, KC, 37], BF16)
        wA = [sb(f"wA{i}", [128, KC, D], BF16) for i in range(2)]
        pA = [ps(f"pA{i}", [128, 512], F32) for i in range(2)]
        K.dma("sp", c_sb[:], cT.rearrange("(kc p) n -> p kc n", p=128), "ld0", writes=["c_sb"])
        K.op("act", lambda e: e.activation(out=s_bf[:], in_=c_sb[:], func=AF.Silu), reads=["c_sb"],
             writes=["s_bf"])
        for j in range(6):
            w = wA[j % 2]
            K.dma("pool", w[:], w_ada[:, j * D:(j + 1) * D].rearrange("(kc p) n -> p kc n", p=128), f"wA{j % 2}",
                  writes=[f"wA{j % 2}"])
            for fo in range(8):
                pp = pA[fo % 2]
                for kc in range(KC):
                    K.op("pe", lambda e, kc=kc, fo=fo, w=w, pp=pp: e.matmul(
                        pp[:, 0:37], lhsT=w[:, kc, fo * 128:(fo + 1) * 128], rhs=s_bf[:, kc, :],
                        start=(kc == 0), stop=(kc == KC - 1)),
                        reads=[f"wA{j % 2}", "s_bf"], writes=[f"pA{fo % 2}"])
                col = j * 8 + fo
                K.op("act", lambda e, col=col, pp=pp: e.activation(
                    out=ada[:, col, :], in_=pp[:, 0:37], func=AF.Identity,
                    bias=vec[:, V_BADA + col:V_BADA + col + 1], scale=1.0),
                    reads=[f"pA{fo % 2}", "vec"], writes=[f"ada{col}"])
        K.op("dve", lambda e: e.tensor_scalar(out=nm32[:], in0=vec[:, V_NMIX:V_NMIX + 24], scalar1=32.0,
                                              scalar2=None, op0=ALU.mult), reads=["vec"], writes=["nm32"])
        for kc in range(KC):
            K.op("dve", lambda e, kc=kc: e.scalar_tensor_tensor(
                out=gm1[:, kc, :], in0=ada[:, 8 + kc, :], scalar=1.0, in1=bcl(nm32[:, kc:kc + 1], 37),
                op0=ALU.add, op1=ALU.mult), reads=ADA + ["nm32"], writes=["gm1"])
            K.op("dve", lambda e, kc=kc: e.scalar_tensor_tensor(
                out=gf1[:, kc, :], in0=ada[:, 32 + kc, :], scalar=1.0, in1=bcl(nm32[:, 8 + kc:9 + kc], 37),
                op0=ALU.add, op1=ALU.mult), reads=ADA + ["nm32"], writes=["gf1"])
        K.barrier()

    def norm_tile(T, xtile, n, xkey, gtab, shrow, scol, hout, hkey):
        sq, p_ssq, rstd, tln, tmpf = T["sq"], T["p_ssq"], T["rstd"], T["tln"], T["tmpf"]
        K.op("act", lambda e: e.activation(out=sq[:, :, :n], in_=xtile[:, :, :n], func=AF.Square),
             reads=[xkey], writes=["sq"])
        for kc in range(KC):
            K.op("pe", lambda e, kc=kc: e.matmul(p_ssq[:, :n], lhsT=ones, rhs=sq[:, kc, :n], start=(kc == 0),
                                                 stop=(kc == KC - 1)), reads=["sq", "cst"], writes=["p_ssq"])
        rsqrt_ps(rstd[:, :n], p_ssq[:, :n], 1.0, eps1k_c, "p_ssq", "rstd", tln[:, :n], "tln")
        if hout is None:
            return
        for kc in range(KC):
            if scol == 0:
                K.op("dve", lambda e, kc=kc: e.scalar_tensor_tensor(
                    out=tmpf[:, :n], in0=xtile[:, kc, :n], scalar=gtab[:, kc, 0:1], in1=rstd[:, :n],
                    op0=ALU.mult, op1=ALU.mult), reads=[xkey, "gm1", "gf1", "rstd"], writes=["tmpf"])
                K.op("act", lambda e, kc=kc: e.activation(
                    out=hout[:, kc, :n], in_=tmpf[:, :n], func=AF.Identity, bias=ada[:, shrow + kc, 0:1],
                    scale=1.0), reads=["tmpf"] + ADA, writes=[hkey])
            else:
                nb = n // TS
                t3 = tmpf[:, :n].rearrange("p (b i) -> p b i", i=TS)
                K.op("dve", lambda e, kc=kc: e.tensor_tensor(out=tmpf[:, :n], in0=xtile[:, kc, :n],
                                                             in1=rstd[:, :n], op=ALU.mult),
                     reads=[xkey, "rstd"], writes=["tmpf"])
                K.op("dve", lambda e, kc=kc, t3=t3: e.tensor_tensor(
                    out=t3, in0=t3, in1=bc(gtab[:, kc, scol:scol + nb], 2, TS), op=ALU.mult),
                    reads=["tmpf", "gm1", "gf1"], writes=["tmpf"])
                K.op("dve", lambda e, kc=kc, t3=t3: e.tensor_tensor(
                    out=hout[:, kc, :n].rearrange("p (b i) -> p b i", i=TS), in0=t3,
                    in1=bc(ada[:, shrow + kc, scol:scol + nb], 2, TS), op=ALU.add),
                    reads=["tmpf"] + ADA, writes=[hkey])

    BNC = []
    CCDONE = set()

    def exchange(k):
        if k in CCDONE or "D" not in phases:
            return
        CCDONE.add(k)
        K.cc(lambda g: g.collective_compute("AllGather", ALU.bypass, replica_groups=[[0, 1, 2, 3], [4, 5, 6, 7]],
                                            ins=[bounce[k]], outs=[g4[k]]), "cc", reads=list(BNC),
             writes=[f"g4_{k}"])
        K.cc(lambda g: g.collective_compute("AllGather", ALU.bypass, replica_groups=[[0, 4], [1, 5], [2, 6], [3, 7]],
                                            ins=[g4[k]], outs=[g8[k]]), "cc", reads=[f"g4_{k}"],
             writes=[f"g8_{k}"])


    main = ExitStack()
    sbM, _ = mk(main)
    QT = sbM("QT", [64, T_ALL], BF16)
    KT = sbM("KT", [64, T_ALL], BF16)
    NBLK = T_ALL // 128
    VA = sbM("VA", [128, NBLK, 64], BF16)
    TT = 256
    tiles = [(t0, min(TT, T_P - t0), False) for t0 in range(0, T_P, TT)] + [(T_P, T_S, True)]
    QTK = [f"QT{i}" for i in range(len(tiles))]
    KTK = [f"KT{i}" for i in range(len(tiles))]
    VAK = [f"VA{i}" for i in range(len(tiles))]

    with ExitStack() as st:
        sb, ps = mk(st)
        Wc = sb("Wc", [128, KC, 576], BF16)
        K.dma("pool", Wc[:], w_c.rearrange("(kc p) n -> p kc n", p=128), "ld1", writes=["Wc"])
        xt = [sb(f"xt{i}", [128, KC, TT], F32) for i in range(2)]
        T = dict(sq=sb("sq", [128, KC, TT], BF16), tmpf=sb("tmpf", [128, TT], F32),
                 rstd=sb("rstd", [128, TT], F32), tln=sb("tln", [128, TT], F32),
                 p_ssq=ps("p_ssq", [128, 512], F32))
        p_ssq = T["p_ssq"]
        hb = sb("hb", [128, KC, TT], BF16)
        kst = [sb(f"kst{i}", [64, TT], F32) for i in range(2)]
        vst = [sb(f"vst{i}", [128, TT // 128, 64], F32) for i in range(2)]
        rt = [sb(f"rt{i}", [128, TT // 128, 8, 64], F32) for i in range(2)]
        ta = sb("ta", [128, 2, 64], F32)
        tb = sb("tb", [128, 2, 64], F32)
        qr = sb("qr", [128, 128], BF16)
        kr = sb("kr", [128, 128], BF16)
        vr = sb("vr", [128, 128], BF16)
        qT = sb("qTr", [128, 128], BF16)
        kTt = sb("kTr", [128, 128], BF16)
        scm = sb("scm", [128, 128], BF16)
        S32 = sb("S32", [128, 128], F32)
        Sbf = sb("Sbf", [128, 128], BF16)
        stmp = sb("stmp", [128, 128], F32)
        SD = nc.vector.BN_STATS_DIM
        AD = nc.vector.BN_AGGR_DIM
        bst = sb("bst", [128, SD], F32)
        mv = sb("mv", [128, AD], F32)
        rs2 = sb("rs2", [128, 2], F32)
        onr = sb("onr", [128, 128], F32)
        onb = sb("onb", [128, 128], BF16)
        rost = [sb(f"rost{i}", [128, TT], BF16) for i in range(2)]
        s0f = sb("s0f", [128, 16, 128], F32)
        s0b = sb("s0b", [128, 16, 128], BF16)
        qm = sb("qm", [128, 16, 128], BF16)
        km = sb("km", [128, 16, 128], BF16)
        snew = sb("snew", [128, 4, 128], F32)
        p_q = ps("p_q", [128, 512], F32)
        p_k = ps("p_k", [128, 512], F32)
        p_v = ps("p_v", [128, 8, 64], F32)
        p_r = ps("p_r", [128, 4, 128], F32)
        p_t = ps("p_t", [128, 8, 128], BF16)
        p_s = ps("p_s", [128, 4, 128], F32)
        p_o = ps("p_o", [128, 512], F32)

        used = T_ALL - (NCH - 1) * CW
        if used < CW:
            zt = sb("zt", [128, CW - used], BF16)
            K.op("dve", lambda e: e.memset(zt[:], 0.0), writes=["zt"])
            K.dma("pool", bounce[NCH - 1, 0:128, used:CW], zt[:], "st_b", reads=["zt"], writes=["bnc_pad0"])
            K.dma("pool", bounce[NCH - 1, 128:192, used:CW], zt[0:64, :], "st_b", reads=["zt"],
                  writes=["bnc_pad1"])
            BNC += ["bnc_pad0", "bnc_pad1"]
        K.op("dve", lambda e: e.memset(S32[:], 0.0), writes=["S32"])
        K.op("dve", lambda e: e.memset(Sbf[:], 0.0), writes=["Sbf"])
        gC = vec[:, V_GC:V_GC + 1]
        g8c = vec[:, V_G8:V_G8 + 1]
        mret = cb[:, C_MRET:C_MRET + 128]
        mrs = cb[:, C_MRS:C_MRS + 128]

        for ti, (t0, n, is_s) in enumerate(tiles if "A" in phases else []):
            sl = ti % 2
            xk = f"xt{sl}"
            nch = n // 128
            K.dma("sp", xt[sl][:, :, :n], xT[:, t0:t0 + n].rearrange("(kc p) n -> p kc n", p=128), xk,
                  writes=[xk])
            K.dma("sp", rt[sl][:, :nch], rot[t0:t0 + n].rearrange("(c p) a k -> p c a k", p=128), f"rt{sl}",
                  writes=[f"rt{sl}"])
            if LV < 2:
                continue
            norm_tile(T, xt[sl], n, xk, gm1, 0, (1 if is_s else 0), hb, "hb")
            if LV < 3:
                continue
            for kc in range(KC):
                K.op("pe", lambda e, kc=kc: e.matmul(p_q[0:64, :n], lhsT=Wc[:, kc, 0:64], rhs=hb[:, kc, :n],
                                                     start=(kc == 0), stop=(kc == KC - 1)),
                     reads=["Wc", "hb"], writes=["p_q"])
            K.op("act", lambda e: e.activation(out=QT[:, t0:t0 + n], in_=p_q[0:64, :n], func=AF.Identity,
                                               scale=0.125), reads=["p_q"], writes=[f"QT{ti}"])
            if LV < 4:
                continue
            for kc in range(KC):
                K.op("pe", lambda e, kc=kc: e.matmul(p_k[0:64, :n], lhsT=Wc[:, kc, 64:128], rhs=hb[:, kc, :n],
                                                     start=(kc == 0), stop=(kc == KC - 1)),
                     reads=["Wc", "hb"], writes=["p_k"])
            if "k4" in DBG:
                K.op("act", lambda e: e.activation(out=KT[:, t0:t0 + n], in_=p_k[0:64, :n], func=AF.Identity),
                     reads=["p_k"], writes=[f"KT{ti}"])
            else:
                K.op("dve", lambda e: e.tensor_copy(out=KT[:, t0:t0 + n], in_=p_k[0:64, :n]), reads=["p_k"],
                     writes=[f"KT{ti}"])
            if "k5" not in DBG:
                K.op("act", lambda e: e.activation(out=kst[sl][:, :n], in_=p_k[0:64, :n], func=AF.Identity),
                     reads=["p_k"], writes=[f"kst{sl}"])
            if "k1" not in DBG:
                K.dma("sp" if "k3" in DBG else "pool", kT_o[:, t0:t0 + n], kst[sl][:, :n], f"st_k{sl}",
                      reads=[f"kst{sl}"], writes=[f"kTo{ti}"])
            if LV < 5:
                continue
            for c in range(nch):
                for kc in range(KC):
                    K.op("pe", lambda e, kc=kc, c=c: e.matmul(p_v[:, c, :], lhsT=hb[:, kc, c * 128:(c + 1) * 128],
                                                              rhs=Wc[:, kc, 128:192], start=(kc == 0),
                                                              stop=(kc == KC - 1)),
                         reads=["Wc", "hb"], writes=["p_v"])
            b0 = t0 // 128
            K.op("dve", lambda e: e.tensor_copy(out=VA[:, b0:b0 + nch, :], in_=p_v[:, :nch, :]), reads=["p_v"],
                 writes=[f"VA{ti}"])
            K.op("act", lambda e: e.activation(out=vst[sl][:, :nch, :], in_=p_v[:, :nch, :], func=AF.Identity),
                 reads=["p_v"], writes=[f"vst{sl}"])
            K.dma("pool", v_o[t0:t0 + n, :].rearrange("(c p) d -> p c d", p=128), vst[sl][:, :nch, :],
                  f"st_v{sl}", reads=[f"vst{sl}"], writes=[f"vo{ti}"])
            for c in range(nch if "R" not in DBG else 0):
                for j in range(3):
                    for kc in range(KC):
                        K.op("pe", lambda e, kc=kc, c=c, j=j: e.matmul(
                            p_r[:, j, :], lhsT=hb[:, kc, c * 128:(c + 1) * 128],
                            rhs=Wc[:, kc, 192 + 128 * j:320 + 128 * j], start=(kc == 0), stop=(kc == KC - 1)),
                            reads=["Wc", "hb"], writes=["p_r"])
                for j, dst, dk_ in ((0, qr, "qr"), (1, kr, "kr")):
                    src = p_r[:, j, :].rearrange("p (a k) -> p a k", a=2)
                    K.op("dve", lambda e, j=j, src=src, c=c: e.tensor_tensor(
                        out=ta[:], in0=src, in1=rt[sl][:, c, 4 * j:4 * j + 2, :], op=ALU.mult),
                        reads=["p_r", f"rt{sl}"], writes=["ta"])
                    K.op("dve", lambda e, j=j, src=src, c=c: e.tensor_tensor(
                        out=tb[:], in0=src, in1=rt[sl][:, c, 4 * j + 2:4 * j + 4, :], op=ALU.mult),
                        reads=["p_r", f"rt{sl}"], writes=["tb"])
                    K.op("pool", lambda e, dst=dst: e.tensor_tensor(out=dst[:, 0:64], in0=ta[:, 0, :],
                                                                    in1=ta[:, 1, :], op=ALU.subtract),
                         reads=["ta"], writes=[dk_ + "a"])
                    K.op("pool", lambda e, dst=dst: e.tensor_tensor(out=dst[:, 64:128], in0=tb[:, 0, :],
                                                                    in1=tb[:, 1, :], op=ALU.add),
                         reads=["tb"], writes=[dk_ + "b"])
                K.op("act", lambda e: e.activation(out=vr[:], in_=p_r[:, 2, :], func=AF.Identity), reads=["p_r"],
                     writes=["vr"])
                K.op("pe", lambda e: e.transpose(out=p_t[:, 0, :], in_=qr[:], identity=ident),
                     reads=["qra", "qrb", "cst"], writes=["p_t"])
                K.op("pe", lambda e: e.transpose(out=p_t[:, 1, :], in_=kr[:], identity=ident),
                     reads=["kra", "krb", "cst"], writes=["p_t"])
                K.op("act", lambda e: e.activation(out=qT[:], in_=p_t[:, 0, :], func=AF.Identity), reads=["p_t"],
                     writes=["qT"])
                K.op("dve", lambda e: e.tensor_copy(out=kTt[:], in_=p_t[:, 1, :]), reads=["p_t"],
                     writes=["kTt"])
                K.op("pe", lambda e: e.matmul(p_s[:, 0, :], lhsT=kTt[:], rhs=qT[:], start=True, stop=True),
                     reads=["kTt", "qT"], writes=["p_s"])
                msk = mrs if is_s else mret
                K.op("dve", lambda e, msk=msk: e.tensor_tensor(out=scm[:], in0=p_s[:, 0, :], in1=msk,
                                                               op=ALU.mult),
                     reads=["p_s", "cst"], writes=["scm"])
                if not is_s:
                    K.op("pe", lambda e: e.matmul(p_o[:, 0:128], lhsT=scm[:], rhs=vr[:], start=True, stop=False),
                         reads=["scm", "vr"], writes=["p_o"])
                    K.op("pe", lambda e: e.matmul(p_o[:, 0:128], lhsT=qT[:], rhs=Sbf[:], start=False, stop=True),
                         reads=["qT", "Sbf"], writes=["p_o"])
                    K.op("pe", lambda e: e.matmul(p_s[:, 1, :], lhsT=kr[:], rhs=vr[:], start=True, stop=True),
                         reads=["kra", "krb", "vr"], writes=["p_s"])
                    K.op("dve", lambda e: e.tensor_tensor(out=stmp[:], in0=S32[:], in1=p_s[:, 1, :], op=ALU.add),
                         reads=["S32", "p_s"], writes=["stmp"])
                    K.op("act", lambda e: e.activation(out=S32[:], in_=stmp[:], func=AF.Identity, scale=gC),
                         reads=["stmp", "vec"], writes=["S32"])
                    K.op("act", lambda e: e.activation(out=Sbf[:], in_=stmp[:], func=AF.Identity, scale=gC),
                         reads=["stmp", "vec"], writes=["Sbf"])
                else:
                    blk = c
                    K.dma("sp", s0f[:], sret[blk * 16:(blk + 1) * 16].rearrange("b k v -> k b v"), "s0f",
                          writes=["s0f"])
                    K.op("act", lambda e: e.activation(out=s0b[:], in_=s0f[:], func=AF.Identity), reads=["s0f"],
                         writes=["s0b"])
                    K.op("dve", lambda e: e.tensor_tensor(
                        out=qm[:], in0=bc(qT[:], 1, 16),
                        in1=cb[:, C_BM:C_BM + 2048].rearrange("p (b n) -> p b n", b=16), op=ALU.mult),
                        reads=["qT", "cst"], writes=["qm"])
                    K.op("dve", lambda e: e.tensor_tensor(out=km[:], in0=bc(kr[:], 1, 16),
                                                          in1=bc(cb[:, C_RM:C_RM + 16], 2, 128), op=ALU.mult),
                         reads=["kra", "krb", "cst"], writes=["km"])
                    K.op("pe", lambda e: e.matmul(p_o[:, 0:128], lhsT=scm[:], rhs=vr[:], start=True, stop=False),
                         reads=["scm", "vr"], writes=["p_o"])
                    for b in range(16):
                        K.op("pe", lambda e, b=b: e.matmul(p_o[:, 0:128], lhsT=qm[:, b, :], rhs=s0b[:, b, :],
                                                           start=False, stop=(b == 15)),
                             reads=["qm", "s0b"], writes=["p_o"])
                    for g in range(4):
                        for bb in range(4):
                            b = g * 4 + bb
                            K.op("pe", lambda e, b=b, bb=bb: e.matmul(p_ssq[:, bb * 128:(bb + 1) * 128],
                                                                      lhsT=km[:, b, :], rhs=vr[:], start=True,
                                                                      stop=True),
                                 reads=["km", "vr"], writes=["p_ssq"])
                        K.op("dve", lambda e, g=g: e.tensor_tensor(
                            out=snew[:], in0=s0f[:, g * 4:(g + 1) * 4, :],
                            in1=p_ssq[:].rearrange("p (a d) -> p a d", a=4), op=ALU.add),
                            reads=["s0f", "p_ssq"], writes=["snew"])
                        K.op("act", lambda e: e.activation(out=snew[:], in_=snew[:], func=AF.Identity, scale=g8c),
                             reads=["snew", "vec"], writes=["snew"])
                        K.dma("pool", rss[blk * 16 + g * 4:blk * 16 + g * 4 + 4].rearrange("b k v -> k b v"),
                              snew[:], "st_rss", reads=["snew"], writes=[f"rss{blk}_{g}"])
                K.op("dve", lambda e: e.bn_stats(out=bst[:], in_=p_o[:, 0:128]), reads=["p_o"], writes=["bst"])
                K.op("dve", lambda e: e.bn_aggr(out=mv[:], in_=bst[:]), reads=["bst"], writes=["mv"])
                rsqrt_ps(rs2[:, 0:1], mv[:, 1:2], 1.0, eps_c, "mv", "rs2", rs2[:, 1:2], "rs2t")
                K.op("dve", lambda e: e.tensor_scalar(out=onr[:], in0=p_o[:, 0:128], scalar1=mv[:, 0:1],
                                                      scalar2=rs2[:, 0:1], op0=ALU.subtract, op1=ALU.mult),
                     reads=["p_o", "mv", "rs2"], writes=["onr"])
                K.op("pool", lambda e: e.tensor_tensor(out=onb[:], in0=onr[:], in1=rgn_sb[:], op=ALU.mult),
                     reads=["onr", "rgn"], writes=["onb"])
                K.op("pe", lambda e: e.transpose(out=p_t[:, 2, :], in_=onb[:], identity=ident),
                     reads=["onb", "cst"], writes=["p_t"])
                K.op("act", lambda e, c=c: e.activation(out=rost[sl][:, c * 128:(c + 1) * 128], in_=p_t[:, 2, :],
                                                        func=AF.Identity), reads=["p_t"], writes=[f"rost{sl}"])
            if "R" in DBG:
                continue
            K.dma("pool", bounce[t0 // CW, 64:192, t0 % CW:t0 % CW + n], rost[sl][:, :n], f"st_ro{sl}",
                  reads=[f"rost{sl}"], writes=[f"bnc_ro{ti}"])
            BNC.append(f"bnc_ro{ti}")
        if "A" in phases:
            K.dma("pool", rsp, S32[:], "st_rsp", reads=["S32"], writes=["rsp"])
        K.barrier()

    with ExitStack() as st:
        sb, ps = mk(st)
        NE = 4
        e_t = [sb(f"e_t{i}", [128, 512], F32) for i in range(NE)]
        L_t = [sb(f"L_t{i}", [128, 512], BF16) for i in range(NE)]
        X_t = [sb(f"X_t{i}", [128, 512], F32) for i in range(2)]
        A_t = [sb(f"A_t{i}", [128, 512], BF16) for i in range(2)]
        osq = sb("osq", [64, 512], BF16)
        orr = sb("orr", [64, 512], F32)
        otl = sb("otl", [64, 512], F32)
        sost = [sb(f"sost{i}", [64, 512], BF16) for i in range(2)]
        p_z = [ps(f"p_z{i}", [128, 512], F32) for i in range(3)]
        pCs = {"B": ps("p_CB", [128, 512], F32), "C": ps("p_CC", [128, 512], F32)}
        pOs = {"B": ps("p_OB", [128, 512], F32)[0:64, :], "C": ps("p_OC", [128, 512], F32)[0:64, :]}
        p_nf = ps("p_n", [128, 512], F32)
        p_n = p_nf[0:64, :]
        bias_c = vec[:, V_SBB:V_SBB + 1]
        gn_c = vec[0:64, V_SBGN:V_SBGN + 1]
        doB, doC = "B" in phases, "C" in phases
        ND = int(_os.environ.get("KND", "0"))
        if doC:
            kstg = [sb(f"kstg{i}", [64, NB, 128], F32) for i in range(2)]
            vstg = [sb(f"vstg{i}", [128, NB, 64], F32) for i in range(2)]
            kbf = sb("kbf", [64, NB, 128], BF16)
            vbf = sb("vbf", [128, NB, 64], BF16)
            pt2 = sb("pt_sb", [NB, NPG], I32)
            K.dma("sp", pt2[:], ptab.rearrange("o (b g) -> (o b) g", g=NPG), "ld_pt", writes=["pt_sb"])
            K.op("dve", lambda e: e.tensor_single_scalar(out=pt2[:], in_=pt2[:], scalar=15,
                                                         op=ALU.logical_shift_left), reads=["pt_sb"],
                 writes=["pt2", "pt_sb"])

        def so_epilogue(kind, nq, col0, si):
            sl = si % 2
            pO, ok = pOs[kind], "p_O" + kind
            K.op("act", lambda e: e.activation(out=osq[:, :nq], in_=pO[:, :nq], func=AF.Square), reads=[ok],
                 writes=["osq"])
            K.op("pe", lambda e: e.matmul(p_n[:, :nq], lhsT=ones[0:64, 0:64], rhs=osq[:, :nq], start=True,
                                          stop=True), reads=["osq", "cst"], writes=["p_n"])
            rsqrt_ps(orr[:, :nq], p_n[:, :nq], 1.0 / 64.0, eps_c[0:64], "p_n", "orr", otl[:, :nq], "otl")
            K.op("dve", lambda e: e.scalar_tensor_tensor(out=sost[sl][:, :nq], in0=pO[:, :nq], scalar=gn_c,
                                                         in1=orr[:, :nq], op0=ALU.mult, op1=ALU.mult),
                 reads=[ok, "orr", "vec"], writes=[f"sost{sl}"])
            K.dma("pool", bounce[col0 // CW, 0:64, col0 % CW:col0 % CW + nq], sost[sl][:, :nq], f"st_so{sl}",
                  reads=[f"sost{sl}"], writes=[f"bnc_so{col0}"])
            BNC.append(f"bnc_so{col0}")
            if kind == "B" and (col0 + nq) % CW == 0 and (col0 // CW) < NCH - 1 and "A" in phases:
                exchange(col0 // CW)

        jobsB = []
        if doB:
            for sbi in range(T_P // 512):
                nkb = 4 * sbi + 4
                for idx, kb in enumerate(reversed(range(nkb))):
                    mk_ = None
                    if kb >= 4 * sbi:
                        j = kb - 4 * sbi
                        mk_ = cb[:, C_DM + 512 * j:C_DM + 512 * (j + 1)]
                    jobsB.append(dict(kind="B", sbi=sbi, kb=kb, first=idx == 0, last=kb == 0, mask=mk_, nq=512))
        jobsC = []
        if doC:
            steps = [("new", 0), ("new", 1)] + [("pg", g) for g in reversed(range(NPG))]
            for si, (kd, g) in enumerate(steps):
                mk_ = cb[:, C_SN + 256 * g:C_SN + 256 * (g + 1)] if kd == "new" else None
                jobsC.append(dict(kind="C", sub=kd, g=g, si=si, first=si == 0, last=si == len(steps) - 1,
                                  mask=mk_, nq=T_S))
        jobs = []
        if jobsB and jobsC:
            r = max(1, len(jobsB) // len(jobsC))
            ci = 0
            for bi, jb in enumerate(jobsB):
                jobs.append(jb)
                if (bi + 1) % r == 0 and ci < len(jobsC):
                    jobs.append(jobsC[ci])
                    ci += 1
            jobs += jobsC[ci:]
        else:
            jobs = jobsB + jobsC
        N = len(jobs)

        def c_load(si):
            if si >= len(jobsC) or jobsC[si]["sub"] != "pg":
                return
            g = jobsC[si]["g"]
            sg = si % 2
            kkeys = [f"kstg{sg}_{b}" for b in range(NB)]
            vkeys = [f"vstg{sg}_{b}" for b in range(NB)]
            K.sync("sp", reads=["pt2"], writes=kkeys + vkeys)
            for b in range(NB):
                pg = nc.sync.value_load(pt2[b:b + 1, g:g + 1])
                K.dma("sp", kstg[sg][:, b, :].bitcast(U8),
                      poolKT[bass.ds(pg, PGB)].rearrange("(d k) -> d k", k=512), f"kstg{sg}", writes=[kkeys[b]])
                K.dma("sp", vstg[sg][:, b, :].bitcast(U8),
                      poolV[bass.ds(pg, PGB)].rearrange("(k d) -> k d", d=256), f"vstg{sg}", writes=[vkeys[b]])
                nc.sync.free_register(nc.sync.to_reg(pg))

        def j_z(i):
            jb = jobs[i]
            zs = i % 3
            zk = f"p_z{zs}"
            nq = jb["nq"]
            if jb["kind"] == "B":
                q0, kb = jb["sbi"] * 512, jb["kb"]
                K.op("pe", lambda e: e.matmul(p_z[zs][:, :], lhsT=KT[:, kb * 128:(kb + 1) * 128],
                                              rhs=QT[:, q0:q0 + 512], start=True, stop=True),
                     reads=KTK + QTK, writes=[zk])
            elif jb["sub"] == "new":
                blk = jb["g"]
                K.op("pe", lambda e: e.matmul(p_z[zs][:, 0:T_S], lhsT=KT[:, T_P + 128 * blk:T_P + 128 * (blk + 1)],
                                              rhs=QT[:, T_P:T_ALL], start=True, stop=True),
                     reads=KTK + QTK, writes=[zk])
            else:
                sg = jb["si"] % 2
                kkeys = [f"kstg{sg}_{b}" for b in range(NB)]
                K.op("dve", lambda e: e.tensor_copy(out=kbf[:], in_=kstg[sg][:]), reads=kkeys, writes=["kbf"])
                for b in range(NB):
                    K.op("pe", lambda e, b=b: e.matmul(
                        p_z[zs][:, b * TS:(b + 1) * TS], lhsT=kbf[:, b, :],
                        rhs=QT[:, T_P + b * TS:T_P + (b + 1) * TS], start=True, stop=True),
                        reads=["kbf"] + QTK, writes=[zk])

        def j_s1(i):
            jb = jobs[i]
            zs = i % 3
            zk = f"p_z{zs}"
            nq = jb["nq"]
            s4 = i % NE
            zp = p_z[zs][:, :nq]
            K.op("act", lambda e: e.activation(out=e_t[s4][:, :nq], in_=zp, func=AF.Exp, bias=bias_c, scale=1.0),
                 reads=[zk, "vec"], writes=[f"e{s4}"])
            if jb["mask"] is not None:
                K.op("dve", lambda e: e.tensor_tensor(out=e_t[s4][:, :nq], in0=e_t[s4][:, :nq], in1=jb["mask"],
                                                      op=ALU.mult), reads=[f"e{s4}", "cst"], writes=[f"e{s4}"])
            K.op("act", lambda e: e.activation(out=L_t[s4][:, :nq], in_=e_t[s4][:, :nq], func=AF.Ln, bias=one_c,
                                               scale=1.0), reads=[f"e{s4}"] + KCON, writes=[f"L{s4}"])

        def j_tri(i):
            jb = jobs[i]
            s4, nq = i % NE, jb["nq"]
            pC, ck = pCs[jb["kind"]], "p_C" + jb["kind"]
            K.op("pe", lambda e: e.matmul(pC[:, :nq], lhsT=tri, rhs=L_t[s4][:, :nq], start=jb["first"], stop=True,
                                          skip_group_check=True), reads=[f"L{s4}", "cst"], writes=[ck])

        def j_x(i):
            jb = jobs[i]
            sl, nq = i % 2, jb["nq"]
            pC, ck = pCs[jb["kind"]], "p_C" + jb["kind"]
            K.op("act", lambda e: e.activation(out=X_t[sl][:, :nq], in_=pC[:, :nq], func=AF.Exp, scale=-1.0),
                 reads=[ck], writes=[f"X{sl}"])

        def j_omt(i):
            jb = jobs[i]
            s4, sl, nq = i % NE, i % 2, jb["nq"]
            pC, ck = pCs[jb["kind"]], "p_C" + jb["kind"]
            K.op("pe", lambda e: e.matmul(pC[:, :nq], lhsT=omt, rhs=L_t[s4][:, :nq], start=False, stop=True,
                                          skip_group_check=True), reads=[f"L{s4}", "cst", f"X{sl}"],
                 writes=[ck])

        def j_a(i):
            jb = jobs[i]
            s4, sl, nq = i % NE, i % 2, jb["nq"]
            K.op("dve", lambda e: e.tensor_tensor(out=A_t[sl][:, :nq], in0=e_t[s4][:, :nq], in1=X_t[sl][:, :nq],
                                                  op=ALU.mult), reads=[f"e{s4}", f"X{sl}"], writes=[f"A{sl}"])

        def j_av(i):
            jb = jobs[i]
            sl, nq = i % 2, jb["nq"]
            pO, ok = pOs[jb["kind"]], "p_O" + jb["kind"]
            if jb["kind"] == "B":
                kb = jb["kb"]
                K.op("pe", lambda e: e.matmul(pO[:, :], lhsT=VA[:, kb, :], rhs=A_t[sl][:, :], start=jb["first"],
                                              stop=jb["last"], skip_group_check=True),
                     reads=VAK + [f"A{sl}"], writes=[ok])
                if jb["last"]:
                    so_epilogue("B", 512, jb["sbi"] * 512, jb["sbi"])
                return
            if jb["sub"] == "new":
                blk = jb["g"]
                K.op("pe", lambda e: e.matmul(pO[:, 0:T_S], lhsT=VA[:, T_P // 128 + blk, :], rhs=A_t[sl][:, 0:T_S],
                                              start=(blk == 0), stop=False, skip_group_check=True),
                     reads=VAK + [f"A{sl}"], writes=[ok])
            else:
                sg = jb["si"] % 2
                vkeys = [f"vstg{sg}_{b}" for b in range(NB)]
                K.op("dve", lambda e: e.tensor_copy(out=vbf[:], in_=vstg[sg][:]), reads=vkeys, writes=["vbf"])
                for b in range(NB):
                    K.op("pe", lambda e, b=b: e.matmul(
                        pO[:, b * TS:(b + 1) * TS], lhsT=vbf[:, b, :], rhs=A_t[sl][:, b * TS:(b + 1) * TS],
                        start=False, stop=jb["last"], skip_group_check=True), reads=["vbf", f"A{sl}"],
                        writes=[ok])
            c_load(jb["si"] + 2)
            if jb["last"]:
                so_epilogue("C", T_S, T_P, 1)

        if N:
            for i in range(min(3, N)):
                j_z(i)
                if i < 2:
                    j_s1(i)
            j_tri(0)
            for i in range(N):
                j_x(i)
                j_a(i)
                if i + 2 < N:
                    j_s1(i + 2)
                if i + 3 < N:
                    j_z(i + 3)
                j_omt(i)
                if i >= 1:
                    j_av(i - 1)
                if i + 1 < N:
                    j_tri(i + 1)
            j_av(N - 1)
        K.barrier()
    main.close()

    for k in range(NCH):
        exchange(k)
    G8K = [f"g8_{k}" for k in range(NCH)]

    g8u8 = g8.bitcast(U8).tensor
    RB = CW * 2
    if "E" in phases:
        with ExitStack() as st:
            sb, ps = mk(st)
            Wrg = sb("Wrg", [128, KC, 512], BF16)
            Wo = sb("Wo", [128, KC, D], BF16)
            K.dma("pool", Wrg[:], w_rg.rearrange("(kc p) n -> p kc n", p=128), "ld1", writes=["Wrg"])
            K.dma("pool", Wo[:], w_out.rearrange("(kc p) n -> p kc n", p=128), "ld1", writes=["Wo"])
            inf_sb = sb("inf_sb", [1, NI * 9], I32)
            K.dma("sp", inf_sb[:], info, "ld_pt2", writes=["inf_sb"])
            xo = [sb(f"xo{i}", [128, KC, NT], F32) for i in range(2)]
            T = dict(sq=sb("sq", [128, KC, NT], BF16), tmpf=sb("tmpf", [128, NT], F32),
                     rstd=sb("rstd", [128, NT], F32), tln=sb("tln", [128, NT], F32),
                     p_ssq=ps("p_ssq", [128, 512], F32))
            hE = sb("hE", [128, KC, NT], BF16)
            M = [sb(f"M{i}", [128, 8, NT], BF16) for i in range(2)]
            sg = sb("sg", [128, 4, NT], F32)
            rog = sb("rog", [128, 4, NT], BF16)
            p_g = [ps(f"p_g{i}", [128, 512], F32) for i in range(2)]
            K.sync("sp", reads=["inf_sb"])
            for ti, (j0, n, scol) in enumerate(ptiles):
                sl = ti % 2
                xk = f"xo{sl}"
                K.dma("sp", xo[sl][:, :, :n], xTo[:, j0:j0 + n].rearrange("(kc p) n -> p kc n", p=128), xk,
                      writes=[xk])
                mk_ = f"M{sl}"
                MK = [mk_] + [f"M{sl}_{j}_{hh}" for j in range(4) for hh in range(2)]
                K.sync("sp", reads=G8K, writes=MK)
                col = nc.sync.value_load(inf_sb[0:1, ti * 9:ti * 9 + 1])
                K.dma("sp", M[sl][:, 0:4, :n].bitcast(U8),
                      bass.AP(g8u8, col, [[RB, 128], [192 * RB, 4], [1, 2 * n]]), mk_,
                      reads=G8K, writes=[mk_])
                nc.sync.free_register(nc.sync.to_reg(col))
                for j in range(4):
                    for hh in range(2):
                        i9 = ti * 9 + 1 + 2 * j + hh
                        col = nc.sync.value_load(inf_sb[0:1, i9:i9 + 1])
                        K.dma("sp", M[sl][64 * hh:64 * hh + 64, 4 + j, :n].bitcast(U8),
                              bass.AP(g8u8, col, [[RB, 64], [1, 2 * n]]),
                              mk_, reads=G8K, writes=[f"M{sl}_{j}_{hh}"])
                        nc.sync.free_register(nc.sync.to_reg(col))
                norm_tile(T, xo[sl], n, xk, gm1, 0, scol, hE, "hE")
                for fo in range(4):
                    pp = p_g[fo % 2]
                    for kc in range(KC):
                        K.op("pe", lambda e, kc=kc, fo=fo, pp=pp: e.matmul(
                            pp[:, :n], lhsT=Wrg[:, kc, fo * 128:(fo + 1) * 128], rhs=hE[:, kc, :n],
                            start=(kc == 0), stop=(kc == KC - 1)), reads=["Wrg", "hE"], writes=[f"p_g{fo % 2}"])
                    K.op("act", lambda e, fo=fo, pp=pp: e.activation(out=sg[:, fo, :n], in_=pp[:, :n],
                                                                     func=AF.Silu),
                         reads=[f"p_g{fo % 2}"], writes=["sg"])
                K.op("dve", lambda e, sl=sl: e.tensor_tensor(out=M[sl][:, 0:4, :n], in0=M[sl][:, 0:4, :n],
                                                             in1=sg[:, :, :n], op=ALU.mult),
                     reads=MK + ["sg"], writes=[mk_])
                for fo in range(KC):
                    pp = p_g[fo % 2]
                    for kc in range(KC):
                        K.op("pe", lambda e, kc=kc, fo=fo, pp=pp, sl=sl: e.matmul(
                            pp[:, :n], lhsT=Wo[:, kc, fo * 128:(fo + 1) * 128], rhs=M[sl][:, kc, :n],
                            start=(kc == 0), stop=(kc == KC - 1)), reads=["Wo"] + MK, writes=[f"p_g{fo % 2}"])
                    if scol == 0:
                        K.op("dve", lambda e, fo=fo, pp=pp, sl=sl: e.scalar_tensor_tensor(
                            out=xo[sl][:, fo, :n], in0=pp[:, :n], scalar=ada[:, 16 + fo, 0:1],
                            in1=xo[sl][:, fo, :n], op0=ALU.mult, op1=ALU.add),
                            reads=[f"p_g{fo % 2}", xk] + ADA, writes=[xk])
                    else:
                        nb = n // TS
                        t3 = T["tmpf"][:, :n].rearrange("p (b i) -> p b i", i=TS)
                        K.op("dve", lambda e, fo=fo, pp=pp, t3=t3: e.tensor_tensor(
                            out=t3, in0=pp[:, :n].rearrange("p (b i) -> p b i", i=TS),
                            in1=bc(ada[:, 16 + fo, scol:scol + nb], 2, TS), op=ALU.mult),
                            reads=[f"p_g{fo % 2}"] + ADA, writes=["tmpf"])
                        K.op("dve", lambda e, fo=fo, sl=sl: e.tensor_tensor(
                            out=xo[sl][:, fo, :n], in0=xo[sl][:, fo, :n], in1=T["tmpf"][:, :n], op=ALU.add),
                            reads=["tmpf", xk], writes=[xk])
                K.dma("pool", x1s[:, j0:j0 + n].rearrange("(kc p) n -> p kc n", p=128), xo[sl][:, :, :n],
                      f"st_x1{sl}", reads=[xk], writes=[f"x1s{ti}"])
            K.barrier()

        with ExitStack() as st:
            sb, ps = mk(st)
            Wug = sb("Wug", [128, KC, DFF], BF16)
            Wuv = sb("Wuv", [128, KC, DFF], BF16)
            Wdn = sb("Wdn", [128, FC, D], BF16)
            for hf in range(2):
                cs_ = slice(hf * (DFF // 2), (hf + 1) * (DFF // 2))
                K.dma("pool", Wug[:, :, cs_], w_ug[:, cs_].rearrange("(kc p) n -> p kc n", p=128), "ld1",
                      writes=[f"Wug{hf}"])
                K.dma("pool", Wuv[:, :, cs_], w_uv[:, cs_].rearrange("(kc p) n -> p kc n", p=128), "ld1",
                      writes=[f"Wuv{hf}"])
            K.dma("pool", Wdn[:], w_dn.rearrange("(fc p) n -> p fc n", p=128), "ld1", writes=["Wdn"])
            xo = [sb("xo0", [128, KC, NT], F32)] * 2
            T = dict(sq=sb("sq", [128, KC, NT], BF16), tmpf=sb("tmpf", [128, NT], F32),
                     rstd=sb("rstd", [128, NT], F32), tln=sb("tln", [128, NT], F32),
                     p_ssq=ps("p_ssq", [128, 512], F32))
            hE = sb("hE", [128, KC, NT], BF16)
            uE = sb("uE", [128, FC, NT], BF16)
            aE = [sb(f"aE{i}", [128, NT + 2], F32) for i in range(2)]
            aS = [sb(f"aS{i}", [128, 4, TS + 2], F32) for i in range(2)]
            cry = sb("cry", [128, FC, 2], F32)
            scv = sb("scv", [128, FC, 4, 2], F32)
            cvo = sb("cvo", [128, FC, 4, 2], F32)
            tcv = [sb(f"tcv{i}", [128, NT], F32) for i in range(2)]
            scv2 = [sb(f"sil{i}", [128, NT], F32) for i in range(2)]
            p_a = [ps(f"p_a{i}", [128, 512], F32) for i in range(2)]
            p_b = [ps(f"p_b{i}", [128, 512], F32) for i in range(2)]
            p_f = [ps(f"p_f{i}", [128, 512], F32) for i in range(2)]
            K.op("dve", lambda e: e.memset(cry[:], 0.0), writes=["cry"])
            K.dma("sp", scv[:], sconvT.rearrange("(fc p) b w -> p fc b w", p=128), "ld_pt2", writes=["scv"])
            halo = vec[:, V_HALO:V_HALO + 1]
            for ti, (j0, n, scol) in enumerate(ptiles):
                sl = 0
                xk = f"xo{sl}"
                K.dma("sp", xo[sl][:, :, :n], x1s[:, j0:j0 + n].rearrange("(kc p) n -> p kc n", p=128), xk,
                      reads=[f"x1s{ti}"], writes=[xk])
                norm_tile(T, xo[sl], n, xk, gf1, 24, scol, hE, "hE")
                lastp = (scol == 0 and j0 + n == 2 + TPC)
                for fc in range(FC):
                    s2 = fc % 2
                    pa, pb = p_a[s2], p_b[s2]
                    for kc in range(KC):
                        K.op("pe", lambda e, kc=kc, fc=fc, pa=pa: e.matmul(
                            pa[:, :n], lhsT=Wug[:, kc, fc * 128:(fc + 1) * 128], rhs=hE[:, kc, :n],
                            start=(kc == 0), stop=(kc == KC - 1)), reads=["Wug0", "Wug1", "hE"], writes=[f"p_a{s2}"])
                    for kc in range(KC):
                        K.op("pe", lambda e, kc=kc, fc=fc, pb=pb: e.matmul(
                            pb[:, :n], lhsT=Wuv[:, kc, fc * 128:(fc + 1) * 128], rhs=hE[:, kc, :n],
                            start=(kc == 0), stop=(kc == KC - 1)), reads=["Wuv0", "Wuv1", "hE"], writes=[f"p_b{s2}"])
                    cw = [vec[:, V_CW + j * FC + fc:V_CW + j * FC + fc + 1] for j in range(3)]
                    cbv = vec[:, V_CB + fc:V_CB + fc + 1]
                    tc_, si_ = tcv[s2], scv2[s2]
                    if scol == 0:
                        a_ = aE[s2]
                        ak = f"aE{s2}"
                        K.op("act", lambda e, a_=a_, pa=pa: e.activation(out=a_[:, 2:2 + n], in_=pa[:, :n],
                                                                         func=AF.Identity),
                             reads=[f"p_a{s2}"], writes=[ak])
                        K.op("pool", lambda e, a_=a_, fc=fc: e.tensor_copy(out=a_[:, 0:2], in_=cry[:, fc, :]),
                             reads=["cry"], writes=[ak])
                        if ti == 0:
                            K.op("pool", lambda e, a_=a_: e.tensor_scalar(out=a_[:, 2:4], in0=a_[:, 2:4],
                                                                          scalar1=halo, scalar2=None,
                                                                          op0=ALU.mult),
                                 reads=[ak, "vec"], writes=[ak])
                        K.op("pool", lambda e, a_=a_, fc=fc: e.tensor_copy(out=cry[:, fc, :], in_=a_[:, n:n + 2]),
                             reads=[ak], writes=["cry"])
                        v0, v1, v2 = a_[:, 0:n], a_[:, 1:n + 1], a_[:, 2:n + 2]
                        tv, sv, pbv = tc_[:, :n], si_[:, :n], pb[:, :n]
                        uo = uE[:, fc, :n]
                    else:
                        a_ = aS[s2]
                        ak = f"aS{s2}"
                        K.op("act", lambda e, a_=a_, pa=pa: e.activation(
                            out=a_[:, :, 2:2 + TS], in_=pa[:, :n].rearrange("p (b i) -> p b i", i=TS),
                            func=AF.Identity), reads=[f"p_a{s2}"], writes=[ak])
                        K.op("pool", lambda e, a_=a_, fc=fc: e.tensor_copy(out=a_[:, :, 0:2], in_=scv[:, fc, :, :]),
                             reads=["scv"], writes=[ak])
                        K.op("pool", lambda e, a_=a_, fc=fc: e.tensor_copy(out=cvo[:, fc, :, :],
                                                                           in_=a_[:, :, TS:TS + 2]),
                             reads=[ak], writes=["cvo"])
                        v0, v1, v2 = a_[:, :, 0:TS], a_[:, :, 1:TS + 1], a_[:, :, 2:TS + 2]
                        r3 = lambda ap_: ap_.rearrange("p (b i) -> p b i", i=TS)
                        tv, sv, pbv = r3(tc_[:, :n]), r3(si_[:, :n]), r3(pb[:, :n])
                        uo = r3(uE[:, fc, :n])
                    K.op("dve", lambda e, v0=v0, tv=tv, cw=cw, cbv=cbv: e.tensor_scalar(
                        out=tv, in0=v0, scalar1=cw[0], scalar2=cbv, op0=ALU.mult, op1=ALU.add),
                        reads=[ak, "vec"], writes=[f"tcv{s2}"])
                    K.op("dve", lambda e, v1=v1, tv=tv, cw=cw: e.scalar_tensor_tensor(
                        out=tv, in0=v1, scalar=cw[1], in1=tv, op0=ALU.mult, op1=ALU.add),
                        reads=[ak, "vec", f"tcv{s2}"], writes=[f"tcv{s2}"])
                    K.op("dve", lambda e, v2=v2, tv=tv, cw=cw: e.scalar_tensor_tensor(
                        out=tv, in0=v2, scalar=cw[2], in1=tv, op0=ALU.mult, op1=ALU.add),
                        reads=[ak, "vec", f"tcv{s2}"], writes=[f"tcv{s2}"])
                    K.op("act", lambda e, tv=tv, sv=sv: e.activation(out=sv, in_=tv, func=AF.Silu),
                         reads=[f"tcv{s2}"], writes=[f"sil{s2}"])
                    K.op("dve", lambda e, sv=sv, pbv=pbv, uo=uo: e.tensor_tensor(out=uo, in0=sv, in1=pbv,
                                                                                op=ALU.mult),
                         reads=[f"sil{s2}", f"p_b{s2}"], writes=["uE"])
                if lastp:
                    K.dma("pool", cvp.rearrange("(fc p) w -> p fc w", p=128), cry[:], "st_cv", reads=["cry"],
                          writes=["cvp"])
                if scol != 0:
                    K.dma("pool", cvs.rearrange("(fc p) b w -> p fc b w", p=128), cvo[:], "st_cv", reads=["cvo"],
                          writes=["cvs"])
                for fo in range(KC):
                    pf = p_f[fo % 2]
                    for fc in range(FC):
                        K.op("pe", lambda e, fc=fc, fo=fo, pf=pf: e.matmul(
                            pf[:, :n], lhsT=Wdn[:, fc, fo * 128:(fo + 1) * 128], rhs=uE[:, fc, :n],
                            start=(fc == 0), stop=(fc == FC - 1)), reads=["Wdn", "uE"], writes=[f"p_f{fo % 2}"])
                    if scol == 0:
                        K.op("dve", lambda e, fo=fo, pf=pf, sl=sl: e.scalar_tensor_tensor(
                            out=xo[sl][:, fo, :n], in0=pf[:, :n], scalar=ada[:, 40 + fo, 0:1],
                            in1=xo[sl][:, fo, :n], op0=ALU.mult, op1=ALU.add),
                            reads=[f"p_f{fo % 2}", xk] + ADA, writes=[xk])
                    else:
                        nb = n // TS
                        t3 = T["tmpf"][:, :n].rearrange("p (b i) -> p b i", i=TS)
                        K.op("dve", lambda e, fo=fo, pf=pf, t3=t3: e.tensor_tensor(
                            out=t3, in0=pf[:, :n].rearrange("p (b i) -> p b i", i=TS),
                            in1=bc(ada[:, 40 + fo, scol:scol + nb], 2, TS), op=ALU.mult),
                            reads=[f"p_f{fo % 2}"] + ADA, writes=["tmpf"])
                        K.op("dve", lambda e, fo=fo, sl=sl: e.tensor_tensor(
                            out=xo[sl][:, fo, :n], in0=xo[sl][:, fo, :n], in1=T["tmpf"][:, :n], op=ALU.add),
                            reads=["tmpf", xk], writes=[xk])
                norm_tile(T, xo[sl], n, xk, None, 0, 0, None, None)
                for kc in range(KC):
                    K.op("dve", lambda e, kc=kc, sl=sl: e.scalar_tensor_tensor(
                        out=xo[sl][:, kc, :n], in0=xo[sl][:, kc, :n], scalar=nm32[:, 16 + kc:17 + kc],
                        in1=T["rstd"][:, :n], op0=ALU.mult, op1=ALU.mult), reads=[xk, "nm32", "rstd"],
                        writes=[xk])
                lo = max(j0, 2)
                if j0 + n > lo:
                    K.dma("pool", yT[:, lo - 2:j0 + n - 2].rearrange("(kc p) n -> p kc n", p=128),
                          xo[sl][:, :, lo - j0:n], f"st_y{sl}", reads=[xk], writes=[f"yT{ti}"])
            K.barrier()
    K.barrier()
    outer.close()
    return nc


def _consts(T_P):
    c = np.zeros((128, NCST), np.float32)
    i = np.arange(128)
    c[:, C_ID:C_ID + 128] = np.eye(128)
    tri = (i[:, None] >= i[None, :]).astype(np.float32)
    c[:, C_TRI:C_TRI + 128] = tri
    c[:, C_OMT:C_OMT + 128] = 1.0 - tri
    c[:, C_ONE:C_ONE + 128] = 1.0
    c[:, C_MRET:C_MRET + 128] = (i[None, :] >= i[:, None])
    c[:, C_MRS:C_MRS + 128] = (i[None, :] >= i[:, None]) & ((i[None, :] // 8) == (i[:, None] // 8))
    t = np.arange(512)
    for j in range(4):
        c[:, C_DM + 512 * j:C_DM + 512 * (j + 1)] = ((128 * j + i)[:, None] < t[None, :])
    q = np.arange(256)
    for blk in range(2):
        kb_, ki_ = blk * 16 + i // 8, i % 8
        c[:, C_SN + 256 * blk:C_SN + 256 * (blk + 1)] = (kb_[:, None] == (q // 8)[None, :]) & \
            (ki_[:, None] < (q % 8)[None, :])
    bm = np.zeros((16, 128), np.float32)
    for b in range(16):
        bm[b, 8 * b:8 * b + 8] = 1.0
    c[:, C_BM:C_BM + 2048] = bm.reshape(1, 2048)
    c[:, C_RM:C_RM + 16] = (i[:, None] // 8 == np.arange(16)[None, :])
    return c


def _rot_table(T_P, past, h):
    half = 64
    inv = ROPE_BASE ** (-np.arange(half, dtype=np.float32) / half)
    lg = np.log(1.0 - 2.0 ** (-5.0 - h))
    pos = np.concatenate([np.arange(T_P), np.tile(past + np.arange(TS), NB)]).astype(np.float32)
    idx = np.concatenate([np.arange(T_P) % 128, np.tile(np.arange(TS), NB)]).astype(np.float64)
    ang = pos[:, None] * inv[None, :]
    cos, sin = np.cos(ang), np.sin(ang)
    dq = np.exp((idx + 1.0) * lg)[:, None]
    dk = (np.exp(-(idx + 1.0) * lg) * (128.0 ** -0.5))[:, None]
    tab = np.stack([cos * dq, sin * dq, sin * dq, cos * dq, cos * dk, sin * dk, sin * dk, cos * dk], axis=1)
    return np.ascontiguousarray(tab.astype(np.float32)), float(np.exp(128 * lg)), float(np.exp(8 * lg))


def _info(c, T_P, TPC):
    CW = 1024
    RB = CW * 2
    toks = [max(c * TPC - 2, 0)] + [c * TPC + j0 for j0 in range(0, TPC, 256)] + [T_P + 32 * c]
    out = []
    for t in toks:
        base = (t // CW) * (1536 * RB) + (t % CW) * 2
        out.append(base + 64 * RB)
        for r in range(8):
            out.append(base + r * 192 * RB)
    return np.array([out], np.int32)


PHASES = "0ABCDE"
DBG = _os.environ.get("KDBG", "")
LV = int(_os.environ.get("KLV", "9"))


def kernel(x_prompt, x_sample, cache_k_pages, cache_v_pages, page_table, state_ret, state_conv,
           c_prompt, c_sample, w_ada, b_ada, norm_mix, w_in, ret_gn, sb_gn, sb_bias, w_out,
           norm_ffn, w_up_gate, w_up_val, conv_w, conv_b, w_down, norm_final):
    f = lambda a: np.ascontiguousarray(np.asarray(a, dtype=np.float32))
    x_prompt, x_sample = f(x_prompt), f(x_sample)
    T_P = x_prompt.shape[1]
    NPG = page_table.shape[1]
    NPHYS = cache_k_pages.shape[1]
    past = NPG * cache_k_pages.shape[2]
    TPC = T_P // NCORES
    T_ALL = T_P + T_S
    nc = build(T_P, NPG, NPHYS, PHASES)

    xT = f(np.concatenate([x_prompt[0].T, x_sample.reshape(T_S, D).T], axis=1))
    w_in0 = f(w_in)[0]
    ck = np.asarray(cache_k_pages)[0]
    cv = np.asarray(cache_v_pages)[0]
    cst = _consts(T_P)
    ptab = np.ascontiguousarray(np.asarray(page_table, dtype=np.int32).reshape(1, NB * NPG))
    vbase = np.zeros((128, NV), np.float32)
    vbase[:, V_BADA:V_BADA + 48] = f(b_ada)[0].reshape(48, 128).T
    vbase[:, V_NMIX:V_NMIX + 8] = f(norm_mix)[0].reshape(8, 128).T
    vbase[:, V_NFFN:V_NFFN + 8] = f(norm_ffn)[0].reshape(8, 128).T
    vbase[:, V_NFIN:V_NFIN + 8] = f(norm_final).reshape(8, 128).T
    vbase[:, V_CW:V_CW + 66] = f(conv_w)[0].reshape(3, FC, 128).transpose(2, 0, 1).reshape(128, 66)
    vbase[:, V_CB:V_CB + FC] = f(conv_b)[0].reshape(FC, 128).T
    in_maps = []
    for c in range(NCORES):
        h = c % 4
        rot, gC, g8 = _rot_table(T_P, past, h)
        vec = vbase.copy()
        vec[:64, V_SBGN] = f(sb_gn)[0][64 * c:64 * c + 64]
        vec[:, V_SBB] = f(sb_bias)[0][c]
        vec[:, V_GC] = gC
        vec[:, V_G8] = g8
        vec[:, V_HALO] = 0.0 if c == 0 else 1.0
        cols = np.r_[0:64] + 64 * c
        rc = np.r_[0:128] + 128 * h
        w_c = np.concatenate([w_in0[:, 2048 + cols], w_in0[:, 2560 + cols], w_in0[:, 3072 + cols],
                              w_in0[:, rc], w_in0[:, 512 + rc], w_in0[:, 1024 + rc]], axis=1)
        xo = np.zeros((D, 2 + TPC + 4 * TS), np.float32)
        if c > 0:
            xo[:, 0:2] = xT[:, c * TPC - 2:c * TPC]
        xo[:, 2:2 + TPC] = xT[:, c * TPC:(c + 1) * TPC]
        xo[:, 2 + TPC:] = xT[:, T_P + 32 * c:T_P + 32 * (c + 1)]
        cT = np.concatenate([f(c_prompt).T, f(c_sample).T, f(c_sample)[4 * c:4 * c + 4].T], axis=1)
        in_maps.append(dict(
            xT=xT, xTo=xo, cT=f(cT), w_ada=f(w_ada)[0], vecsT=vec,
            rgn=f(np.tile(f(ret_gn)[0][128 * h:128 * h + 128][None, :], (128, 1))),
            w_c=f(w_c), w_rg=f(w_in0[:, 1536:2048]), w_out=f(w_out)[0], w_ug=f(w_up_gate)[0],
            w_uv=f(w_up_val)[0], w_dn=f(w_down)[0], rot=rot, cst=cst,
            poolKT=f(ck[:, :, c, :].transpose(0, 2, 1)).view(np.uint8).reshape(-1),
            poolV=f(cv[:, :, c, :]).view(np.uint8).reshape(-1), ptab=ptab,
            sret=f(np.asarray(state_ret)[0][:, h]),
            sconvT=f(np.asarray(state_conv)[0][4 * c:4 * c + 4].transpose(2, 0, 1)),
            info=_info(c, T_P, TPC)))
    res = run_bass_kernel_spmd(nc, in_maps, core_ids=list(range(NCORES))).results

    y_prompt = np.zeros((1, T_P, D), np.float32)
    y_sample = np.zeros((NB, TS, D), np.float32)
    kp = np.zeros((1, 1, T_P, 8, 64), np.float32)
    vp = np.zeros((1, 1, T_P, 8, 64), np.float32)
    ks = np.zeros((1, NB, TS, 8, 64), np.float32)
    vs = np.zeros((1, NB, TS, 8, 64), np.float32)
    rp = np.zeros((1, 1, 4, 128, 128), np.float32)
    rs = np.zeros((1, NB, 4, 128, 128), np.float32)
    cp = np.zeros((1, 1, 2, DFF), np.float32)
    cs = np.zeros((1, NB, 2, DFF), np.float32)
    for c in range(NCORES):
        r = res[c]
        y_prompt[0, c * TPC:(c + 1) * TPC] = r["yT"][:, :TPC].T
        y_sample[4 * c:4 * c + 4] = r["yT"][:, TPC:].T.reshape(4, TS, D)
        kT = r["kT_o"]
        kp[0, 0, :, c, :] = kT[:, :T_P].T
        ks[0, :, :, c, :] = kT[:, T_P:].T.reshape(NB, TS, 64)
        vp[0, 0, :, c, :] = r["v_o"][:T_P]
        vs[0, :, :, c, :] = r["v_o"][T_P:].reshape(NB, TS, 64)
        if c < 4:
            rp[0, 0, c] = r["rsp"]
            rs[0, :, c] = r["rss"]
        cs[0, 4 * c:4 * c + 4] = r["cvs"].transpose(1, 2, 0)
        if c == NCORES - 1:
            cp[0, 0] = r["cvp"].T
    return (y_prompt, y_sample, kp, vp, ks, vs, rp, rs, cp, cs)
```
